# Optimizing a Trainium2 kernel written in Bass

```python
import jax, jax.numpy as jnp
from jax import lax
import numpy as np

D_MODEL = 2048
BATCH = 16
SEQ = 2048
DEPTH = 2

N_A_LAYERS = DEPTH - DEPTH // 2
N_B_LAYERS = DEPTH // 2
PLE_DIM = 256
D_FF = 5632
HG_EXPAND = 128
HG_HEADS = D_MODEL // HG_EXPAND
HG_DK = HG_EXPAND
HG_DV = D_MODEL // HG_HEADS
HG_CHUNK = 32
DA_HEAD_DIM = 128
DA_HEADS = D_MODEL // (2 * DA_HEAD_DIM)
DA_VDIM = 2 * DA_HEAD_DIM
Q_BLOCK = 128
ROPE_THETA = 10000.0
NORM_EPS = 1e-6

kernel_name = "yoco_hgrn2_diffattn_macaron_ple"


def rms_norm(x, g):
    xf = x.astype(jnp.float32)
    y = xf * lax.rsqrt(jnp.mean(xf * xf, axis=-1, keepdims=True) + NORM_EPS)
    return (y * g.astype(jnp.float32)).astype(x.dtype)


def swiglu_ffn(x, w_gate_up, w_down):
    gate, up = jnp.split(x @ w_gate_up, 2, axis=-1)
    return (jax.nn.silu(gate) * up) @ w_down


def rope(x, positions):
    dh = x.shape[-1]
    half = dh // 2
    inv_freq = ROPE_THETA ** (-jnp.arange(0, dh, 2, dtype=jnp.float32) / dh)
    ang = positions.astype(jnp.float32)[:, None] * inv_freq[None, :]
    bshape = (1, x.shape[1]) + (1,) * (x.ndim - 3) + (half,)
    cos = jnp.cos(ang).reshape(bshape)
    sin = jnp.sin(ang).reshape(bshape)
    xf = x.astype(jnp.float32)
    x1, x2 = xf[..., :half], xf[..., half:]
    out = jnp.concatenate([x1 * cos - x2 * sin, x2 * cos + x1 * sin], axis=-1)
    return out.astype(x.dtype)


def hgrn2_mixer(x, w_in, lb, out_gain, w_out):
    B, S, _ = x.shape
    d_f = HG_HEADS * HG_DK
    d_i = HG_HEADS * HG_DV
    proj = x @ w_in
    q = proj[..., :d_f]
    fz = proj[..., d_f:2 * d_f].astype(jnp.float32)
    v_in = proj[..., 2 * d_f:2 * d_f + d_i]
    g = proj[..., 2 * d_f + d_i:]
    lbf = lb.astype(jnp.float32)
    log_f = jnp.logaddexp(jnp.log(lbf), jnp.log1p(-lbf) + jax.nn.log_sigmoid(fz))
    k = (1.0 - lbf) * jax.nn.sigmoid(-fz)

    nc = S // HG_CHUNK

    def to_chunks(t, d):
        return t.reshape(B, nc, HG_CHUNK, HG_HEADS, d).transpose(1, 0, 3, 2, 4)

    qc = to_chunks(q.astype(jnp.float32) * (HG_DK ** -0.5), HG_DK)
    kc = to_chunks(k, HG_DK)
    vc = to_chunks(v_in.astype(jnp.float32), HG_DV)
    bc = jnp.cumsum(to_chunks(log_f, HG_DK), axis=3)
    causal = jnp.tril(jnp.ones((HG_CHUNK, HG_CHUNK), dtype=bool))

    def step(state, inp):
        q_c, k_c, v_c, b_c = inp
        o_inter = jnp.einsum('bhck,bhkv->bhcv', q_c * jnp.exp(b_c), state)
        diff = b_c[:, :, :, None, :] - b_c[:, :, None, :, :]
        decay = jnp.exp(jnp.where(causal[None, None, :, :, None], diff, -jnp.inf))
        attn = jnp.einsum('bhtk,bhsk,bhtsk->bhts', q_c, k_c, decay)
        o_intra = jnp.einsum('bhts,bhsv->bhtv', attn, v_c)
        b_last = b_c[:, :, -1:, :]
        new_state = state * jnp.exp(b_last)[:, :, 0, :, None] + jnp.einsum(
            'bhsk,bhsv->bhkv', k_c * jnp.exp(b_last - b_c), v_c)
        return new_state, o_inter + o_intra

    state0 = jnp.zeros((B, HG_HEADS, HG_DK, HG_DV), jnp.float32)
    _, o = lax.scan(step, state0, (qc, kc, vc, bc))
    o = o.transpose(1, 0, 3, 2, 4).reshape(B, S, HG_HEADS, HG_DV)
    o = rms_norm(o, out_gain).reshape(B, S, d_i).astype(x.dtype)
    o = o * jax.nn.silu(g)
    return o @ w_out


def shared_kv(h, kv_norm, w_kv, positions):
    B, S, _ = h.shape
    kv = rms_norm(h, kv_norm) @ w_kv
    d_k = DA_HEADS * 2 * DA_HEAD_DIM
    k = kv[..., :d_k].reshape(B, S, DA_HEADS, 2, DA_HEAD_DIM)
    v = kv[..., d_k:].reshape(B, S, DA_HEADS, DA_VDIM)
    return rope(k, positions), v


def diff_attention(x, k, v, w_q, lam, subln, w_out, lambda_init, positions):
    B, S, _ = x.shape
    q = (x @ w_q).reshape(B, S, DA_HEADS, 2, DA_HEAD_DIM)
    q = rope(q, positions) * (DA_HEAD_DIM ** -0.5)
    lam32 = lam.astype(jnp.float32)
    lambda_full = (jnp.exp(jnp.sum(lam32[0] * lam32[1])) - jnp.exp(jnp.sum(lam32[2] * lam32[3]))
                   + lambda_init)
    nq = S // Q_BLOCK
    qb = q.reshape(B, nq, Q_BLOCK, DA_HEADS, 2, DA_HEAD_DIM).transpose(1, 0, 2, 3, 4, 5)
    q_pos = positions.reshape(nq, Q_BLOCK)

    def one_block(args):
        q_blk, qp = args
        s = jnp.einsum('bqhcd,bkhcd->bhcqk', q_blk, k).astype(jnp.float32)
        mask = positions[None, :] <= qp[:, None]
        s = jnp.where(mask[None, None, None], s, -jnp.inf)
        pm = jax.nn.softmax(s, axis=-1)
        w = pm[:, :, 0] - lambda_full * pm[:, :, 1]
        return jnp.einsum('bhqk,bkhv->bqhv', w.astype(v.dtype), v)

    o = lax.map(one_block, (qb, q_pos))
    o = o.transpose(1, 0, 2, 3, 4).reshape(B, S, DA_HEADS, DA_VDIM)
    o = rms_norm(o, subln) * (1.0 - lambda_init)
    return o.reshape(B, S, DA_HEADS * DA_VDIM) @ w_out


def setup_inputs(seed: int = 0) -> dict:
    key = jax.random.key(seed)
    ks = jax.random.split(key, 22)
    f32 = jnp.float32

    def nrm(k, shape, scale):
        return jax.random.normal(k, shape, f32) * scale

    def gain(k, shape):
        return 1.0 + 0.02 * jax.random.normal(k, shape, f32)

    d_f = HG_HEADS * HG_DK
    d_i = HG_HEADS * HG_DV
    return {
        "x": nrm(ks[0], (BATCH, SEQ, D_MODEL), 1.0),
        "p": nrm(ks[1], (DEPTH, BATCH, SEQ, PLE_DIM), 1.0),
        "ffn_norm": gain(ks[2], (DEPTH, 2, D_MODEL)),
        "ffn_w_gate_up": nrm(ks[3], (DEPTH, 2, D_MODEL, 2 * D_FF), D_MODEL ** -0.5),
        "ffn_w_down": nrm(ks[4], (DEPTH, 2, D_FF, D_MODEL), D_FF ** -0.5),
        "mix_norm": gain(ks[5], (DEPTH, D_MODEL)),
        "hgrn_w_in": nrm(ks[6], (N_A_LAYERS, D_MODEL, 2 * d_f + 2 * d_i), D_MODEL ** -0.5),
        "hgrn_lower_bounds": nrm(ks[7], (N_A_LAYERS + 1, d_f), 0.5),
        "hgrn_out_norm": gain(ks[8], (N_A_LAYERS, HG_DV)),
        "hgrn_w_out": nrm(ks[9], (N_A_LAYERS, d_i, D_MODEL), d_i ** -0.5),
        "kv_norm": gain(ks[10], (D_MODEL,)),
        "w_kv": nrm(ks[11], (D_MODEL, DA_HEADS * 2 * DA_HEAD_DIM + DA_HEADS * DA_VDIM), D_MODEL ** -0.5),
        "diff_w_q": nrm(ks[12], (N_B_LAYERS, D_MODEL, DA_HEADS * 2 * DA_HEAD_DIM), D_MODEL ** -0.5),
        "diff_lambda": nrm(ks[13], (N_B_LAYERS, 4, DA_HEAD_DIM), 0.1),
        "diff_subln": gain(ks[14], (N_B_LAYERS, DA_VDIM)),
        "diff_w_out": nrm(ks[15], (N_B_LAYERS, DA_HEADS * DA_VDIM, D_MODEL), (DA_HEADS * DA_VDIM) ** -0.5),
        "ple_norm": gain(ks[16], (DEPTH, D_MODEL)),
        "ple_w_gate": nrm(ks[17], (DEPTH, D_MODEL, D_MODEL), D_MODEL ** -0.5),
        "ple_w_proj": nrm(ks[18], (DEPTH, PLE_DIM, D_MODEL), PLE_DIM ** -0.5),
        "final_norm": gain(ks[19], (D_MODEL,)),
    }


def reference(x, p, ffn_norm, ffn_w_gate_up, ffn_w_down, mix_norm, hgrn_w_in, hgrn_lower_bounds,
              hgrn_out_norm, hgrn_w_out, kv_norm, w_kv, diff_w_q, diff_lambda, diff_subln, diff_w_out,
              ple_norm, ple_w_gate, ple_w_proj, final_norm):
    positions = jnp.arange(x.shape[1], dtype=jnp.int32)
    lb_all = jnp.cumsum(jax.nn.softmax(hgrn_lower_bounds.astype(jnp.float32), axis=0), axis=0)
    h = x
    k_shared = None
    v_shared = None
    for i in range(DEPTH):
        h = h + 0.5 * swiglu_ffn(rms_norm(h, ffn_norm[i, 0]), ffn_w_gate_up[i, 0], ffn_w_down[i, 0])
        hn = rms_norm(h, mix_norm[i])
        if i < N_A_LAYERS:
            mix = hgrn2_mixer(hn, hgrn_w_in[i], lb_all[i], hgrn_out_norm[i], hgrn_w_out[i])
        else:
            j = i - N_A_LAYERS
            lambda_init = 0.8 - 0.6 * math_exp(-0.3 * i)
            mix = diff_attention(hn, k_shared, v_shared, diff_w_q[j], diff_lambda[j], diff_subln[j],
                                 diff_w_out[j], lambda_init, positions)
        h = h + mix
        h = h + 0.5 * swiglu_ffn(rms_norm(h, ffn_norm[i, 1]), ffn_w_gate_up[i, 1], ffn_w_down[i, 1])
        gate = jax.nn.sigmoid(rms_norm(h, ple_norm[i]) @ ple_w_gate[i])
        h = h + gate * (p[i] @ ple_w_proj[i])
        if i == N_A_LAYERS - 1:
            k_shared, v_shared = shared_kv(h, kv_norm, w_kv, positions)
    return rms_norm(h, final_norm)


def math_exp(t):
    return float(np.exp(t))
```

```python
import math
import numpy as np
import concourse.bass as bass
import concourse.mybir as mybir
from concourse.bass_utils import run_bass_kernel_spmd

F32 = mybir.dt.float32
BF16 = mybir.dt.bfloat16
AF = mybir.ActivationFunctionType
ALU = mybir.AluOpType
AX = mybir.AxisListType

EPS = 1e-6
ROPE_THETA = 10000.0
NEG = -30000.0


ALL_STAGES = ("ffn00", "hgrn", "ffn01", "ple0", "kv", "ffn10", "dattn", "ffn11", "ple1")


class Cfg:
    def __init__(self, D=2048, DFF=5632, S=2048, T=256, NSEQ=2, PLE=256, NSLOT=5, LAG=2):
        self.D, self.DFF, self.S, self.T, self.NSEQ, self.PLE, self.NSLOT = D, DFF, S, T, NSEQ, PLE, NSLOT
        self.LAG = LAG
        self.DC = D // 128
        self.FB = DFF // 128
        self.HGH = D // 128
        self.DAH = D // 256
        self.NT = S // T
        self.NTB = T // 128
        self.NCH = T // 32
        self.PC = PLE // 128
        assert self.PC * D == self.DC * 256
        self.SA = self.DC * 256
        self.FH = self.FB // 2
        assert self.FH * 2 == self.FB
        self.SB = self.FH * 128
        self.SLOTE = max(self.SA, self.SB)
        self.NJ = self.DC // 2
        o = 0
        self.iGU = o; o += 4 * self.FB
        self.iWINA = o; o += self.HGH
        self.iWINB = o; o += self.HGH
        self.iWOUT = o; o += self.NJ
        self.iWK = o; o += self.DAH
        self.iWV = o; o += self.DAH
        self.iWQ = o; o += self.DAH
        self.iDWOUT = o; o += self.NJ
        self.iPLEG = o; o += 2 * self.NJ
        self.iPLEP = o; o += 2
        self.NA = o
        self.NB = 4 * self.DC * 2
        c = 0
        self.pG = c; c += 10 * self.DC
        self.pLB0 = c; c += self.HGH
        self.pLB1 = c; c += self.HGH
        self.pON = c; c += 1
        self.pSUB = c; c += 2
        self.pLAM = c; c += 512
        self.NP = c
        self.cbID = 0
        self.cbONE = 128
        self.cbSW = 256
        self.cbMASK = 384
        self.cbM4 = 384 + self.NTB * T
        self.NCB = 384 + self.NTB * T + 4
        self.cfID = 0
        self.cfBD = 128
        self.cfM4 = 256
        self.cfSCAN = 260
        self.NCF = 260 + T
        self.NAR = max(self.FB, 22 + 2 * (2 * S // T), 2 * self.DC, 46)
        self.lam_init = 0.8 - 0.6 * float(np.exp(-0.3 * 1))


def _slabA(W, cols, DC):
    sub = W[:, cols]
    return sub.reshape(DC, 128, -1).transpose(1, 0, 2).reshape(128, -1)


def host_weights(inp, cfg):
    D, DC, DFF, FB = cfg.D, cfg.DC, cfg.DFF, cfg.FB
    wA = np.empty((cfg.NA, 128, cfg.SA), np.float32)
    wB = np.empty((cfg.NB, 128, cfg.SB), np.float32)
    ar = np.arange
    for l in range(2):
        for f in range(2):
            Wgu = inp["ffn_w_gate_up"][l, f]
            for fb in range(FB):
                cols = np.concatenate([ar(fb * 128, fb * 128 + 128), DFF + ar(fb * 128, fb * 128 + 128)])
                wA[cfg.iGU + (l * 2 + f) * FB + fb] = _slabA(Wgu, cols, DC)
            Wd = inp["ffn_w_down"][l, f]
            for ob in range(DC):
                sub = Wd[:, ob * 128:(ob + 1) * 128].reshape(2, cfg.FH, 128, 128)
                for hf in range(2):
                    wB[((l * 2 + f) * DC + ob) * 2 + hf] = sub[hf].transpose(1, 0, 2).reshape(128, -1)
    Win = inp["hgrn_w_in"][0]
    for hd in range(cfg.HGH):
        r = ar(hd * 128, hd * 128 + 128)
        wA[cfg.iWINA + hd] = _slabA(Win, np.concatenate([r, D + r]), DC)
        wA[cfg.iWINB + hd] = _slabA(Win, np.concatenate([2 * D + r, 3 * D + r]), DC)
    for j in range(cfg.NJ):
        r = ar(j * 256, j * 256 + 256)
        wA[cfg.iWOUT + j] = _slabA(inp["hgrn_w_out"][0], r, DC)
        wA[cfg.iDWOUT + j] = _slabA(inp["diff_w_out"][0], r, DC)
        for l in range(2):
            wA[cfg.iPLEG + l * cfg.NJ + j] = _slabA(inp["ple_w_gate"][l], r, DC)
    for hd in range(cfg.DAH):
        r = ar(hd * 256, hd * 256 + 256)
        wA[cfg.iWK + hd] = _slabA(inp["w_kv"], r, DC)
        wA[cfg.iWV + hd] = _slabA(inp["w_kv"], D + r, DC)
        wA[cfg.iWQ + hd] = _slabA(inp["diff_w_q"][0], r, DC)
    for l in range(2):
        Wp = inp["ple_w_proj"][l]
        wA[cfg.iPLEP + l] = Wp.reshape(cfg.PC, 128, D).transpose(1, 0, 2).reshape(128, -1)
    return wA, wB


def host_params(inp, cfg):
    DC = cfg.DC
    prm = np.zeros((128, cfg.NP), np.float32)

    def fm(v):
        return np.asarray(v, np.float32).reshape(-1, 128).T

    gl = [inp["ffn_norm"][0, 0], inp["ffn_norm"][0, 1], inp["ffn_norm"][1, 0], inp["ffn_norm"][1, 1],
          inp["mix_norm"][0], inp["mix_norm"][1], inp["kv_norm"], inp["ple_norm"][0], inp["ple_norm"][1],
          inp["final_norm"]]
    for i, g in enumerate(gl):
        prm[:, cfg.pG + i * DC: cfg.pG + (i + 1) * DC] = fm(g)
    prm[:, cfg.pLB0:cfg.pLB0 + cfg.HGH] = fm(inp["hgrn_lower_bounds"][0])
    prm[:, cfg.pLB1:cfg.pLB1 + cfg.HGH] = fm(inp["hgrn_lower_bounds"][1])
    prm[:, cfg.pON] = np.asarray(inp["hgrn_out_norm"][0], np.float32)
    prm[:, cfg.pSUB:cfg.pSUB + 2] = fm(inp["diff_subln"][0])
    prm[:, cfg.pLAM:cfg.pLAM + 512] = np.broadcast_to(
        np.asarray(inp["diff_lambda"][0], np.float32).reshape(1, 512), (128, 512))
    return prm


def host_consts(cfg):
    T, S = cfg.T, cfg.S
    cb = np.zeros((128, cfg.NCB), np.float32)
    cb[:, cfg.cbID:cfg.cbID + 128] = np.eye(128)
    cb[:, cfg.cbONE:cfg.cbONE + 128] = 1.0
    sw = np.zeros((128, 128), np.float32)
    for d in range(128):
        sw[d, (d + 64) % 128] = 1.0
    cb[:, cfg.cbSW:cfg.cbSW + 128] = sw
    p = np.arange(128)[:, None]
    q = np.arange(T)[None, :]
    for j in range(cfg.NTB):
        cb[:, cfg.cbMASK + j * T: cfg.cbMASK + (j + 1) * T] = np.where(j * 128 + p <= q, 0.0, NEG)
    cb[:, cfg.cbM4:cfg.cbM4 + 4] = (np.arange(128)[:, None] // 32 == np.arange(4)[None, :]).astype(np.float32)
    cf = np.zeros((128, cfg.NCF), np.float32)
    cf[:, cfg.cfID:cfg.cfID + 128] = np.eye(128)
    s = np.arange(128)[:, None]
    t = np.arange(128)[None, :]
    cf[:, cfg.cfBD:cfg.cfBD + 128] = ((s <= t) & (s // 32 == t // 32)).astype(np.float32)
    cf[:, cfg.cfM4:cfg.cfM4 + 4] = (s // 32 == np.arange(4)[None, :]).astype(np.float32)
    cf[:, cfg.cfSCAN:cfg.cfSCAN + T] = (np.arange(T) % 32 != 0).astype(np.float32)[None, :]
    inv = (ROPE_THETA ** (-np.arange(0, 128, 2, dtype=np.float32) / 128)).astype(np.float32)
    ang = np.arange(S, dtype=np.float32)[None, :] * np.concatenate([inv, inv])[:, None]
    rope = np.stack([np.cos(ang), np.sin(ang) * np.where(np.arange(128) < 64, -1.0, 1.0)[:, None]]).astype(np.float32)
    return cb, cf, rope


def host_acts(x, p, cfg):
    NS, NT, T, DC, PC = cfg.NSEQ, cfg.NT, cfg.T, cfg.DC, cfg.PC
    xT = np.ascontiguousarray(x.reshape(NS, NT, T, DC, 128).transpose(0, 1, 4, 3, 2))
    pT = np.ascontiguousarray(p.reshape(2, NS, NT, T, PC, 128).transpose(0, 1, 2, 5, 4, 3))
    return xT, pT


def host_out(oT, cfg):
    return np.ascontiguousarray(oT.transpose(0, 1, 4, 3, 2)).reshape(cfg.NSEQ, cfg.S, cfg.D)


class Buf:
    __slots__ = ("name", "w", "r", "dsem", "dcnt", "excl")

    def __init__(self, name, excl=False):
        self.name = name
        self.excl = excl
        self.w = {}
        self.r = {}
        self.dsem = None
        self.dcnt = 0


class Sched:
    def __init__(self, nc):
        self.nc = nc
        self.engs = {"pe": nc.tensor, "act": nc.scalar, "dve": nc.vector, "pool": nc.gpsimd, "sp": nc.sync}
        self.csem = {k: nc.alloc_semaphore("cs_" + k) for k in self.engs}
        self.cnt = {k: 0 for k in self.engs}
        self.seen = {k: {} for k in self.engs}
        self.nsem = 0

    def new_dsem(self, buf):
        buf.dsem = self.nc.alloc_semaphore("ds_%d_%s" % (self.nsem, buf.name))
        self.nsem += 1

    def _wait(self, e, reads, writes):
        need = {}
        own = self.csem[e]
        for b in reads:
            for s, v in b.w.items():
                if need.get(s, 0) < v:
                    need[s] = v
            if b.excl:
                for s, v in b.r.items():
                    if s is not own and s != own and need.get(s, 0) < v:
                        need[s] = v
        for b in writes:
            for s, v in b.w.items():
                if need.get(s, 0) < v:
                    need[s] = v
            for s, v in b.r.items():
                if need.get(s, 0) < v:
                    need[s] = v
        seen = self.seen[e]
        for s, v in need.items():
            if seen.get(s, 0) >= v:
                continue
            self.engs[e].wait_ge(s, v)
            seen[s] = v

    def _mark(self, s, v, reads, writes):
        for b in reads:
            if b.r.get(s, 0) < v:
                b.r[s] = v
        for b in writes:
            if b.w.get(s, 0) < v:
                b.w[s] = v

    def op(self, e, fn, reads=(), writes=()):
        self._wait(e, reads, writes)
        ins = fn(self.engs[e])
        self.cnt[e] += 1
        s = self.csem[e]
        ins.then_inc(s, 1)
        self._mark(s, self.cnt[e], reads, writes)

    def dma(self, q, fns, owner, reads=(), writes=()):
        self._wait(q, reads, writes)
        if owner.dsem is None:
            self.new_dsem(owner)
        for fn in fns:
            fn(self.engs[q]).then_inc(owner.dsem, 16)
        owner.dcnt += 16 * len(fns)
        self._mark(owner.dsem, owner.dcnt, reads, writes)

    def final_wait(self, e, bufs):
        self._wait(e, bufs, bufs)


class Pl:
    __slots__ = ("ap", "bufs")

    def __init__(self, ap, bufs):
        self.ap = ap
        self.bufs = bufs


class Stream:
    def __init__(self, nc, c, i, NF):
        A = nc.alloc_sbuf_tensor
        T = c.T
        n = "_s%d" % i
        self.seq = i
        self.t_h = A("h" + n, [128, c.DC * T], F32)
        self.t_xn = A("xn" + n, [128, c.DC * T], BF16)
        self.t_ar = A("arena" + n, [128, c.NAR * T], BF16)
        self.t_sh = A("sh" + n, [128, (c.NCH + 1) * 128], F32)
        self.t_carry = A("carry" + n, [128, c.HGH * 128], F32)
        self.t_f = A("ftmp" + n, [128, NF * T], F32)
        self.t_rope = A("ropes" + n, [128, 2 * T], F32)
        self.t_pb = A("pb" + n, [128, c.PC * T], BF16)
        self.t_sq = A("sqtmp" + n, [128, 2 * T], BF16)
        B = Buf
        self.b_h = [B("h%d" % j + n) for j in range(c.DC)]
        self.b_xn = [B("xn%d" % j + n) for j in range(c.DC)]
        self.b_ar = [B("ar%d" % j + n) for j in range(c.NAR)]
        self.b_sh = [B("sh%d" % j + n) for j in range(c.NCH + 1)]
        self.b_carry = [B("carry%d" % j + n) for j in range(c.HGH)]
        self.b_f = [B("f%d" % j + n) for j in range(NF)]
        self.b_rope = B("rope" + n)
        self.b_pb = B("pb" + n)
        self.b_sq = [B("sq0" + n), B("sq1" + n)]
        self.sq_i = 0
        self.tile = 0
        self.b_out = B("out" + n)


class Prog:
    def __init__(self, cfg):
        self.cfg = cfg
        c = cfg
        nc = bass.Bass("TRN2", target_bir_lowering=False)
        self.nc = nc
        T = c.T
        self.d_x = nc.dram_tensor("xT", [c.NSEQ, c.NT, 128, c.DC * T], F32, kind="ExternalInput").ap()
        self.d_p = nc.dram_tensor("pT", [2, c.NSEQ, c.NT, 128, c.PC * T], F32, kind="ExternalInput").ap()
        self.d_wA = nc.dram_tensor("wA", [c.NA, 128, c.SA], F32, kind="ExternalInput").ap()
        self.d_wB = nc.dram_tensor("wB", [c.NB, 128, c.SB], F32, kind="ExternalInput").ap()
        self.d_prm = nc.dram_tensor("prm", [128, c.NP], F32, kind="ExternalInput").ap()
        self.d_cb = nc.dram_tensor("cb", [128, c.NCB], F32, kind="ExternalInput").ap()
        self.d_cf = nc.dram_tensor("cf", [128, c.NCF], F32, kind="ExternalInput").ap()
        self.d_rope = nc.dram_tensor("rope", [2, 128, c.S], F32, kind="ExternalInput").ap()
        self.d_out = nc.dram_tensor("outT", [c.NSEQ, c.NT, 128, c.DC * T], F32, kind="ExternalOutput").ap()
        self.d_k = nc.dram_tensor("kscr", [c.NSEQ, c.DAH, 128, 2 * c.S], BF16).ap()
        self.d_v = nc.dram_tensor("vscr", [c.NSEQ, c.DAH, c.S, 256], BF16).ap()
        self.NAH = (c.NA + 2) // 3
        self.d_wAb = [nc.dram_tensor("wAb%d" % i, [self.NAH, 128, c.SA], BF16).ap() for i in range(3)]
        self.d_wBb = nc.dram_tensor("wBb", [c.NB, 128, c.SB], BF16).ap()
        A = nc.alloc_sbuf_tensor
        self.NF = 12
        self.streams = [Stream(nc, c, i, self.NF) for i in range(c.NSEQ)]
        self.cur = self.streams[0]
        self.t_slot = [A("slot%d" % i, [128, c.SLOTE], BF16) for i in range(c.NSLOT)]
        self.t_cb = A("cbs", [128, c.NCB], BF16)
        self.t_cf = A("cfs", [128, c.NCF], F32)
        self.t_prm = A("prms", [128, c.NP], F32)
        self.t_sm = A("small", [128, 64], F32)
        self.t_lt = A("lamtmp", [128, 256], F32)
        self.t_ps = nc.alloc_psum_tensor("ps", [128, 8, 512], F32)
        self.WP0 = c.NAR - (c.SA + T - 1) // T
        B = Buf
        self.b_slot = [B("slot%d" % i) for i in range(c.NSLOT)]
        self.b_cb = B("cb")
        self.b_cf = B("cf")
        self.b_prm = B("prm")
        self.b_sm = B("sm")
        self.b_lt = B("lt")
        self.b_ps = [B("ps%d" % i, excl=True) for i in range(8)]
        self.b_kd = [[B("kd%d_%d" % (s, h)) for h in range(c.DAH)] for s in range(c.NSEQ)]
        self.b_vd = [[B("vd%d_%d" % (s, h)) for h in range(c.DAH)] for s in range(c.NSEQ)]
        self.slab_cache = {}
        self.slot_key = [None] * c.NSLOT
        self.slot_dirty = [None] * c.NSLOT
        self.b_sst = [B("sst%d" % i) for i in range(c.NSLOT)]
        self.b_wsc = {}
        self.stored = set()
        self.wp_key = None
        self.S = Sched(nc)
        self.free_banks = list(range(8))
        self.slot_i = 0
        self.stages = ALL_STAGES

    t_h = property(lambda self: self.cur.t_h)
    t_xn = property(lambda self: self.cur.t_xn)
    t_ar = property(lambda self: self.cur.t_ar)
    t_f = property(lambda self: self.cur.t_f)
    t_sh = property(lambda self: self.cur.t_sh)
    t_carry = property(lambda self: self.cur.t_carry)
    t_rope = property(lambda self: self.cur.t_rope)
    t_pb = property(lambda self: self.cur.t_pb)
    t_sq = property(lambda self: self.cur.t_sq)
    b_h = property(lambda self: self.cur.b_h)
    b_xn = property(lambda self: self.cur.b_xn)
    b_ar = property(lambda self: self.cur.b_ar)
    b_f = property(lambda self: self.cur.b_f)
    b_sh = property(lambda self: self.cur.b_sh)
    b_carry = property(lambda self: self.cur.b_carry)
    b_rope = property(lambda self: self.cur.b_rope)
    b_pb = property(lambda self: self.cur.b_pb)
    b_sq = property(lambda self: self.cur.b_sq)

    def h(self, i):
        T = self.cfg.T
        return Pl(self.t_h[:, i * T:(i + 1) * T], [self.b_h[i]])

    def xn(self, i):
        T = self.cfg.T
        return Pl(self.t_xn[:, i * T:(i + 1) * T], [self.b_xn[i]])

    def ar(self, i, n=1):
        T = self.cfg.T
        return Pl(self.t_ar[:, i * T:(i + n) * T], self.b_ar[i:i + n])

    def arf(self, i, n=1):
        T = self.cfg.T
        return Pl(self.t_ar[:, i * T:(i + 2 * n) * T].bitcast(F32), self.b_ar[i:i + 2 * n])

    def f(self, i):
        T = self.cfg.T
        return Pl(self.t_f[:, i * T:(i + 1) * T], [self.b_f[i]])

    def prm(self, col, n=1):
        return self.t_prm[:, col:col + n]

    def cbv(self, col, n):
        return self.t_cb[:, col:col + n]

    def cfv(self, col, n):
        return self.t_cf[:, col:col + n]

    def sm(self, col, n=1):
        return self.t_sm[:, col:col + n]

    def bank(self):
        i = self.free_banks.pop(0)
        return i

    def rel(self, i):
        self.free_banks.append(i)

    def pb(self, i, lo=0, n=None):
        n = self.cfg.T if n is None else n
        return Pl(self.t_ps[:, i, lo:lo + n], [self.b_ps[i]])

    def _slab(self, key, src, dst, n):
        c = self.cfg
        k = self.slab_cache.get(key)
        if k is not None and self.slot_key[k] == key:
            return Pl(self.t_slot[k], [self.b_slot[k]])
        k = self.slot_i
        self.slot_i = (k + 1) % c.NSLOT
        t, b = self.t_slot[k], self.b_slot[k]
        if self.slot_dirty[k] is not None:
            dkey, ddst, dn = self.slot_dirty[k]
            wb = self.b_wsc.setdefault(dkey, Buf("wsc%s%d" % dkey))
            self.S.dma("sp", [lambda e: e.dma_start(out=ddst, in_=t[:, 0:dn])], self.b_sst[k], reads=[b], writes=[wb])
            self.stored.add(dkey)
            self.slot_dirty[k] = None
        self.slot_key[k] = key
        self.slab_cache[key] = k
        if key in self.stored:
            wb = self.b_wsc[key]
            self.S.dma("pool", [lambda e: e.dma_start(out=t[:, 0:n], in_=dst)], b, reads=[wb], writes=[b])
        else:
            self.S.dma("pool", [lambda e: e.dma_start(out=t[:, 0:n], in_=src)], b, writes=[b])
            self.slot_dirty[k] = (key, dst, n)
        return Pl(t, [b])

    def slabA(self, idx):
        return self._slab(("A", idx), self.d_wA[idx], self.d_wAb[idx // self.NAH][idx % self.NAH], self.cfg.SA)

    def slabB(self, idx):
        return self._slab(("B", idx), self.d_wB[idx], self.d_wBb[idx], self.cfg.SB)

    def tile_of_cur(self):
        return self.cur.tile

    def wproj(self, l):
        c = self.cfg
        T = c.T
        s0 = self.streams[0]
        n = (c.SA + T - 1) // T
        ap = s0.t_ar[:, self.WP0 * T: self.WP0 * T + c.SA]
        bufs = s0.b_ar[self.WP0:self.WP0 + n]
        key = (l, self.cur.tile)
        if self.wp_key != key:
            self.wp_key = key
            src = self.d_wA[c.iPLEP + l]
            self.S.dma("pool", [lambda e: e.dma_start(out=ap, in_=src)], bufs[0], writes=bufs)
        return Pl(ap, bufs)

    def mm(self, out, pairs, reads):
        def fn(e):
            n = len(pairs)
            ins = None
            for i, (l, r) in enumerate(pairs):
                ins = e.matmul(out.ap, l, r, start=(i == 0), stop=(i == n - 1))
            return ins
        self.S.op("pe", fn, reads=reads, writes=out.bufs)

    def rstd(self, srcs, dim, fa, fb):
        c = self.cfg
        T = c.T
        S = self.S
        bk = self.bank()
        bp = self.pb(bk)
        ones = self.cbv(c.cbONE, 128)
        n = len(srcs)
        for i, s in enumerate(srcs):
            k = self.cur.sq_i
            self.cur.sq_i ^= 1
            sq = Pl(self.t_sq[:, k * T:(k + 1) * T], [self.b_sq[k]])
            if n > 2 and i % 2 == 1:
                S.op("dve", lambda e, s=s, sq=sq: e.tensor_tensor(out=sq.ap, in0=s.ap, in1=s.ap, op=ALU.mult),
                     reads=s.bufs, writes=sq.bufs)
            else:
                S.op("act", lambda e, s=s, sq=sq: e.activation(out=sq.ap, in_=s.ap, func=AF.Square),
                     reads=s.bufs, writes=sq.bufs)
            S.op("pe", lambda e, sq=sq, i=i: e.matmul(bp.ap, ones, sq.ap, start=(i == 0), stop=(i == n - 1)),
                 reads=sq.bufs + [self.b_cb], writes=bp.bufs)
        sd = self.f(fa)
        S.op("act", lambda e: e.activation(out=sd.ap, in_=bp.ap, func=AF.Ln, bias=self.sm(0), scale=1.0 / dim),
             reads=bp.bufs + [self.b_sm], writes=sd.bufs)
        self.rel(bk)
        rs = self.f(fb)
        S.op("act", lambda e: e.activation(out=rs.ap, in_=sd.ap, func=AF.Exp, scale=-0.5), reads=sd.bufs, writes=rs.bufs)
        return rs

    def stream_norm(self, gi, dst_fn):
        c = self.cfg
        rs = self.rstd([self.h(i) for i in range(c.DC)], c.D, 6, 7)
        for i in range(c.DC):
            hp = self.h(i)
            d = dst_fn(i)
            g = self.prm(c.pG + gi * c.DC + i)
            self.S.op("dve", lambda e, hp=hp, d=d, g=g: e.scalar_tensor_tensor(
                out=d.ap, in0=hp.ap, scalar=g, in1=rs.ap, op0=ALU.mult, op1=ALU.mult),
                reads=hp.bufs + rs.bufs + [self.b_prm], writes=d.bufs)

    def proj_fm(self, slab, col, srcs, ncols=128):
        c = self.cfg
        bk = self.bank()
        out = self.pb(bk)
        W = slab.ap
        n = len(srcs)
        per = c.SA // c.DC if n == c.DC else None
        pairs = []
        rd = list(slab.bufs)
        for i, s in enumerate(srcs):
            pairs.append((W[:, i * per + col: i * per + col + ncols], s.ap))
            rd += s.bufs
        self.mm(out, pairs, rd)
        return bk, out

    def ffn(self, l, f):
        c = self.cfg
        S = self.S
        T = c.T
        self.stream_norm(l * 2 + f, self.xn)
        xs = [self.xn(i) for i in range(c.DC)]
        yield
        for fb in range(c.FB):
            slab = self.slabA(c.iGU + (l * 2 + f) * c.FB + fb)
            bg, pg = self.proj_fm(slab, 0, xs)
            bu, pu = self.proj_fm(slab, 128, xs)
            sg = self.f(fb % 2)
            S.op("act", lambda e, sg=sg, pg=pg: e.activation(out=sg.ap, in_=pg.ap, func=AF.Silu),
                 reads=pg.bufs, writes=sg.bufs)
            self.rel(bg)
            a = self.ar(fb)
            S.op("dve", lambda e, a=a, sg=sg, pu=pu: e.tensor_tensor(out=a.ap, in0=sg.ap, in1=pu.ap, op=ALU.mult),
                 reads=sg.bufs + pu.bufs, writes=a.bufs)
            self.rel(bu)
            yield
        acts = [self.ar(i) for i in range(c.FB)]
        for ob in range(c.DC):
            bk = self.bank()
            out = self.pb(bk)
            for hf in range(2):
                sl = self.slabB(((l * 2 + f) * c.DC + ob) * 2 + hf)
                rd = list(sl.bufs)
                for a in acts[hf * c.FH:(hf + 1) * c.FH]:
                    rd += a.bufs

                def fn(e, sl=sl, hf=hf):
                    ins = None
                    for i in range(c.FH):
                        ins = e.matmul(out.ap, sl.ap[:, i * 128:(i + 1) * 128], acts[hf * c.FH + i].ap,
                                       start=(hf == 0 and i == 0), stop=(hf == 1 and i == c.FH - 1))
                    return ins
                S.op("pe", fn, reads=rd, writes=out.bufs)
                if hf == 0:
                    yield
            hp = self.h(ob)
            S.op("dve", lambda e, hp=hp, out=out: e.scalar_tensor_tensor(
                out=hp.ap, in0=out.ap, scalar=0.5, in1=hp.ap, op0=ALU.mult, op1=ALU.add),
                reads=out.bufs + hp.bufs, writes=hp.bufs)
            self.rel(bk)
            yield

    def add_proj(self, base, srcs):
        c = self.cfg
        for j in range(c.NJ):
            slab = self.slabA(base + j)
            for half in range(2):
                bk, out = self.proj_fm(slab, half * 128, srcs)
                hp = self.h(j * 2 + half)
                self.S.op("dve", lambda e, hp=hp, out=out: e.tensor_tensor(out=hp.ap, in0=out.ap, in1=hp.ap, op=ALU.add),
                          reads=out.bufs + hp.bufs, writes=hp.bufs)
                self.rel(bk)
            yield

    def ple(self, l, seq, tile):
        c = self.cfg
        S = self.S
        T = c.T
        self.stream_norm(7 + l, self.xn)
        xs = [self.xn(i) for i in range(c.DC)]
        src = self.d_p[l, seq, tile]
        S.dma("pool", [lambda e: e.dma_start(out=self.t_pb[:, :], in_=src)], self.b_pb, writes=[self.b_pb])
        wp = self.wproj(l)
        yield
        for j in range(c.NJ):
            slab = self.slabA(c.iPLEG + l * c.NJ + j)
            for half in range(2):
                ob = j * 2 + half
                bg, pg = self.proj_fm(slab, half * 128, xs)
                bp_ = self.bank()
                pp = self.pb(bp_)
                pairs = [(wp.ap[:, pc * c.D + ob * 128: pc * c.D + ob * 128 + 128], self.t_pb[:, pc * T:(pc + 1) * T])
                         for pc in range(c.PC)]
                self.mm(pp, pairs, wp.bufs + [self.b_pb])
                sg = self.f(ob % 2)
                S.op("act", lambda e, sg=sg, pg=pg: e.activation(out=sg.ap, in_=pg.ap, func=AF.Sigmoid),
                     reads=pg.bufs, writes=sg.bufs)
                self.rel(bg)
                t2 = self.f(2 + ob % 2)
                S.op("dve", lambda e, t2=t2, sg=sg, pp=pp: e.tensor_tensor(out=t2.ap, in0=sg.ap, in1=pp.ap, op=ALU.mult),
                     reads=sg.bufs + pp.bufs, writes=t2.bufs)
                self.rel(bp_)
                hp = self.h(ob)
                S.op("dve", lambda e, hp=hp, t2=t2: e.tensor_tensor(out=hp.ap, in0=hp.ap, in1=t2.ap, op=ALU.add),
                     reads=hp.bufs + t2.bufs, writes=hp.bufs)
            yield

    def hgrn(self, first_tile):
        c = self.cfg
        S = self.S
        T = c.T
        NTB, NCH = c.NTB, c.NCH
        NF = self.NF
        self.stream_norm(4, self.xn)
        xs = [self.xn(i) for i in range(c.DC)]
        yield
        per = c.SA // c.DC
        ident_f = self.cfv(c.cfID, 128)
        one = self.sm(2)
        ctx = {}

        def X1(hd):
            st = hd % 2
            PB = 16 + 15 * st
            FO = 8 * st
            slA = self.slabA(c.iWINA + hd)
            slB = self.slabA(c.iWINB + hd)
            bq, pq = self.proj_fm(slA, 0, xs)
            bf_, pf = self.proj_fm(slA, 128, xs)
            bg, pg = self.proj_fm(slB, 128, xs)
            bv = self.bank()
            for tb in range(NTB):
                out = self.pb(bv, tb * 128, 128)
                pairs = [(xs[i].ap[:, tb * 128:(tb + 1) * 128], slB.ap[:, i * per: i * per + 128]) for i in range(c.DC)]
                rd = list(slB.bufs)
                for x_ in xs:
                    rd += x_.bufs
                self.mm(out, pairs, rd)
            pv = self.pb(bv, 0, NTB * 128)
            f0, f1, f2, f3 = self.f(FO), self.f(FO + 1), self.f(FO + 2), self.f(FO + 3)
            qt = self.ar(PB)
            S.op("act", lambda e: e.activation(out=qt.ap, in_=pq.ap, func=AF.Copy, scale=128.0 ** -0.5), reads=pq.bufs, writes=qt.bufs)
            self.rel(bq)
            S.op("act", lambda e: e.activation(out=f1.ap, in_=pf.ap, func=AF.Exp, scale=-1.0), reads=pf.bufs, writes=f1.bufs)
            self.rel(bf_)
            S.op("act", lambda e: e.activation(out=f0.ap, in_=pg.ap, func=AF.Exp, scale=-1.0), reads=pg.bufs, writes=f0.bufs)
            S.op("act", lambda e: e.activation(out=f0.ap, in_=f0.ap, func=AF.Ln, bias=one), reads=f0.bufs + [self.b_sm], writes=f0.bufs)
            S.op("act", lambda e: e.activation(out=f0.ap, in_=f0.ap, func=AF.Exp, scale=-1.0), reads=f0.bufs, writes=f0.bufs)
            gs = self.ar(PB + 9)
            S.op("dve", lambda e: e.tensor_tensor(out=gs.ap, in0=pg.ap, in1=f0.ap, op=ALU.mult), reads=pg.bufs + f0.bufs, writes=gs.bufs)
            self.rel(bg)
            vb = self.ar(PB + 3)
            S.op("act", lambda e: e.activation(out=vb.ap, in_=pv.ap, func=AF.Copy), reads=pv.bufs, writes=vb.bufs)
            v4 = self.ar(PB + 4, 4)
            self.rel(bv)
            m4 = bass.AP(self.t_cb, c.cbM4, [[c.NCB, 128], [1, 4], [0, 128]])
            for tb in range(NTB):
                vin = bass.AP(self.t_ar, (PB + 3) * T + tb * 128, [[c.NAR * T, 128], [0, 4], [1, 128]])
                vout = bass.AP(self.t_ar, (PB + 4) * T + tb * 512, [[c.NAR * T, 128], [128, 4], [1, 128]])
                S.op("dve", lambda e, vin=vin, vout=vout: e.tensor_tensor(out=vout, in0=vin, in1=m4, op=ALU.mult),
                     reads=vb.bufs + [self.b_cb], writes=v4.bufs)
            S.op("act", lambda e: e.activation(out=f2.ap, in_=f1.ap, func=AF.Ln, bias=one, scale=self.sm(8 + hd)),
                 reads=f1.bufs + [self.b_sm], writes=f2.bufs)
            S.op("act", lambda e: e.activation(out=f3.ap, in_=f1.ap, func=AF.Ln, bias=one), reads=f1.bufs + [self.b_sm], writes=f3.bufs)
            S.op("dve", lambda e: e.tensor_tensor(out=f2.ap, in0=f2.ap, in1=f3.ap, op=ALU.subtract), reads=f2.bufs + f3.bufs, writes=f2.bufs)
            S.op("dve", lambda e: e.tensor_tensor_scan(out=f0.ap, data0=self.cfv(c.cfSCAN, T), data1=f2.ap,
                                                       initial=0.0, op0=ALU.mult, op1=ALU.add),
                 reads=f2.bufs + [self.b_cf], writes=f0.bufs)
            ctx[hd] = dict(PB=PB, FO=FO, qt=qt, vb=vb, v4=v4, gs=gs)

        def X3(hd):
            x = ctx[hd]
            PB, FO = x["PB"], x["FO"]
            qt = x["qt"]
            f0, f1, f2, f3 = self.f(FO), self.f(FO + 1), self.f(FO + 2), self.f(FO + 3)
            S.op("act", lambda e: e.activation(out=f3.ap, in_=f2.ap, func=AF.Exp), reads=f2.bufs, writes=f3.bufs)
            S.op("act", lambda e: e.activation(out=f1.ap, in_=f0.ap, func=AF.Exp), reads=f0.bufs, writes=f1.bufs)
            S.op("act", lambda e: e.activation(out=f2.ap, in_=f0.ap, func=AF.Exp, scale=-1.0), reads=f0.bufs, writes=f2.bufs)
            S.op("act", lambda e: e.activation(out=f3.ap, in_=f3.ap, func=AF.Identity, scale=-1.0, bias=one),
                 reads=f3.bufs + [self.b_sm], writes=f3.bufs)
            S.op("dve", lambda e: e.tensor_tensor(out=qt.ap, in0=qt.ap, in1=f1.ap, op=ALU.mult), reads=qt.bufs + f1.bufs, writes=qt.bufs)
            S.op("dve", lambda e: e.tensor_tensor(out=f3.ap, in0=f3.ap, in1=f2.ap, op=ALU.mult), reads=f3.bufs + f2.bufs, writes=f3.bufs)
            kt = self.ar(PB + 1)
            S.op("act", lambda e: e.activation(out=kt.ap, in_=f3.ap, func=AF.Copy), reads=f3.bufs, writes=kt.bufs)
            ebl = bass.AP(self.t_f, (FO + 1) * T + 31, [[NF * T, 128], [32, NCH], [0, 32]])
            kh3 = bass.AP(self.t_f, (FO + 2) * T, [[NF * T, 128], [32, NCH], [1, 32]])
            kk3 = bass.AP(self.t_f, (FO + 3) * T, [[NF * T, 128], [32, NCH], [1, 32]])
            S.op("dve", lambda e: e.tensor_tensor(out=kh3, in0=kk3, in1=ebl, op=ALU.mult),
                 reads=f3.bufs + f1.bufs + f2.bufs, writes=f2.bufs)
            x["kt"], x["eb"], x["kh"] = kt, f1, f2

        def X4(hd):
            x = ctx[hd]
            PB = x["PB"]
            kh = x["kh"]
            bt = self.bank()
            for tb in range(NTB):
                o = self.pb(bt, tb * 128, 128)
                S.op("pe", lambda e, o=o, tb=tb: e.transpose(o.ap, kh.ap[:, tb * 128:(tb + 1) * 128], ident_f),
                     reads=kh.bufs + [self.b_cf], writes=o.bufs)
            ktok = self.ar(PB + 2)
            pt_ = self.pb(bt, 0, NTB * 128)
            S.op("act", lambda e: e.activation(out=ktok.ap, in_=pt_.ap, func=AF.Copy), reads=pt_.bufs, writes=ktok.bufs)
            self.rel(bt)
            x["ktok"] = ktok

        def X2(hd):
            x = ctx[hd]
            PB, FO = x["PB"], x["FO"]
            eb, kt, qt, v4, ktok = x["eb"], x["kt"], x["qt"], x["v4"], x["ktok"]
            bus = []
            for tb in range(NTB):
                bu = self.bank()
                bus.append(bu)
                o = self.pb(bu, 0, 512)
                self.mm(o, [(ktok.ap[:, tb * 128:(tb + 1) * 128], v4.ap[:, tb * 512:(tb + 1) * 512])],
                        ktok.bufs + v4.bufs)
            ba = self.bank()
            for tb in range(NTB):
                o = self.pb(ba, tb * 128, 128)
                self.mm(o, [(kt.ap[:, tb * 128:(tb + 1) * 128], qt.ap[:, tb * 128:(tb + 1) * 128])], kt.bufs + qt.bufs)
            ptm = self.ar(PB + 8)
            pa3 = bass.AP(self.t_ps, ba * 512, [[8 * 512, 128], [128, NTB], [1, 128]])
            pt3 = bass.AP(self.t_ar, (PB + 8) * T, [[c.NAR * T, 128], [128, NTB], [1, 128]])
            bd3 = bass.AP(self.t_cf, c.cfBD, [[c.NCF, 128], [0, NTB], [1, 128]])
            S.op("dve", lambda e: e.tensor_tensor(out=pt3, in0=pa3, in1=bd3, op=ALU.mult),
                 reads=[self.b_ps[ba], self.b_cf], writes=ptm.bufs)
            self.rel(ba)
            sh0 = Pl(self.t_sh[:, 0:128], [self.b_sh[0]])
            car = Pl(self.t_carry[:, hd * 128:(hd + 1) * 128], [self.b_carry[hd]])
            if first_tile:
                S.op("dve", lambda e: e.memset(sh0.ap, 0.0), writes=sh0.bufs)
            else:
                S.op("dve", lambda e: e.tensor_copy(out=sh0.ap, in_=car.ap), reads=car.bufs, writes=sh0.bufs)
            for ch in range(NCH):
                sp = Pl(self.t_sh[:, ch * 128:(ch + 1) * 128], [self.b_sh[ch]])
                sn = Pl(self.t_sh[:, (ch + 1) * 128:(ch + 2) * 128], [self.b_sh[ch + 1]])
                u = self.pb(bus[ch // 4], (ch % 4) * 128, 128)
                dec = self.t_f[:, (FO + 1) * T + ch * 32 + 31: (FO + 1) * T + ch * 32 + 32]
                S.op("dve", lambda e, sp=sp, sn=sn, u=u, dec=dec: e.scalar_tensor_tensor(
                    out=sn.ap, in0=sp.ap, scalar=dec, in1=u.ap, op0=ALU.mult, op1=ALU.add),
                    reads=sp.bufs + u.bufs + eb.bufs, writes=sn.bufs)
            for bu in bus:
                self.rel(bu)
            sb = self.ar(PB + 10, 4)
            shall = Pl(self.t_sh[:, 0:NCH * 128], self.b_sh[0:NCH])
            S.op("act", lambda e: e.activation(out=sb.ap, in_=shall.ap, func=AF.Copy), reads=shall.bufs, writes=sb.bufs)
            slast = Pl(self.t_sh[:, NCH * 128:(NCH + 1) * 128], [self.b_sh[NCH]])
            S.op("act", lambda e: e.activation(out=car.ap, in_=slast.ap, func=AF.Copy), reads=slast.bufs, writes=car.bufs)
            x["ptm"], x["sb"] = ptm, sb

        def P3(hd):
            x = ctx.pop(hd)
            vb, ptm, sb, qt, gs = x["vb"], x["ptm"], x["sb"], x["qt"], x["gs"]
            bo = self.bank()
            for tb in range(NTB):
                o = self.pb(bo, tb * 128, 128)

                def fn(e, tb=tb, o=o):
                    e.matmul(o.ap, vb.ap[:, tb * 128:(tb + 1) * 128], ptm.ap[:, tb * 128:(tb + 1) * 128],
                             start=True, stop=False)
                    ins = None
                    for j in range(4):
                        ch = tb * 4 + j
                        ins = e.matmul(self.t_ps[:, bo, ch * 32:(ch + 1) * 32], sb.ap[:, ch * 128:(ch + 1) * 128],
                                       qt.ap[:, ch * 32:(ch + 1) * 32], start=False, stop=(j == 3))
                    return ins
                S.op("pe", fn, reads=vb.bufs + ptm.bufs + sb.bufs + qt.bufs, writes=o.bufs)
            po = self.pb(bo)
            rs = self.rstd([po], 128, 4, 5)
            t1 = self.f(4)
            S.op("dve", lambda e: e.scalar_tensor_tensor(out=t1.ap, in0=po.ap, scalar=self.prm(c.pON), in1=rs.ap,
                                                         op0=ALU.mult, op1=ALU.mult),
                 reads=po.bufs + rs.bufs + [self.b_prm], writes=t1.bufs)
            self.rel(bo)
            on = self.ar(hd)
            S.op("dve", lambda e: e.tensor_tensor(out=on.ap, in0=t1.ap, in1=gs.ap, op=ALU.mult),
                 reads=t1.bufs + gs.bufs, writes=on.bufs)

        X1(0)
        yield
        X3(0)
        yield
        X4(0)
        yield
        for hd in range(c.HGH):
            nxt = hd + 1 < c.HGH
            if nxt:
                X1(hd + 1)
                yield
            X2(hd)
            yield
            if nxt:
                X3(hd + 1)
                yield
                X4(hd + 1)
                yield
            P3(hd)
            yield
        yield from self.add_proj(c.iWOUT, [self.ar(i) for i in range(c.HGH)])

    def rope_to(self, pk, dst, tmp_plane):
        c = self.cfg
        S = self.S
        T = c.T
        xb = self.ar(tmp_plane)
        S.op("act", lambda e: e.activation(out=xb.ap, in_=pk.ap, func=AF.Copy), reads=pk.bufs, writes=xb.bufs)
        bs = self.bank()
        ps_ = self.pb(bs)
        self.mm(ps_, [(self.cbv(c.cbSW, 128), xb.ap)], xb.bufs + [self.b_cb])
        t1 = self.f(4)
        S.op("dve", lambda e: e.tensor_tensor(out=t1.ap, in0=pk.ap, in1=self.t_rope[:, 0:T], op=ALU.mult),
             reads=pk.bufs + [self.b_rope], writes=t1.bufs)
        t2 = self.f(5)
        S.op("dve", lambda e: e.tensor_tensor(out=t2.ap, in0=ps_.ap, in1=self.t_rope[:, T:2 * T], op=ALU.mult),
             reads=ps_.bufs + [self.b_rope], writes=t2.bufs)
        self.rel(bs)
        S.op("dve", lambda e: e.tensor_tensor(out=dst.ap, in0=t1.ap, in1=t2.ap, op=ALU.add),
             reads=t1.bufs + t2.bufs, writes=dst.bufs)

    def load_rope(self, tile):
        c = self.cfg
        T = c.T
        self.S.dma("sp", [lambda e: e.dma_start(out=self.t_rope[:, 0:T], in_=self.d_rope[0, :, tile * T:(tile + 1) * T]),
                          lambda e: e.dma_start(out=self.t_rope[:, T:2 * T], in_=self.d_rope[1, :, tile * T:(tile + 1) * T])],
                   self.b_rope, writes=[self.b_rope])

    def kv(self, seq, tile):
        c = self.cfg
        S = self.S
        T = c.T
        NTB = c.NTB
        per = c.SA // c.DC
        self.stream_norm(6, self.xn)
        xs = [self.xn(i) for i in range(c.DC)]
        yield
        for hd in range(c.DAH):
            base = (hd % 2) * 4
            slK = self.slabA(c.iWK + hd)
            for comp in range(2):
                bk, pk = self.proj_fm(slK, comp * 128, xs)
                dst = self.ar(base + comp)
                self.rope_to(pk, dst, 8 + comp)
                self.rel(bk)
                yield
            slV = self.slabA(c.iWV + hd)
            vt = self.ar(base + 2, 2)
            for tb in range(NTB):
                bv = self.bank()
                o = self.pb(bv, 0, 256)
                pairs = [(xs[i].ap[:, tb * 128:(tb + 1) * 128], slV.ap[:, i * per: i * per + 256]) for i in range(c.DC)]
                rd = list(slV.bufs)
                for x_ in xs:
                    rd += x_.bufs
                self.mm(o, pairs, rd)
                S.op("act", lambda e, o=o, tb=tb: e.activation(out=vt.ap[:, tb * 256:(tb + 1) * 256], in_=o.ap, func=AF.Copy),
                     reads=o.bufs, writes=vt.bufs)
                self.rel(bv)
            kd, vd = self.b_kd[seq][hd], self.b_vd[seq][hd]
            ksrc = self.ar(base, 2)
            fns = []
            for comp in range(2):
                fns.append(lambda e, comp=comp: e.dma_start(
                    out=self.d_k[seq, hd, :, comp * c.S + tile * T: comp * c.S + (tile + 1) * T],
                    in_=self.t_ar[:, (base + comp) * T:(base + comp + 1) * T]))
            S.dma("sp", fns, kd, reads=ksrc.bufs, writes=[kd])
            vdst = self.d_v[seq, hd, tile * T:(tile + 1) * T, :].rearrange("(tb p) v -> p tb v", p=128)
            vsrc = self.t_ar[:, (base + 2) * T:(base + 4) * T].rearrange("p (tb v) -> p tb v", v=256)
            S.dma("sp", [lambda e: e.dma_start(out=vdst, in_=vsrc)], vd, reads=vt.bufs, writes=[vd])
            yield

    def dattn(self, seq, tile):
        c = self.cfg
        S = self.S
        T = c.T
        NTB = c.NTB
        self.stream_norm(5, self.xn)
        xs = [self.xn(i) for i in range(c.DC)]
        yield
        for hd in range(c.DAH):
            slQ = self.slabA(c.iWQ + hd)
            for comp in range(2):
                bq, pq = self.proj_fm(slQ, comp * 128, xs)
                self.rope_to(pq, self.ar(hd * 2 + comp), 16 + comp)
                self.rel(bq)
                yield
        ntok = (tile + 1) * T
        NKB = ntok // 128
        npl = 2 * c.S // T
        K0 = 18
        V0 = 18 + npl
        P0 = V0 + npl
        ident = self.cbv(c.cbID, 128)
        ones = self.cbv(c.cbONE, 128)
        scale = 128.0 ** -0.5
        pending = None
        for hd in range(c.DAH):
            kd, vd = self.b_kd[seq][hd], self.b_vd[seq][hd]
            ks = self.ar(K0, npl)
            vs = self.ar(V0, npl)
            kdst = self.t_ar[:, K0 * T: K0 * T + 2 * c.S].rearrange("p (c t) -> p c t", c=2)[:, :, 0:ntok]
            ksrc = self.d_k[seq, hd].rearrange("p (c t) -> p c t", c=2)[:, :, 0:ntok]
            S.dma("sp", [lambda e, kdst=kdst, ksrc=ksrc: e.dma_start(out=kdst, in_=ksrc)], ks.bufs[0], reads=[kd], writes=ks.bufs)
            vdst = self.t_ar[:, V0 * T: V0 * T + NKB * 256].rearrange("p (kb v) -> p kb v", v=256)
            vsrc = self.d_v[seq, hd, 0:ntok, :].rearrange("(kb p) v -> p kb v", p=128)
            S.dma("sp", [lambda e, vdst=vdst, vsrc=vsrc: e.dma_start(out=vdst, in_=vsrc)], vs.bufs[0], reads=[vd], writes=vs.bufs)
            FO = 8 * (hd % 2)
            od = [self.f(FO), self.f(FO + 1)]
            for comp in range(2):
                qt = self.ar(hd * 2 + comp)
                bo0, bo1, bl = self.bank(), self.bank(), self.bank()
                po0, po1, pl_ = self.pb(bo0), self.pb(bo1), self.pb(bl)
                def qk2(kb0, it):
                    nk = min(2, NKB - kb0)
                    bs = self.bank()
                    ps2 = self.pb(bs, 0, nk * T)
                    rd = ks.bufs + qt.bufs + [self.b_cb]

                    def fn(e):
                        ins = None
                        for u in range(nk):
                            kb = kb0 + u
                            o = self.t_ps[:, bs, u * T:(u + 1) * T]
                            jd = kb - tile * NTB
                            ins = e.matmul(o, self.t_ar[:, K0 * T + comp * c.S + kb * 128: K0 * T + comp * c.S + (kb + 1) * 128],
                                           qt.ap, start=True, stop=(jd < 0))
                            if jd >= 0:
                                ins = e.matmul(o, ident, self.cbv(c.cbMASK + jd * T, T), start=False, stop=True)
                        return ins
                    S.op("pe", fn, reads=rd, writes=ps2.bufs)
                    pt = self.ar(P0 + 2 * (it % 2), 2)
                    pt2 = Pl(pt.ap[:, 0:nk * T], pt.bufs)
                    S.op("act", lambda e: e.activation(out=pt2.ap, in_=ps2.ap, func=AF.Exp, scale=scale),
                         reads=ps2.bufs, writes=pt2.bufs)
                    self.rel(bs)
                    return (kb0, nk, pt2)

                def pv2(item):
                    kb0, nk, pt2 = item

                    def fn(e):
                        ins = None
                        for u in range(nk):
                            kb = kb0 + u
                            vblk = self.t_ar[:, V0 * T + kb * 256: V0 * T + (kb + 1) * 256]
                            p = pt2.ap[:, u * T:(u + 1) * T]
                            st, sp_ = (kb == 0), (kb == NKB - 1)
                            e.matmul(po0.ap, vblk[:, 0:128], p, start=st, stop=sp_)
                            e.matmul(po1.ap, vblk[:, 128:256], p, start=st, stop=sp_)
                            ins = e.matmul(pl_.ap, ones, p, start=st, stop=sp_)
                        return ins
                    S.op("pe", fn, reads=vs.bufs + pt2.bufs + [self.b_cb], writes=po0.bufs + po1.bufs + pl_.bufs)

                prev = None
                nit = (NKB + 1) // 2
                for it in range(nit + 1):
                    cur_it = qk2(2 * it, it) if it < nit else None
                    if prev is not None:
                        pv2(prev)
                    prev = cur_it
                    if comp == 0 and it == min(1, nit) and pending is not None:
                        pending()
                        pending = None
                    yield
                rl = self.f(FO + 2)
                S.op("act", lambda e: e.activation(out=rl.ap, in_=pl_.ap, func=AF.Ln), reads=pl_.bufs, writes=rl.bufs)
                self.rel(bl)
                S.op("act", lambda e: e.activation(out=rl.ap, in_=rl.ap, func=AF.Exp, scale=-1.0), reads=rl.bufs, writes=rl.bufs)
                for v, (bo, po) in enumerate(((bo0, po0), (bo1, po1))):
                    if comp == 0:
                        S.op("dve", lambda e, po=po, v=v: e.tensor_tensor(out=od[v].ap, in0=po.ap, in1=rl.ap, op=ALU.mult),
                             reads=po.bufs + rl.bufs, writes=od[v].bufs)
                    else:
                        t = self.f(FO + 3)
                        S.op("dve", lambda e, po=po, t=t: e.tensor_tensor(out=t.ap, in0=po.ap, in1=rl.ap, op=ALU.mult),
                             reads=po.bufs + rl.bufs, writes=t.bufs)
                        S.op("dve", lambda e, t=t, v=v: e.scalar_tensor_tensor(
                            out=od[v].ap, in0=t.ap, scalar=self.sm(3), in1=od[v].ap, op0=ALU.mult, op1=ALU.add),
                            reads=t.bufs + od[v].bufs + [self.b_sm], writes=od[v].bufs)
                    self.rel(bo)
            def epi(hd=hd, od=od):
                rs = self.rstd(od, 256, 4, 5)
                for v in range(2):
                    on = self.xn(hd * 2 + v)
                    S.op("dve", lambda e, on=on, v=v: e.scalar_tensor_tensor(
                        out=on.ap, in0=od[v].ap, scalar=self.sm(4 + v), in1=rs.ap, op0=ALU.mult, op1=ALU.mult),
                        reads=od[v].bufs + rs.bufs + [self.b_sm], writes=on.bufs)
            pending = epi
            yield
        if pending is not None:
            pending()
            yield
        yield from self.add_proj(c.iDWOUT, [self.xn(i) for i in range(c.DC)])

    def setup(self):
        c = self.cfg
        S = self.S
        S.dma("sp", [lambda e: e.dma_start(out=self.t_prm[:, :], in_=self.d_prm)], self.b_prm, writes=[self.b_prm])
        S.dma("sp", [lambda e: e.dma_start(out=self.t_cf[:, :], in_=self.d_cf)], self.b_cf, writes=[self.b_cf])
        S.dma("pool", [lambda e: e.dma_start(out=self.t_cb[:, :], in_=self.d_cb)], self.b_cb, writes=[self.b_cb])
        sm = self.sm
        H = c.HGH
        S.op("dve", lambda e: e.memset(self.t_sm[:, :], 0.0), writes=[self.b_sm])
        S.op("dve", lambda e: e.memset(sm(0), EPS), reads=[self.b_sm], writes=[self.b_sm])
        S.op("dve", lambda e: e.memset(sm(2), 1.0), reads=[self.b_sm], writes=[self.b_sm])
        S.op("dve", lambda e: e.tensor_tensor(out=sm(8, H), in0=self.prm(c.pLB0, H), in1=self.prm(c.pLB1, H), op=ALU.subtract),
             reads=[self.b_prm, self.b_sm], writes=[self.b_sm])
        S.op("act", lambda e: e.activation(out=sm(8, H), in_=sm(8, H), func=AF.Sigmoid), reads=[self.b_sm], writes=[self.b_sm])
        S.op("dve", lambda e: e.tensor_scalar(out=sm(8 + H, H), in0=sm(8, H), scalar1=-1.0, scalar2=1.0, op0=ALU.mult, op1=ALU.add),
             reads=[self.b_sm], writes=[self.b_sm])
        S.op("dve", lambda e: e.tensor_scalar(out=sm(8 + 2 * H, H), in0=sm(8 + H, H), scalar1=-1.0, scalar2=None, op0=ALU.mult),
             reads=[self.b_sm], writes=[self.b_sm])
        L = c.pLAM
        S.op("dve", lambda e: e.tensor_tensor(out=self.t_lt[:, 0:128], in0=self.prm(L, 128), in1=self.prm(L + 128, 128), op=ALU.mult),
             reads=[self.b_prm], writes=[self.b_lt])
        S.op("dve", lambda e: e.tensor_tensor(out=self.t_lt[:, 128:256], in0=self.prm(L + 256, 128), in1=self.prm(L + 384, 128), op=ALU.mult),
             reads=[self.b_prm, self.b_lt], writes=[self.b_lt])
        S.op("dve", lambda e: e.reduce_sum(out=sm(6, 2), in_=self.t_lt[:, :].rearrange("p (a b) -> p a b", a=2), axis=AX.X),
             reads=[self.b_lt, self.b_sm], writes=[self.b_sm])
        S.op("act", lambda e: e.activation(out=sm(6, 2), in_=sm(6, 2), func=AF.Exp), reads=[self.b_sm], writes=[self.b_sm])
        S.op("dve", lambda e: e.tensor_tensor(out=sm(1), in0=sm(6), in1=sm(7), op=ALU.subtract), reads=[self.b_sm], writes=[self.b_sm])
        S.op("dve", lambda e: e.tensor_scalar(out=sm(3), in0=sm(1), scalar1=c.lam_init, scalar2=-1.0, op0=ALU.add, op1=ALU.mult),
             reads=[self.b_sm], writes=[self.b_sm])
        S.op("dve", lambda e: e.tensor_scalar(out=sm(4, 2), in0=self.prm(c.pSUB, 2), scalar1=1.0 - c.lam_init, scalar2=None, op0=ALU.mult),
             reads=[self.b_sm, self.b_prm], writes=[self.b_sm])

    def final(self, seq, tile):
        c = self.cfg
        T = c.T
        outp = self.arf(0, c.DC)
        self.stream_norm(9, lambda i: Pl(self.t_ar[:, 2 * i * T:(2 * i + 2) * T].bitcast(F32), self.b_ar[2 * i:2 * i + 2]))
        src = self.t_ar[:, 0:2 * c.DC * T].bitcast(F32)
        self.S.dma("sp", [lambda e: e.dma_start(out=self.d_out[seq, tile], in_=src)], self.cur.b_out, reads=outp.bufs, writes=[self.cur.b_out])

    def seq_gen(self, st):
        c = self.cfg
        S = self.S
        seq = st.seq
        for tile in range(c.NT):
            st.tile = tile
            hall = Pl(self.t_h[:, :], self.b_h)
            S.dma("sp", [lambda e: e.dma_start(out=self.t_h[:, :], in_=self.d_x[seq, tile])],
                  self.b_h[0], writes=hall.bufs)
            self.load_rope(tile)
            stg = self.stages
            if "ffn00" in stg: yield from self.ffn(0, 0)
            if "hgrn" in stg: yield from self.hgrn(tile == 0)
            if "ffn01" in stg: yield from self.ffn(0, 1)
            if "ple0" in stg: yield from self.ple(0, seq, tile)
            if "kv" in stg: yield from self.kv(seq, tile)
            if "ffn10" in stg: yield from self.ffn(1, 0)
            if "dattn" in stg: yield from self.dattn(seq, tile)
            if "ffn11" in stg: yield from self.ffn(1, 1)
            if "ple1" in stg: yield from self.ple(1, seq, tile)
            self.final(seq, tile)
            yield

    def build(self):
        c = self.cfg
        self.setup()
        gens = [self.seq_gen(st) for st in self.streams]
        alive = [True] * len(gens)
        steps = [0] * len(gens)
        while any(alive):
            for i, g in enumerate(gens):
                if not alive[i]:
                    continue
                if i > 0 and alive[i - 1] and steps[i - 1] - steps[i] < c.LAG:
                    continue
                self.cur = self.streams[i]
                try:
                    next(g)
                    steps[i] += 1
                except StopIteration:
                    alive[i] = False
        self.S.final_wait("sp", [st.b_out for st in self.streams])
        return self.nc


_CACHE = {}


def kernel(**inputs):
    cfg = Cfg()
    n = 8
    x = np.asarray(inputs["x"], np.float32)
    p = np.asarray(inputs["p"], np.float32)
    inp = {k: np.asarray(v) for k, v in inputs.items()}
    wA, wB = host_weights(inp, cfg)
    prm = host_params(inp, cfg)
    cb, cf, rope = host_consts(cfg)
    nc = Prog(cfg).build()
    in_maps = []
    for i in range(n):
        xs = x[i * cfg.NSEQ:(i + 1) * cfg.NSEQ]
        ps = p[:, i * cfg.NSEQ:(i + 1) * cfg.NSEQ]
        xT, pT = host_acts(xs, ps, cfg)
        in_maps.append({"xT": xT.reshape(cfg.NSEQ, cfg.NT, 128, -1), "pT": pT.reshape(2, cfg.NSEQ, cfg.NT, 128, -1),
                        "wA": wA, "wB": wB, "prm": prm, "cb": cb, "cf": cf, "rope": rope})
    res = run_bass_kernel_spmd(nc, in_maps, core_ids=list(range(n)))
    outs = []
    for i in range(n):
        oT = np.asarray(res.results[i]["outT"]).reshape(cfg.NSEQ, cfg.NT, 128, cfg.DC, cfg.T)
        outs.append(host_out(oT, cfg))
    return np.concatenate(outs, axis=0).astype(np.float32)
```

```python
import math
import numpy as np
import concourse.bass as bass
import concourse.mybir as mybir
from concourse.bass_utils import run_bass_kernel_spmd

F32 = mybir.dt.float32
BF16 = mybir.dt.bfloat16
AF = mybir.ActivationFunctionType
ALU = mybir.AluOpType
AX = mybir.AxisListType

EPS = 1e-6
ROPE_THETA = 10000.0
NEG = -30000.0


ALL_STAGES = ("ffn00", "hgrn", "ffn01", "ple0", "kv", "ffn10", "dattn", "ffn11", "ple1")


class Cfg:
    def __init__(self, D=2048, DFF=5632, S=2048, T=256, NSEQ=2, PLE=256, NSLOT=5, LAG=2):
        self.D, self.DFF, self.S, self.T, self.NSEQ, self.PLE, self.NSLOT = D, DFF, S, T, NSEQ, PLE, NSLOT
        self.LAG = LAG
        self.DC = D // 128
        self.FB = DFF // 128
        self.HGH = D // 128
        self.DAH = D // 256
        self.NT = S // T
        self.NTB = T // 128
        self.NCH = T // 32
        self.PC = PLE // 128
        assert self.PC * D == self.DC * 256
        self.SA = self.DC * 256
        self.FH = self.FB // 2
        assert self.FH * 2 == self.FB
        self.SB = self.FH * 128
        self.SLOTE = max(self.SA, self.SB)
        self.NJ = self.DC // 2
        o = 0
        self.iGU = o; o += 4 * self.FB
        self.iWINA = o; o += self.HGH
        self.iWINB = o; o += self.HGH
        self.iWOUT = o; o += self.NJ
        self.iWK = o; o += self.DAH
        self.iWV = o; o += self.DAH
        self.iWQ = o; o += self.DAH
        self.iDWOUT = o; o += self.NJ
        self.iPLEG = o; o += 2 * self.NJ
        self.iPLEP = o; o += 2
        self.NA = o
        self.NB = 4 * self.DC * 2
        c = 0
        self.pG = c; c += 10 * self.DC
        self.pLB0 = c; c += self.HGH
        self.pLB1 = c; c += self.HGH
        self.pON = c; c += 1
        self.pSUB = c; c += 2
        self.pLAM = c; c += 512
        self.NP = c
        self.cbID = 0
        self.cbONE = 128
        self.cbSW = 256
        self.cbMASK = 384
        self.cbM4 = 384 + self.NTB * T
        self.NCB = 384 + self.NTB * T + 4
        self.cfID = 0
        self.cfBD = 128
        self.cfM4 = 256
        self.cfSCAN = 260
        self.NCF = 260 + T
        self.NAR = max(self.FB, 22 + 2 * (2 * S // T), 2 * self.DC, 46)
        self.lam_init = 0.8 - 0.6 * float(np.exp(-0.3 * 1))


def _slabA(W, cols, DC):
    sub = W[:, cols]
    return sub.reshape(DC, 128, -1).transpose(1, 0, 2).reshape(128, -1)


def host_weights(inp, cfg):
    D, DC, DFF, FB = cfg.D, cfg.DC, cfg.DFF, cfg.FB
    wA = np.empty((cfg.NA, 128, cfg.SA), np.float32)
    wB = np.empty((cfg.NB, 128, cfg.SB), np.float32)
    ar = np.arange
    for l in range(2):
        for f in range(2):
            Wgu = inp["ffn_w_gate_up"][l, f]
            for fb in range(FB):
                cols = np.concatenate([ar(fb * 128, fb * 128 + 128), DFF + ar(fb * 128, fb * 128 + 128)])
                wA[cfg.iGU + (l * 2 + f) * FB + fb] = _slabA(Wgu, cols, DC)
            Wd = inp["ffn_w_down"][l, f]
            for ob in range(DC):
                sub = Wd[:, ob * 128:(ob + 1) * 128].reshape(2, cfg.FH, 128, 128)
                for hf in range(2):
                    wB[((l * 2 + f) * DC + ob) * 2 + hf] = sub[hf].transpose(1, 0, 2).reshape(128, -1)
    Win = inp["hgrn_w_in"][0]
    for hd in range(cfg.HGH):
        r = ar(hd * 128, hd * 128 + 128)
        wA[cfg.iWINA + hd] = _slabA(Win, np.concatenate([r, D + r]), DC)
        wA[cfg.iWINB + hd] = _slabA(Win, np.concatenate([2 * D + r, 3 * D + r]), DC)
    for j in range(cfg.NJ):
        r = ar(j * 256, j * 256 + 256)
        wA[cfg.iWOUT + j] = _slabA(inp["hgrn_w_out"][0], r, DC)
        wA[cfg.iDWOUT + j] = _slabA(inp["diff_w_out"][0], r, DC)
        for l in range(2):
            wA[cfg.iPLEG + l * cfg.NJ + j] = _slabA(inp["ple_w_gate"][l], r, DC)
    for hd in range(cfg.DAH):
        r = ar(hd * 256, hd * 256 + 256)
        wA[cfg.iWK + hd] = _slabA(inp["w_kv"], r, DC)
        wA[cfg.iWV + hd] = _slabA(inp["w_kv"], D + r, DC)
        wA[cfg.iWQ + hd] = _slabA(inp["diff_w_q"][0], r, DC)
    for l in range(2):
        Wp = inp["ple_w_proj"][l]
        wA[cfg.iPLEP + l] = Wp.reshape(cfg.PC, 128, D).transpose(1, 0, 2).reshape(128, -1)
    return wA, wB


def host_params(inp, cfg):
    DC = cfg.DC
    prm = np.zeros((128, cfg.NP), np.float32)

    def fm(v):
        return np.asarray(v, np.float32).reshape(-1, 128).T

    gl = [inp["ffn_norm"][0, 0], inp["ffn_norm"][0, 1], inp["ffn_norm"][1, 0], inp["ffn_norm"][1, 1],
          inp["mix_norm"][0], inp["mix_norm"][1], inp["kv_norm"], inp["ple_norm"][0], inp["ple_norm"][1],
          inp["final_norm"]]
    for i, g in enumerate(gl):
        prm[:, cfg.pG + i * DC: cfg.pG + (i + 1) * DC] = fm(g)
    prm[:, cfg.pLB0:cfg.pLB0 + cfg.HGH] = fm(inp["hgrn_lower_bounds"][0])
    prm[:, cfg.pLB1:cfg.pLB1 + cfg.HGH] = fm(inp["hgrn_lower_bounds"][1])
    prm[:, cfg.pON] = np.asarray(inp["hgrn_out_norm"][0], np.float32)
    prm[:, cfg.pSUB:cfg.pSUB + 2] = fm(inp["diff_subln"][0])
    prm[:, cfg.pLAM:cfg.pLAM + 512] = np.broadcast_to(
        np.asarray(inp["diff_lambda"][0], np.float32).reshape(1, 512), (128, 512))
    return prm


def host_consts(cfg):
    T, S = cfg.T, cfg.S
    cb = np.zeros((128, cfg.NCB), np.float32)
    cb[:, cfg.cbID:cfg.cbID + 128] = np.eye(128)
    cb[:, cfg.cbONE:cfg.cbONE + 128] = 1.0
    sw = np.zeros((128, 128), np.float32)
    for d in range(128):
        sw[d, (d + 64) % 128] = 1.0
    cb[:, cfg.cbSW:cfg.cbSW + 128] = sw
    p = np.arange(128)[:, None]
    q = np.arange(T)[None, :]
    for j in range(cfg.NTB):
        cb[:, cfg.cbMASK + j * T: cfg.cbMASK + (j + 1) * T] = np.where(j * 128 + p <= q, 0.0, NEG)
    cb[:, cfg.cbM4:cfg.cbM4 + 4] = (np.arange(128)[:, None] // 32 == np.arange(4)[None, :]).astype(np.float32)
    cf = np.zeros((128, cfg.NCF), np.float32)
    cf[:, cfg.cfID:cfg.cfID + 128] = np.eye(128)
    s = np.arange(128)[:, None]
    t = np.arange(128)[None, :]
    cf[:, cfg.cfBD:cfg.cfBD + 128] = ((s <= t) & (s // 32 == t // 32)).astype(np.float32)
    cf[:, cfg.cfM4:cfg.cfM4 + 4] = (s // 32 == np.arange(4)[None, :]).astype(np.float32)
    cf[:, cfg.cfSCAN:cfg.cfSCAN + T] = (np.arange(T) % 32 != 0).astype(np.float32)[None, :]
    inv = (ROPE_THETA ** (-np.arange(0, 128, 2, dtype=np.float32) / 128)).astype(np.float32)
    ang = np.arange(S, dtype=np.float32)[None, :] * np.concatenate([inv, inv])[:, None]
    rope = np.stack([np.cos(ang), np.sin(ang) * np.where(np.arange(128) < 64, -1.0, 1.0)[:, None]]).astype(np.float32)
    return cb, cf, rope


def host_acts(x, p, cfg):
    NS, NT, T, DC, PC = cfg.NSEQ, cfg.NT, cfg.T, cfg.DC, cfg.PC
    xT = np.ascontiguousarray(x.reshape(NS, NT, T, DC, 128).transpose(0, 1, 4, 3, 2))
    pT = np.ascontiguousarray(p.reshape(2, NS, NT, T, PC, 128).transpose(0, 1, 2, 5, 4, 3))
    return xT, pT


def host_out(oT, cfg):
    return np.ascontiguousarray(oT.transpose(0, 1, 4, 3, 2)).reshape(cfg.NSEQ, cfg.S, cfg.D)


class Buf:
    __slots__ = ("name", "w", "r", "dsem", "dcnt", "excl")

    def __init__(self, name, excl=False):
        self.name = name
        self.excl = excl
        self.w = {}
        self.r = {}
        self.dsem = None
        self.dcnt = 0


class Sched:
    def __init__(self, nc):
        self.nc = nc
        self.engs = {"pe": nc.tensor, "act": nc.scalar, "dve": nc.vector, "pool": nc.gpsimd, "sp": nc.sync}
        self.csem = {k: nc.alloc_semaphore("cs_" + k) for k in self.engs}
        self.cnt = {k: 0 for k in self.engs}
        self.seen = {k: {} for k in self.engs}
        self.nsem = 0

    def new_dsem(self, buf):
        buf.dsem = self.nc.alloc_semaphore("ds_%d_%s" % (self.nsem, buf.name))
        self.nsem += 1

    def _wait(self, e, reads, writes):
        need = {}
        own = self.csem[e]
        for b in reads:
            for s, v in b.w.items():
                if need.get(s, 0) < v:
                    need[s] = v
            if b.excl:
                for s, v in b.r.items():
                    if s is not own and s != own and need.get(s, 0) < v:
                        need[s] = v
        for b in writes:
            for s, v in b.w.items():
                if need.get(s, 0) < v:
                    need[s] = v
            for s, v in b.r.items():
                if need.get(s, 0) < v:
                    need[s] = v
        seen = self.seen[e]
        for s, v in need.items():
            if seen.get(s, 0) >= v:
                continue
            self.engs[e].wait_ge(s, v)
            seen[s] = v

    def _mark(self, s, v, reads, writes):
        for b in reads:
            if b.r.get(s, 0) < v:
                b.r[s] = v
        for b in writes:
            if b.w.get(s, 0) < v:
                b.w[s] = v

    def op(self, e, fn, reads=(), writes=()):
        self._wait(e, reads, writes)
        ins = fn(self.engs[e])
        self.cnt[e] += 1
        s = self.csem[e]
        ins.then_inc(s, 1)
        self._mark(s, self.cnt[e], reads, writes)

    def dma(self, q, fns, owner, reads=(), writes=()):
        self._wait(q, reads, writes)
        if owner.dsem is None:
            self.new_dsem(owner)
        for fn in fns:
            fn(self.engs[q]).then_inc(owner.dsem, 16)
        owner.dcnt += 16 * len(fns)
        self._mark(owner.dsem, owner.dcnt, reads, writes)

    def final_wait(self, e, bufs):
        self._wait(e, bufs, bufs)


class Pl:
    __slots__ = ("ap", "bufs")

    def __init__(self, ap, bufs):
        self.ap = ap
        self.bufs = bufs


class Stream:
    def __init__(self, nc, c, i, NF):
        A = nc.alloc_sbuf_tensor
        T = c.T
        n = "_s%d" % i
        self.seq = i
        self.t_h = A("h" + n, [128, c.DC * T], F32)
        self.t_xn = A("xn" + n, [128, c.DC * T], BF16)
        self.t_ar = A("arena" + n, [128, c.NAR * T], BF16)
        self.t_sh = A("sh" + n, [128, (c.NCH + 1) * 128], F32)
        self.t_carry = A("carry" + n, [128, c.HGH * 128], F32)
        self.t_f = A("ftmp" + n, [128, NF * T], F32)
        self.t_rope = A("ropes" + n, [128, 2 * T], F32)
        self.t_pb = A("pb" + n, [128, c.PC * T], BF16)
        self.t_sq = A("sqtmp" + n, [128, 2 * T], BF16)
        B = Buf
        self.b_h = [B("h%d" % j + n) for j in range(c.DC)]
        self.b_xn = [B("xn%d" % j + n) for j in range(c.DC)]
        self.b_ar = [B("ar%d" % j + n) for j in range(c.NAR)]
        self.b_sh = [B("sh%d" % j + n) for j in range(c.NCH + 1)]
        self.b_carry = [B("carry%d" % j + n) for j in range(c.HGH)]
        self.b_f = [B("f%d" % j + n) for j in range(NF)]
        self.b_rope = B("rope" + n)
        self.b_pb = B("pb" + n)
        self.b_sq = [B("sq0" + n), B("sq1" + n)]
        self.sq_i = 0
        self.tile = 0
        self.b_out = B("out" + n)


class Prog:
    def __init__(self, cfg):
        self.cfg = cfg
        c = cfg
        nc = bass.Bass("TRN2", target_bir_lowering=False)
        self.nc = nc
        T = c.T
        self.d_x = nc.dram_tensor("xT", [c.NSEQ, c.NT, 128, c.DC * T], F32, kind="ExternalInput").ap()
        self.d_p = nc.dram_tensor("pT", [2, c.NSEQ, c.NT, 128, c.PC * T], F32, kind="ExternalInput").ap()
        self.d_wA = nc.dram_tensor("wA", [c.NA, 128, c.SA], F32, kind="ExternalInput").ap()
        self.d_wB = nc.dram_tensor("wB", [c.NB, 128, c.SB], F32, kind="ExternalInput").ap()
        self.d_prm = nc.dram_tensor("prm", [128, c.NP], F32, kind="ExternalInput").ap()
        self.d_cb = nc.dram_tensor("cb", [128, c.NCB], F32, kind="ExternalInput").ap()
        self.d_cf = nc.dram_tensor("cf", [128, c.NCF], F32, kind="ExternalInput").ap()
        self.d_rope = nc.dram_tensor("rope", [2, 128, c.S], F32, kind="ExternalInput").ap()
        self.d_out = nc.dram_tensor("outT", [c.NSEQ, c.NT, 128, c.DC * T], F32, kind="ExternalOutput").ap()
        self.d_k = nc.dram_tensor("kscr", [c.NSEQ, c.DAH, 128, 2 * c.S], BF16).ap()
        self.d_v = nc.dram_tensor("vscr", [c.NSEQ, c.DAH, c.S, 256], BF16).ap()
        self.NAH = (c.NA + 2) // 3
        self.d_wAb = [nc.dram_tensor("wAb%d" % i, [self.NAH, 128, c.SA], BF16).ap() for i in range(3)]
        self.d_wBb = nc.dram_tensor("wBb", [c.NB, 128, c.SB], BF16).ap()
        A = nc.alloc_sbuf_tensor
        self.NF = 12
        self.streams = [Stream(nc, c, i, self.NF) for i in range(c.NSEQ)]
        self.cur = self.streams[0]
        self.t_slot = [A("slot%d" % i, [128, c.SLOTE], BF16) for i in range(c.NSLOT)]
        self.t_cb = A("cbs", [128, c.NCB], BF16)
        self.t_cf = A("cfs", [128, c.NCF], F32)
        self.t_prm = A("prms", [128, c.NP], F32)
        self.t_sm = A("small", [128, 64], F32)
        self.t_lt = A("lamtmp", [128, 256], F32)
        self.t_ps = nc.alloc_psum_tensor("ps", [128, 8, 512], F32)
        self.WP0 = c.NAR - (c.SA + T - 1) // T
        B = Buf
        self.b_slot = [B("slot%d" % i) for i in range(c.NSLOT)]
        self.b_cb = B("cb")
        self.b_cf = B("cf")
        self.b_prm = B("prm")
        self.b_sm = B("sm")
        self.b_lt = B("lt")
        self.b_ps = [B("ps%d" % i, excl=True) for i in range(8)]
        self.b_kd = [[B("kd%d_%d" % (s, h)) for h in range(c.DAH)] for s in range(c.NSEQ)]
        self.b_vd = [[B("vd%d_%d" % (s, h)) for h in range(c.DAH)] for s in range(c.NSEQ)]
        self.slab_cache = {}
        self.slot_key = [None] * c.NSLOT
        self.slot_dirty = [None] * c.NSLOT
        self.b_sst = [B("sst%d" % i) for i in range(c.NSLOT)]
        self.b_wsc = {}
        self.stored = set()
        self.wp_key = None
        self.S = Sched(nc)
        self.free_banks = list(range(8))
        self.slot_i = 0
        self.stages = ALL_STAGES

    t_h = property(lambda self: self.cur.t_h)
    t_xn = property(lambda self: self.cur.t_xn)
    t_ar = property(lambda self: self.cur.t_ar)
    t_f = property(lambda self: self.cur.t_f)
    t_sh = property(lambda self: self.cur.t_sh)
    t_carry = property(lambda self: self.cur.t_carry)
    t_rope = property(lambda self: self.cur.t_rope)
    t_pb = property(lambda self: self.cur.t_pb)
    t_sq = property(lambda self: self.cur.t_sq)
    b_h = property(lambda self: self.cur.b_h)
    b_xn = property(lambda self: self.cur.b_xn)
    b_ar = property(lambda self: self.cur.b_ar)
    b_f = property(lambda self: self.cur.b_f)
    b_sh = property(lambda self: self.cur.b_sh)
    b_carry = property(lambda self: self.cur.b_carry)
    b_rope = property(lambda self: self.cur.b_rope)
    b_pb = property(lambda self: self.cur.b_pb)
    b_sq = property(lambda self: self.cur.b_sq)

    def h(self, i):
        T = self.cfg.T
        return Pl(self.t_h[:, i * T:(i + 1) * T], [self.b_h[i]])

    def xn(self, i):
        T = self.cfg.T
        return Pl(self.t_xn[:, i * T:(i + 1) * T], [self.b_xn[i]])

    def ar(self, i, n=1):
        T = self.cfg.T
        return Pl(self.t_ar[:, i * T:(i + n) * T], self.b_ar[i:i + n])

    def arf(self, i, n=1):
        T = self.cfg.T
        return Pl(self.t_ar[:, i * T:(i + 2 * n) * T].bitcast(F32), self.b_ar[i:i + 2 * n])

    def f(self, i):
        T = self.cfg.T
        return Pl(self.t_f[:, i * T:(i + 1) * T], [self.b_f[i]])

    def prm(self, col, n=1):
        return self.t_prm[:, col:col + n]

    def cbv(self, col, n):
        return self.t_cb[:, col:col + n]

    def cfv(self, col, n):
        return self.t_cf[:, col:col + n]

    def sm(self, col, n=1):
        return self.t_sm[:, col:col + n]

    def bank(self):
        i = self.free_banks.pop(0)
        return i

    def rel(self, i):
        self.free_banks.append(i)

    def pb(self, i, lo=0, n=None):
        n = self.cfg.T if n is None else n
        return Pl(self.t_ps[:, i, lo:lo + n], [self.b_ps[i]])

    def _slab(self, key, src, dst, n):
        c = self.cfg
        k = self.slab_cache.get(key)
        if k is not None and self.slot_key[k] == key:
            return Pl(self.t_slot[k], [self.b_slot[k]])
        k = self.slot_i
        self.slot_i = (k + 1) % c.NSLOT
        t, b = self.t_slot[k], self.b_slot[k]
        if self.slot_dirty[k] is not None:
            dkey, ddst, dn = self.slot_dirty[k]
            wb = self.b_wsc.setdefault(dkey, Buf("wsc%s%d" % dkey))
            self.S.dma("sp", [lambda e: e.dma_start(out=ddst, in_=t[:, 0:dn])], self.b_sst[k], reads=[b], writes=[wb])
            self.stored.add(dkey)
            self.slot_dirty[k] = None
        self.slot_key[k] = key
        self.slab_cache[key] = k
        if key in self.stored:
            wb = self.b_wsc[key]
            self.S.dma("pool", [lambda e: e.dma_start(out=t[:, 0:n], in_=dst)], b, reads=[wb], writes=[b])
        else:
            self.S.dma("pool", [lambda e: e.dma_start(out=t[:, 0:n], in_=src)], b, writes=[b])
            self.slot_dirty[k] = (key, dst, n)
        return Pl(t, [b])

    def slabA(self, idx):
        return self._slab(("A", idx), self.d_wA[idx], self.d_wAb[idx // self.NAH][idx % self.NAH], self.cfg.SA)

    def slabB(self, idx):
        return self._slab(("B", idx), self.d_wB[idx], self.d_wBb[idx], self.cfg.SB)

    def tile_of_cur(self):
        return self.cur.tile

    def wproj(self, l):
        c = self.cfg
        T = c.T
        s0 = self.streams[0]
        n = (c.SA + T - 1) // T
        ap = s0.t_ar[:, self.WP0 * T: self.WP0 * T + c.SA]
        bufs = s0.b_ar[self.WP0:self.WP0 + n]
        key = (l, self.cur.tile)
        if self.wp_key != key:
            self.wp_key = key
            src = self.d_wA[c.iPLEP + l]
            self.S.dma("pool", [lambda e: e.dma_start(out=ap, in_=src)], bufs[0], writes=bufs)
        return Pl(ap, bufs)

    def mm(self, out, pairs, reads):
        def fn(e):
            n = len(pairs)
            ins = None
            for i, (l, r) in enumerate(pairs):
                ins = e.matmul(out.ap, l, r, start=(i == 0), stop=(i == n - 1))
            return ins
        self.S.op("pe", fn, reads=reads, writes=out.bufs)

    def rstd(self, srcs, dim, fa, fb):
        c = self.cfg
        T = c.T
        S = self.S
        bk = self.bank()
        bp = self.pb(bk)
        ones = self.cbv(c.cbONE, 128)
        n = len(srcs)
        for i, s in enumerate(srcs):
            k = self.cur.sq_i
            self.cur.sq_i ^= 1
            sq = Pl(self.t_sq[:, k * T:(k + 1) * T], [self.b_sq[k]])
            if n > 2 and i % 2 == 1:
                S.op("dve", lambda e, s=s, sq=sq: e.tensor_tensor(out=sq.ap, in0=s.ap, in1=s.ap, op=ALU.mult),
                     reads=s.bufs, writes=sq.bufs)
            else:
                S.op("act", lambda e, s=s, sq=sq: e.activation(out=sq.ap, in_=s.ap, func=AF.Square),
                     reads=s.bufs, writes=sq.bufs)
            S.op("pe", lambda e, sq=sq, i=i: e.matmul(bp.ap, ones, sq.ap, start=(i == 0), stop=(i == n - 1)),
                 reads=sq.bufs + [self.b_cb], writes=bp.bufs)
        sd = self.f(fa)
        S.op("act", lambda e: e.activation(out=sd.ap, in_=bp.ap, func=AF.Ln, bias=self.sm(0), scale=1.0 / dim),
             reads=bp.bufs + [self.b_sm], writes=sd.bufs)
        self.rel(bk)
        rs = self.f(fb)
        S.op("act", lambda e: e.activation(out=rs.ap, in_=sd.ap, func=AF.Exp, scale=-0.5), reads=sd.bufs, writes=rs.bufs)
        return rs

    def stream_norm(self, gi, dst_fn):
        c = self.cfg
        rs = self.rstd([self.h(i) for i in range(c.DC)], c.D, 6, 7)
        for i in range(c.DC):
            hp = self.h(i)
            d = dst_fn(i)
            g = self.prm(c.pG + gi * c.DC + i)
            self.S.op("dve", lambda e, hp=hp, d=d, g=g: e.scalar_tensor_tensor(
                out=d.ap, in0=hp.ap, scalar=g, in1=rs.ap, op0=ALU.mult, op1=ALU.mult),
                reads=hp.bufs + rs.bufs + [self.b_prm], writes=d.bufs)

    def proj_fm(self, slab, col, srcs, ncols=128):
        c = self.cfg
        bk = self.bank()
        out = self.pb(bk)
        W = slab.ap
        n = len(srcs)
        per = c.SA // c.DC if n == c.DC else None
        pairs = []
        rd = list(slab.bufs)
        for i, s in enumerate(srcs):
            pairs.append((W[:, i * per + col: i * per + col + ncols], s.ap))
            rd += s.bufs
        self.mm(out, pairs, rd)
        return bk, out

    def ffn(self, l, f):
        c = self.cfg
        S = self.S
        T = c.T
        self.stream_norm(l * 2 + f, self.xn)
        xs = [self.xn(i) for i in range(c.DC)]
        yield
        for fb in range(c.FB):
            slab = self.slabA(c.iGU + (l * 2 + f) * c.FB + fb)
            bg, pg = self.proj_fm(slab, 0, xs)
            bu, pu = self.proj_fm(slab, 128, xs)
            sg = self.f(fb % 2)
            S.op("act", lambda e, sg=sg, pg=pg: e.activation(out=sg.ap, in_=pg.ap, func=AF.Silu),
                 reads=pg.bufs, writes=sg.bufs)
            self.rel(bg)
            a = self.ar(fb)
            S.op("dve", lambda e, a=a, sg=sg, pu=pu: e.tensor_tensor(out=a.ap, in0=sg.ap, in1=pu.ap, op=ALU.mult),
                 reads=sg.bufs + pu.bufs, writes=a.bufs)
            self.rel(bu)
            yield
        acts = [self.ar(i) for i in range(c.FB)]
        for ob in range(c.DC):
            bk = self.bank()
            out = self.pb(bk)
            for hf in range(2):
                sl = self.slabB(((l * 2 + f) * c.DC + ob) * 2 + hf)
                rd = list(sl.bufs)
                for a in acts[hf * c.FH:(hf + 1) * c.FH]:
                    rd += a.bufs

                def fn(e, sl=sl, hf=hf):
                    ins = None
                    for i in range(c.FH):
                        ins = e.matmul(out.ap, sl.ap[:, i * 128:(i + 1) * 128], acts[hf * c.FH + i].ap,
                                       start=(hf == 0 and i == 0), stop=(hf == 1 and i == c.FH - 1))
                    return ins
                S.op("pe", fn, reads=rd, writes=out.bufs)
                if hf == 0:
                    yield
            hp = self.h(ob)
            S.op("dve", lambda e, hp=hp, out=out: e.scalar_tensor_tensor(
                out=hp.ap, in0=out.ap, scalar=0.5, in1=hp.ap, op0=ALU.mult, op1=ALU.add),
                reads=out.bufs + hp.bufs, writes=hp.bufs)
            self.rel(bk)
            yield

    def add_proj(self, base, srcs):
        c = self.cfg
        for j in range(c.NJ):
            slab = self.slabA(base + j)
            for half in range(2):
                bk, out = self.proj_fm(slab, half * 128, srcs)
                hp = self.h(j * 2 + half)
                self.S.op("dve", lambda e, hp=hp, out=out: e.tensor_tensor(out=hp.ap, in0=out.ap, in1=hp.ap, op=ALU.add),
                          reads=out.bufs + hp.bufs, writes=hp.bufs)
                self.rel(bk)
            yield

    def ple(self, l, seq, tile):
        c = self.cfg
        S = self.S
        T = c.T
        self.stream_norm(7 + l, self.xn)
        xs = [self.xn(i) for i in range(c.DC)]
        src = self.d_p[l, seq, tile]
        S.dma("pool", [lambda e: e.dma_start(out=self.t_pb[:, :], in_=src)], self.b_pb, writes=[self.b_pb])
        wp = self.wproj(l)
        yield
        for j in range(c.NJ):
            slab = self.slabA(c.iPLEG + l * c.NJ + j)
            for half in range(2):
                ob = j * 2 + half
                bg, pg = self.proj_fm(slab, half * 128, xs)
                bp_ = self.bank()
                pp = self.pb(bp_)
                pairs = [(wp.ap[:, pc * c.D + ob * 128: pc * c.D + ob * 128 + 128], self.t_pb[:, pc * T:(pc + 1) * T])
                         for pc in range(c.PC)]
                self.mm(pp, pairs, wp.bufs + [self.b_pb])
                sg = self.f(ob % 2)
                S.op("act", lambda e, sg=sg, pg=pg: e.activation(out=sg.ap, in_=pg.ap, func=AF.Sigmoid),
                     reads=pg.bufs, writes=sg.bufs)
                self.rel(bg)
                t2 = self.f(2 + ob % 2)
                S.op("dve", lambda e, t2=t2, sg=sg, pp=pp: e.tensor_tensor(out=t2.ap, in0=sg.ap, in1=pp.ap, op=ALU.mult),
                     reads=sg.bufs + pp.bufs, writes=t2.bufs)
                self.rel(bp_)
                hp = self.h(ob)
                S.op("dve", lambda e, hp=hp, t2=t2: e.tensor_tensor(out=hp.ap, in0=hp.ap, in1=t2.ap, op=ALU.add),
                     reads=hp.bufs + t2.bufs, writes=hp.bufs)
            yield

    def hgrn(self, first_tile):
        c = self.cfg
        S = self.S
        T = c.T
        NTB, NCH = c.NTB, c.NCH
        NF = self.NF
        self.stream_norm(4, self.xn)
        xs = [self.xn(i) for i in range(c.DC)]
        yield
        per = c.SA // c.DC
        ident_f = self.cfv(c.cfID, 128)
        one = self.sm(2)
        ctx = {}

        def X1(hd):
            st = hd % 2
            PB = 16 + 15 * st
            FO = 8 * st
            slA = self.slabA(c.iWINA + hd)
            slB = self.slabA(c.iWINB + hd)
            bq, pq = self.proj_fm(slA, 0, xs)
            bf_, pf = self.proj_fm(slA, 128, xs)
            bg, pg = self.proj_fm(slB, 128, xs)
            bv = self.bank()
            for tb in range(NTB):
                out = self.pb(bv, tb * 128, 128)
                pairs = [(xs[i].ap[:, tb * 128:(tb + 1) * 128], slB.ap[:, i * per: i * per + 128]) for i in range(c.DC)]
                rd = list(slB.bufs)
                for x_ in xs:
                    rd += x_.bufs
                self.mm(out, pairs, rd)
            pv = self.pb(bv, 0, NTB * 128)
            f0, f1, f2, f3 = self.f(FO), self.f(FO + 1), self.f(FO + 2), self.f(FO + 3)
            qt = self.ar(PB)
            gs = self.ar(PB + 9)
            vb = self.ar(PB + 3)
            v4 = self.ar(PB + 4, 4)
            S.op("act", lambda e: e.activation(out=qt.ap, in_=pq.ap, func=AF.Copy, scale=128.0 ** -0.5), reads=pq.bufs, writes=qt.bufs)
            self.rel(bq)
            S.op("act", lambda e: e.activation(out=f1.ap, in_=pf.ap, func=AF.Exp, scale=-1.0), reads=pf.bufs, writes=f1.bufs)
            self.rel(bf_)
            S.op("act", lambda e: e.activation(out=gs.ap, in_=pg.ap, func=AF.Copy), reads=pg.bufs, writes=gs.bufs)
            S.op("act", lambda e: e.activation(out=f0.ap, in_=pg.ap, func=AF.Exp, scale=-1.0), reads=pg.bufs, writes=f0.bufs)
            self.rel(bg)
            S.op("act", lambda e: e.activation(out=vb.ap, in_=pv.ap, func=AF.Copy), reads=pv.bufs, writes=vb.bufs)
            self.rel(bv)
            ctx[hd] = dict(PB=PB, FO=FO, qt=qt, vb=vb, v4=v4, gs=gs)
            yield
            S.op("act", lambda e: e.activation(out=f2.ap, in_=f1.ap, func=AF.Ln, bias=one, scale=self.sm(8 + hd)),
                 reads=f1.bufs + [self.b_sm], writes=f2.bufs)
            S.op("act", lambda e: e.activation(out=f3.ap, in_=f1.ap, func=AF.Ln, bias=one), reads=f1.bufs + [self.b_sm], writes=f3.bufs)
            yield
            S.op("act", lambda e: e.activation(out=f0.ap, in_=f0.ap, func=AF.Ln, bias=one), reads=f0.bufs + [self.b_sm], writes=f0.bufs)
            S.op("dve", lambda e: e.tensor_tensor(out=f2.ap, in0=f2.ap, in1=f3.ap, op=ALU.subtract), reads=f2.bufs + f3.bufs, writes=f2.bufs)
            yield
            S.op("act", lambda e: e.activation(out=f0.ap, in_=f0.ap, func=AF.Exp, scale=-1.0), reads=f0.bufs, writes=f0.bufs)
            m4 = bass.AP(self.t_cb, c.cbM4, [[c.NCB, 128], [1, 4], [0, 128]])
            for tb in range(NTB):
                vin = bass.AP(self.t_ar, (PB + 3) * T + tb * 128, [[c.NAR * T, 128], [0, 4], [1, 128]])
                vout = bass.AP(self.t_ar, (PB + 4) * T + tb * 512, [[c.NAR * T, 128], [128, 4], [1, 128]])
                S.op("dve", lambda e, vin=vin, vout=vout: e.tensor_tensor(out=vout, in0=vin, in1=m4, op=ALU.mult),
                     reads=vb.bufs + [self.b_cb], writes=v4.bufs)
            yield
            S.op("dve", lambda e: e.tensor_tensor(out=gs.ap, in0=gs.ap, in1=f0.ap, op=ALU.mult), reads=gs.bufs + f0.bufs, writes=gs.bufs)
            yield
            S.op("act", lambda e: e.activation(out=f3.ap, in_=f2.ap, func=AF.Exp), reads=f2.bufs, writes=f3.bufs)
            S.op("dve", lambda e: e.tensor_tensor_scan(out=f0.ap, data0=self.cfv(c.cfSCAN, T), data1=f2.ap,
                                                       initial=0.0, op0=ALU.mult, op1=ALU.add),
                 reads=f2.bufs + [self.b_cf], writes=f0.bufs)
            yield

        def X3(hd):
            x = ctx[hd]
            PB, FO = x["PB"], x["FO"]
            qt = x["qt"]
            f0, f1, f2, f3 = self.f(FO), self.f(FO + 1), self.f(FO + 2), self.f(FO + 3)
            S.op("act", lambda e: e.activation(out=f1.ap, in_=f0.ap, func=AF.Exp), reads=f0.bufs, writes=f1.bufs)
            S.op("act", lambda e: e.activation(out=f2.ap, in_=f0.ap, func=AF.Exp, scale=-1.0), reads=f0.bufs, writes=f2.bufs)
            yield
            S.op("act", lambda e: e.activation(out=f3.ap, in_=f3.ap, func=AF.Identity, scale=-1.0, bias=one),
                 reads=f3.bufs + [self.b_sm], writes=f3.bufs)
            S.op("dve", lambda e: e.tensor_tensor(out=qt.ap, in0=qt.ap, in1=f1.ap, op=ALU.mult), reads=qt.bufs + f1.bufs, writes=qt.bufs)
            yield
            S.op("dve", lambda e: e.tensor_tensor(out=f3.ap, in0=f3.ap, in1=f2.ap, op=ALU.mult), reads=f3.bufs + f2.bufs, writes=f3.bufs)
            yield
            kt = self.ar(PB + 1)
            S.op("act", lambda e: e.activation(out=kt.ap, in_=f3.ap, func=AF.Copy), reads=f3.bufs, writes=kt.bufs)
            ebl = bass.AP(self.t_f, (FO + 1) * T + 31, [[NF * T, 128], [32, NCH], [0, 32]])
            kh3 = bass.AP(self.t_f, (FO + 2) * T, [[NF * T, 128], [32, NCH], [1, 32]])
            kk3 = bass.AP(self.t_f, (FO + 3) * T, [[NF * T, 128], [32, NCH], [1, 32]])
            S.op("dve", lambda e: e.tensor_tensor(out=kh3, in0=kk3, in1=ebl, op=ALU.mult),
                 reads=f3.bufs + f1.bufs + f2.bufs, writes=f2.bufs)
            x["kt"], x["eb"], x["kh"] = kt, f1, f2
            yield

        def X4(hd):
            x = ctx[hd]
            PB = x["PB"]
            kh = x["kh"]
            bt = self.bank()
            for tb in range(NTB):
                o = self.pb(bt, tb * 128, 128)
                S.op("pe", lambda e, o=o, tb=tb: e.transpose(o.ap, kh.ap[:, tb * 128:(tb + 1) * 128], ident_f),
                     reads=kh.bufs + [self.b_cf], writes=o.bufs)
            ktok = self.ar(PB + 2)
            pt_ = self.pb(bt, 0, NTB * 128)
            S.op("act", lambda e: e.activation(out=ktok.ap, in_=pt_.ap, func=AF.Copy), reads=pt_.bufs, writes=ktok.bufs)
            self.rel(bt)
            x["ktok"] = ktok
            yield

        def X2(hd):
            x = ctx[hd]
            PB, FO = x["PB"], x["FO"]
            eb, kt, qt, v4, ktok = x["eb"], x["kt"], x["qt"], x["v4"], x["ktok"]
            bus = []
            for tb in range(NTB):
                bu = self.bank()
                bus.append(bu)
                o = self.pb(bu, 0, 512)
                self.mm(o, [(ktok.ap[:, tb * 128:(tb + 1) * 128], v4.ap[:, tb * 512:(tb + 1) * 512])],
                        ktok.bufs + v4.bufs)
            ba = self.bank()
            for tb in range(NTB):
                o = self.pb(ba, tb * 128, 128)
                self.mm(o, [(kt.ap[:, tb * 128:(tb + 1) * 128], qt.ap[:, tb * 128:(tb + 1) * 128])], kt.bufs + qt.bufs)
            ptm = self.ar(PB + 8)
            pa3 = bass.AP(self.t_ps, ba * 512, [[8 * 512, 128], [128, NTB], [1, 128]])
            pt3 = bass.AP(self.t_ar, (PB + 8) * T, [[c.NAR * T, 128], [128, NTB], [1, 128]])
            bd3 = bass.AP(self.t_cf, c.cfBD, [[c.NCF, 128], [0, NTB], [1, 128]])
            S.op("dve", lambda e: e.tensor_tensor(out=pt3, in0=pa3, in1=bd3, op=ALU.mult),
                 reads=[self.b_ps[ba], self.b_cf], writes=ptm.bufs)
            self.rel(ba)
            sh0 = Pl(self.t_sh[:, 0:128], [self.b_sh[0]])
            car = Pl(self.t_carry[:, hd * 128:(hd + 1) * 128], [self.b_carry[hd]])
            if first_tile:
                S.op("dve", lambda e: e.memset(sh0.ap, 0.0), writes=sh0.bufs)
            else:
                S.op("act", lambda e: e.activation(out=sh0.ap, in_=car.ap, func=AF.Copy), reads=car.bufs, writes=sh0.bufs)
            yield
            for ch in range(NCH):
                sp = Pl(self.t_sh[:, ch * 128:(ch + 1) * 128], [self.b_sh[ch]])
                sn = Pl(self.t_sh[:, (ch + 1) * 128:(ch + 2) * 128], [self.b_sh[ch + 1]])
                u = self.pb(bus[ch // 4], (ch % 4) * 128, 128)
                dec = self.t_f[:, (FO + 1) * T + ch * 32 + 31: (FO + 1) * T + ch * 32 + 32]
                S.op("dve", lambda e, sp=sp, sn=sn, u=u, dec=dec: e.scalar_tensor_tensor(
                    out=sn.ap, in0=sp.ap, scalar=dec, in1=u.ap, op0=ALU.mult, op1=ALU.add),
                    reads=sp.bufs + u.bufs + eb.bufs, writes=sn.bufs)
                yield
            for bu in bus:
                self.rel(bu)
            sb = self.ar(PB + 10, 4)
            shall = Pl(self.t_sh[:, 0:NCH * 128], self.b_sh[0:NCH])
            S.op("act", lambda e: e.activation(out=sb.ap, in_=shall.ap, func=AF.Copy), reads=shall.bufs, writes=sb.bufs)
            slast = Pl(self.t_sh[:, NCH * 128:(NCH + 1) * 128], [self.b_sh[NCH]])
            S.op("act", lambda e: e.activation(out=car.ap, in_=slast.ap, func=AF.Copy), reads=slast.bufs, writes=car.bufs)
            x["ptm"], x["sb"] = ptm, sb
            yield

        def P3(hd):
            x = ctx.pop(hd)
            vb, ptm, sb, qt, gs = x["vb"], x["ptm"], x["sb"], x["qt"], x["gs"]
            bo = self.bank()
            for tb in range(NTB):
                o = self.pb(bo, tb * 128, 128)

                def fn(e, tb=tb, o=o):
                    e.matmul(o.ap, vb.ap[:, tb * 128:(tb + 1) * 128], ptm.ap[:, tb * 128:(tb + 1) * 128],
                             start=True, stop=False)
                    ins = None
                    for j in range(4):
                        ch = tb * 4 + j
                        ins = e.matmul(self.t_ps[:, bo, ch * 32:(ch + 1) * 32], sb.ap[:, ch * 128:(ch + 1) * 128],
                                       qt.ap[:, ch * 32:(ch + 1) * 32], start=False, stop=(j == 3))
                    return ins
                S.op("pe", fn, reads=vb.bufs + ptm.bufs + sb.bufs + qt.bufs, writes=o.bufs)
            po = self.pb(bo)
            ov = self.f(4)
            S.op("act", lambda e: e.activation(out=ov.ap, in_=po.ap, func=AF.Copy), reads=po.bufs, writes=ov.bufs)
            self.rel(bo)
            yield
            rs = self.rstd([ov], 128, 5, 5)
            yield
            S.op("dve", lambda e: e.scalar_tensor_tensor(out=ov.ap, in0=ov.ap, scalar=self.prm(c.pON), in1=rs.ap,
                                                         op0=ALU.mult, op1=ALU.mult),
                 reads=ov.bufs + rs.bufs + [self.b_prm], writes=ov.bufs)
            yield
            on = self.ar(hd)
            S.op("dve", lambda e: e.tensor_tensor(out=on.ap, in0=ov.ap, in1=gs.ap, op=ALU.mult),
                 reads=ov.bufs + gs.bufs, writes=on.bufs)
            yield

        yield from X1(0)
        yield from X3(0)
        yield from X4(0)
        for hd in range(c.HGH):
            nxt = hd + 1 < c.HGH
            if nxt:
                yield from X1(hd + 1)
            yield from X2(hd)
            if nxt:
                yield from X3(hd + 1)
                yield from X4(hd + 1)
            yield from P3(hd)
        yield from self.add_proj(c.iWOUT, [self.ar(i) for i in range(c.HGH)])

    def rope_to(self, pk, dst, tmp_plane):
        c = self.cfg
        S = self.S
        T = c.T
        xb = self.ar(tmp_plane)
        S.op("act", lambda e: e.activation(out=xb.ap, in_=pk.ap, func=AF.Copy), reads=pk.bufs, writes=xb.bufs)
        bs = self.bank()
        ps_ = self.pb(bs)
        self.mm(ps_, [(self.cbv(c.cbSW, 128), xb.ap)], xb.bufs + [self.b_cb])
        t1 = self.f(4)
        S.op("dve", lambda e: e.tensor_tensor(out=t1.ap, in0=pk.ap, in1=self.t_rope[:, 0:T], op=ALU.mult),
             reads=pk.bufs + [self.b_rope], writes=t1.bufs)
        t2 = self.f(5)
        S.op("dve", lambda e: e.tensor_tensor(out=t2.ap, in0=ps_.ap, in1=self.t_rope[:, T:2 * T], op=ALU.mult),
             reads=ps_.bufs + [self.b_rope], writes=t2.bufs)
        self.rel(bs)
        S.op("dve", lambda e: e.tensor_tensor(out=dst.ap, in0=t1.ap, in1=t2.ap, op=ALU.add),
             reads=t1.bufs + t2.bufs, writes=dst.bufs)

    def load_rope(self, tile):
        c = self.cfg
        T = c.T
        self.S.dma("sp", [lambda e: e.dma_start(out=self.t_rope[:, 0:T], in_=self.d_rope[0, :, tile * T:(tile + 1) * T]),
                          lambda e: e.dma_start(out=self.t_rope[:, T:2 * T], in_=self.d_rope[1, :, tile * T:(tile + 1) * T])],
                   self.b_rope, writes=[self.b_rope])

    def kv(self, seq, tile):
        c = self.cfg
        S = self.S
        T = c.T
        NTB = c.NTB
        per = c.SA // c.DC
        self.stream_norm(6, self.xn)
        xs = [self.xn(i) for i in range(c.DC)]
        yield
        for hd in range(c.DAH):
            base = (hd % 2) * 4
            slK = self.slabA(c.iWK + hd)
            for comp in range(2):
                bk, pk = self.proj_fm(slK, comp * 128, xs)
                dst = self.ar(base + comp)
                self.rope_to(pk, dst, 8 + comp)
                self.rel(bk)
                yield
            slV = self.slabA(c.iWV + hd)
            vt = self.ar(base + 2, 2)
            for tb in range(NTB):
                bv = self.bank()
                o = self.pb(bv, 0, 256)
                pairs = [(xs[i].ap[:, tb * 128:(tb + 1) * 128], slV.ap[:, i * per: i * per + 256]) for i in range(c.DC)]
                rd = list(slV.bufs)
                for x_ in xs:
                    rd += x_.bufs
                self.mm(o, pairs, rd)
                S.op("act", lambda e, o=o, tb=tb: e.activation(out=vt.ap[:, tb * 256:(tb + 1) * 256], in_=o.ap, func=AF.Copy),
                     reads=o.bufs, writes=vt.bufs)
                self.rel(bv)
            kd, vd = self.b_kd[seq][hd], self.b_vd[seq][hd]
            ksrc = self.ar(base, 2)
            fns = []
            for comp in range(2):
                fns.append(lambda e, comp=comp: e.dma_start(
                    out=self.d_k[seq, hd, :, comp * c.S + tile * T: comp * c.S + (tile + 1) * T],
                    in_=self.t_ar[:, (base + comp) * T:(base + comp + 1) * T]))
            S.dma("sp", fns, kd, reads=ksrc.bufs, writes=[kd])
            vdst = self.d_v[seq, hd, tile * T:(tile + 1) * T, :].rearrange("(tb p) v -> p tb v", p=128)
            vsrc = self.t_ar[:, (base + 2) * T:(base + 4) * T].rearrange("p (tb v) -> p tb v", v=256)
            S.dma("sp", [lambda e: e.dma_start(out=vdst, in_=vsrc)], vd, reads=vt.bufs, writes=[vd])
            yield

    def dattn(self, seq, tile):
        c = self.cfg
        S = self.S
        T = c.T
        NTB = c.NTB
        self.stream_norm(5, self.xn)
        xs = [self.xn(i) for i in range(c.DC)]
        yield
        for hd in range(c.DAH):
            slQ = self.slabA(c.iWQ + hd)
            for comp in range(2):
                bq, pq = self.proj_fm(slQ, comp * 128, xs)
                self.rope_to(pq, self.ar(hd * 2 + comp), 16 + comp)
                self.rel(bq)
                yield
        ntok = (tile + 1) * T
        NKB = ntok // 128
        npl = 2 * c.S // T
        K0 = 18
        V0 = 18 + npl
        P0 = V0 + npl
        ident = self.cbv(c.cbID, 128)
        ones = self.cbv(c.cbONE, 128)
        scale = 128.0 ** -0.5
        pending = None
        for hd in range(c.DAH):
            kd, vd = self.b_kd[seq][hd], self.b_vd[seq][hd]
            ks = self.ar(K0, npl)
            vs = self.ar(V0, npl)
            kdst = self.t_ar[:, K0 * T: K0 * T + 2 * c.S].rearrange("p (c t) -> p c t", c=2)[:, :, 0:ntok]
            ksrc = self.d_k[seq, hd].rearrange("p (c t) -> p c t", c=2)[:, :, 0:ntok]
            S.dma("sp", [lambda e, kdst=kdst, ksrc=ksrc: e.dma_start(out=kdst, in_=ksrc)], ks.bufs[0], reads=[kd], writes=ks.bufs)
            vdst = self.t_ar[:, V0 * T: V0 * T + NKB * 256].rearrange("p (kb v) -> p kb v", v=256)
            vsrc = self.d_v[seq, hd, 0:ntok, :].rearrange("(kb p) v -> p kb v", p=128)
            S.dma("sp", [lambda e, vdst=vdst, vsrc=vsrc: e.dma_start(out=vdst, in_=vsrc)], vs.bufs[0], reads=[vd], writes=vs.bufs)
            FO = 8 * (hd % 2)
            od = [self.f(FO), self.f(FO + 1)]
            for comp in range(2):
                qt = self.ar(hd * 2 + comp)
                bo0, bo1, bl = self.bank(), self.bank(), self.bank()
                po0, po1, pl_ = self.pb(bo0), self.pb(bo1), self.pb(bl)
                def qk2(kb0, it):
                    nk = min(2, NKB - kb0)
                    bs = self.bank()
                    ps2 = self.pb(bs, 0, nk * T)
                    rd = ks.bufs + qt.bufs + [self.b_cb]

                    def fn(e):
                        ins = None
                        for u in range(nk):
                            kb = kb0 + u
                            o = self.t_ps[:, bs, u * T:(u + 1) * T]
                            jd = kb - tile * NTB
                            ins = e.matmul(o, self.t_ar[:, K0 * T + comp * c.S + kb * 128: K0 * T + comp * c.S + (kb + 1) * 128],
                                           qt.ap, start=True, stop=(jd < 0))
                            if jd >= 0:
                                ins = e.matmul(o, ident, self.cbv(c.cbMASK + jd * T, T), start=False, stop=True)
                        return ins
                    S.op("pe", fn, reads=rd, writes=ps2.bufs)
                    pt = self.ar(P0 + 2 * (it % 2), 2)
                    pt2 = Pl(pt.ap[:, 0:nk * T], pt.bufs)
                    S.op("act", lambda e: e.activation(out=pt2.ap, in_=ps2.ap, func=AF.Exp, scale=scale),
                         reads=ps2.bufs, writes=pt2.bufs)
                    self.rel(bs)
                    return (kb0, nk, pt2)

                def pv2(item):
                    kb0, nk, pt2 = item

                    def fn(e):
                        ins = None
                        for u in range(nk):
                            kb = kb0 + u
                            vblk = self.t_ar[:, V0 * T + kb * 256: V0 * T + (kb + 1) * 256]
                            p = pt2.ap[:, u * T:(u + 1) * T]
                            st, sp_ = (kb == 0), (kb == NKB - 1)
                            e.matmul(po0.ap, vblk[:, 0:128], p, start=st, stop=sp_)
                            e.matmul(po1.ap, vblk[:, 128:256], p, start=st, stop=sp_)
                            ins = e.matmul(pl_.ap, ones, p, start=st, stop=sp_)
                        return ins
                    S.op("pe", fn, reads=vs.bufs + pt2.bufs + [self.b_cb], writes=po0.bufs + po1.bufs + pl_.bufs)

                prev = None
                nit = (NKB + 1) // 2
                for it in range(nit + 1):
                    cur_it = qk2(2 * it, it) if it < nit else None
                    if prev is not None:
                        pv2(prev)
                    prev = cur_it
                    if comp == 0 and it == min(1, nit) and pending is not None:
                        pending()
                        pending = None
                    yield
                rl = self.f(FO + 2)
                S.op("act", lambda e: e.activation(out=rl.ap, in_=pl_.ap, func=AF.Ln), reads=pl_.bufs, writes=rl.bufs)
                self.rel(bl)
                S.op("act", lambda e: e.activation(out=rl.ap, in_=rl.ap, func=AF.Exp, scale=-1.0), reads=rl.bufs, writes=rl.bufs)
                for v, (bo, po) in enumerate(((bo0, po0), (bo1, po1))):
                    if comp == 0:
                        S.op("dve", lambda e, po=po, v=v: e.tensor_tensor(out=od[v].ap, in0=po.ap, in1=rl.ap, op=ALU.mult),
                             reads=po.bufs + rl.bufs, writes=od[v].bufs)
                    else:
                        t = self.f(FO + 3)
                        S.op("dve", lambda e, po=po, t=t: e.tensor_tensor(out=t.ap, in0=po.ap, in1=rl.ap, op=ALU.mult),
                             reads=po.bufs + rl.bufs, writes=t.bufs)
                        S.op("dve", lambda e, t=t, v=v: e.scalar_tensor_tensor(
                            out=od[v].ap, in0=t.ap, scalar=self.sm(3), in1=od[v].ap, op0=ALU.mult, op1=ALU.add),
                            reads=t.bufs + od[v].bufs + [self.b_sm], writes=od[v].bufs)
                    self.rel(bo)
            def epi(hd=hd, od=od):
                rs = self.rstd(od, 256, 4, 5)
                for v in range(2):
                    on = self.xn(hd * 2 + v)
                    S.op("dve", lambda e, on=on, v=v: e.scalar_tensor_tensor(
                        out=on.ap, in0=od[v].ap, scalar=self.sm(4 + v), in1=rs.ap, op0=ALU.mult, op1=ALU.mult),
                        reads=od[v].bufs + rs.bufs + [self.b_sm], writes=on.bufs)
            pending = epi
            yield
        if pending is not None:
            pending()
            yield
        yield from self.add_proj(c.iDWOUT, [self.xn(i) for i in range(c.DC)])

    def setup(self):
        c = self.cfg
        S = self.S
        S.dma("sp", [lambda e: e.dma_start(out=self.t_prm[:, :], in_=self.d_prm)], self.b_prm, writes=[self.b_prm])
        S.dma("sp", [lambda e: e.dma_start(out=self.t_cf[:, :], in_=self.d_cf)], self.b_cf, writes=[self.b_cf])
        S.dma("pool", [lambda e: e.dma_start(out=self.t_cb[:, :], in_=self.d_cb)], self.b_cb, writes=[self.b_cb])
        sm = self.sm
        H = c.HGH
        S.op("dve", lambda e: e.memset(self.t_sm[:, :], 0.0), writes=[self.b_sm])
        S.op("dve", lambda e: e.memset(sm(0), EPS), reads=[self.b_sm], writes=[self.b_sm])
        S.op("dve", lambda e: e.memset(sm(2), 1.0), reads=[self.b_sm], writes=[self.b_sm])
        S.op("dve", lambda e: e.tensor_tensor(out=sm(8, H), in0=self.prm(c.pLB0, H), in1=self.prm(c.pLB1, H), op=ALU.subtract),
             reads=[self.b_prm, self.b_sm], writes=[self.b_sm])
        S.op("act", lambda e: e.activation(out=sm(8, H), in_=sm(8, H), func=AF.Sigmoid), reads=[self.b_sm], writes=[self.b_sm])
        S.op("dve", lambda e: e.tensor_scalar(out=sm(8 + H, H), in0=sm(8, H), scalar1=-1.0, scalar2=1.0, op0=ALU.mult, op1=ALU.add),
             reads=[self.b_sm], writes=[self.b_sm])
        S.op("dve", lambda e: e.tensor_scalar(out=sm(8 + 2 * H, H), in0=sm(8 + H, H), scalar1=-1.0, scalar2=None, op0=ALU.mult),
             reads=[self.b_sm], writes=[self.b_sm])
        L = c.pLAM
        S.op("dve", lambda e: e.tensor_tensor(out=self.t_lt[:, 0:128], in0=self.prm(L, 128), in1=self.prm(L + 128, 128), op=ALU.mult),
             reads=[self.b_prm], writes=[self.b_lt])
        S.op("dve", lambda e: e.tensor_tensor(out=self.t_lt[:, 128:256], in0=self.prm(L + 256, 128), in1=self.prm(L + 384, 128), op=ALU.mult),
             reads=[self.b_prm, self.b_lt], writes=[self.b_lt])
        S.op("dve", lambda e: e.reduce_sum(out=sm(6, 2), in_=self.t_lt[:, :].rearrange("p (a b) -> p a b", a=2), axis=AX.X),
             reads=[self.b_lt, self.b_sm], writes=[self.b_sm])
        S.op("act", lambda e: e.activation(out=sm(6, 2), in_=sm(6, 2), func=AF.Exp), reads=[self.b_sm], writes=[self.b_sm])
        S.op("dve", lambda e: e.tensor_tensor(out=sm(1), in0=sm(6), in1=sm(7), op=ALU.subtract), reads=[self.b_sm], writes=[self.b_sm])
        S.op("dve", lambda e: e.tensor_scalar(out=sm(3), in0=sm(1), scalar1=c.lam_init, scalar2=-1.0, op0=ALU.add, op1=ALU.mult),
             reads=[self.b_sm], writes=[self.b_sm])
        S.op("dve", lambda e: e.tensor_scalar(out=sm(4, 2), in0=self.prm(c.pSUB, 2), scalar1=1.0 - c.lam_init, scalar2=None, op0=ALU.mult),
             reads=[self.b_sm, self.b_prm], writes=[self.b_sm])

    def final(self, seq, tile):
        c = self.cfg
        T = c.T
        outp = self.arf(0, c.DC)
        self.stream_norm(9, lambda i: Pl(self.t_ar[:, 2 * i * T:(2 * i + 2) * T].bitcast(F32), self.b_ar[2 * i:2 * i + 2]))
        src = self.t_ar[:, 0:2 * c.DC * T].bitcast(F32)
        self.S.dma("sp", [lambda e: e.dma_start(out=self.d_out[seq, tile], in_=src)], self.cur.b_out, reads=outp.bufs, writes=[self.cur.b_out])

    def seq_gen(self, st):
        c = self.cfg
        S = self.S
        seq = st.seq
        for tile in range(c.NT):
            st.tile = tile
            hall = Pl(self.t_h[:, :], self.b_h)
            S.dma("sp", [lambda e: e.dma_start(out=self.t_h[:, :], in_=self.d_x[seq, tile])],
                  self.b_h[0], writes=hall.bufs)
            self.load_rope(tile)
            stg = self.stages
            if "ffn00" in stg: yield from self.ffn(0, 0)
            if "hgrn" in stg: yield from self.hgrn(tile == 0)
            if "ffn01" in stg: yield from self.ffn(0, 1)
            if "ple0" in stg: yield from self.ple(0, seq, tile)
            if "kv" in stg: yield from self.kv(seq, tile)
            if "ffn10" in stg: yield from self.ffn(1, 0)
            if "dattn" in stg: yield from self.dattn(seq, tile)
            if "ffn11" in stg: yield from self.ffn(1, 1)
            if "ple1" in stg: yield from self.ple(1, seq, tile)
            self.final(seq, tile)
            yield

    def build(self):
        c = self.cfg
        self.setup()
        gens = [self.seq_gen(st) for st in self.streams]
        alive = [True] * len(gens)
        steps = [0] * len(gens)
        while any(alive):
            for i, g in enumerate(gens):
                if not alive[i]:
                    continue
                if i > 0 and alive[i - 1] and steps[i - 1] - steps[i] < c.LAG:
                    continue
                self.cur = self.streams[i]
                try:
                    next(g)
                    steps[i] += 1
                except StopIteration:
                    alive[i] = False
        self.S.final_wait("sp", [st.b_out for st in self.streams])
        return self.nc


_CACHE = {}


def kernel(**inputs):
    cfg = Cfg()
    n = 8
    x = np.asarray(inputs["x"], np.float32)
    p = np.asarray(inputs["p"], np.float32)
    inp = {k: np.asarray(v) for k, v in inputs.items()}
    wA, wB = host_weights(inp, cfg)
    prm = host_params(inp, cfg)
    cb, cf, rope = host_consts(cfg)
    nc = Prog(cfg).build()
    in_maps = []
    for i in range(n):
        xs = x[i * cfg.NSEQ:(i + 1) * cfg.NSEQ]
        ps = p[:, i * cfg.NSEQ:(i + 1) * cfg.NSEQ]
        xT, pT = host_acts(xs, ps, cfg)
        in_maps.append({"xT": xT.reshape(cfg.NSEQ, cfg.NT, 128, -1), "pT": pT.reshape(2, cfg.NSEQ, cfg.NT, 128, -1),
                        "wA": wA, "wB": wB, "prm": prm, "cb": cb, "cf": cf, "rope": rope})
    res = run_bass_kernel_spmd(nc, in_maps, core_ids=list(range(n)))
    outs = []
    for i in range(n):
        oT = np.asarray(res.results[i]["outT"]).reshape(cfg.NSEQ, cfg.NT, 128, cfg.DC, cfg.T)
        outs.append(host_out(oT, cfg))
    return np.concatenate(outs, axis=0).astype(np.float32)
```

```python
import math
import numpy as np
import concourse.bass as bass
import concourse.mybir as mybir
from concourse.bass_utils import run_bass_kernel_spmd

F32 = mybir.dt.float32
BF16 = mybir.dt.bfloat16
AF = mybir.ActivationFunctionType
ALU = mybir.AluOpType
AX = mybir.AxisListType

EPS = 1e-6
ROPE_THETA = 10000.0
NEG = -30000.0


ALL_STAGES = ("ffn00", "hgrn", "ffn01", "ple0", "kv", "ffn10", "dattn", "ffn11", "ple1")


class Cfg:
    def __init__(self, D=2048, DFF=5632, S=2048, T=256, NSEQ=2, PLE=256, NSLOT=5, LAG=2):
        self.D, self.DFF, self.S, self.T, self.NSEQ, self.PLE, self.NSLOT = D, DFF, S, T, NSEQ, PLE, NSLOT
        self.LAG = LAG
        self.DC = D // 128
        self.FB = DFF // 128
        self.HGH = D // 128
        self.DAH = D // 256
        self.NT = S // T
        self.NTB = T // 128
        self.NCH = T // 32
        self.PC = PLE // 128
        assert self.PC * D == self.DC * 256
        self.SA = self.DC * 256
        self.FH = self.FB // 2
        assert self.FH * 2 == self.FB
        self.SB = self.FH * 128
        self.SLOTE = max(self.SA, self.SB)
        self.NJ = self.DC // 2
        o = 0
        self.iGU = o; o += 4 * self.FB
        self.iWINA = o; o += self.HGH
        self.iWINB = o; o += self.HGH
        self.iWOUT = o; o += self.NJ
        self.iWK = o; o += self.DAH
        self.iWV = o; o += self.DAH
        self.iWQ = o; o += self.DAH
        self.iDWOUT = o; o += self.NJ
        self.iPLEG = o; o += 2 * self.NJ
        self.iPLEP = o; o += 2
        self.NA = o
        self.NB = 4 * self.DC * 2
        c = 0
        self.pG = c; c += 10 * self.DC
        self.pLB0 = c; c += self.HGH
        self.pLB1 = c; c += self.HGH
        self.pON = c; c += 1
        self.pSUB = c; c += 2
        self.pLAM = c; c += 512
        self.NP = c
        self.cbID = 0
        self.cbONE = 128
        self.cbSW = 256
        self.cbMASK = 384
        self.cbM4 = 384 + self.NTB * T
        self.NCB = 384 + self.NTB * T + 4
        self.cfID = 0
        self.cfBD = 128
        self.cfM4 = 256
        self.cfSCAN = 260
        self.NCF = 260 + T
        self.NAR = max(self.FB, 22 + 2 * (2 * S // T), 2 * self.DC, 46)
        self.lam_init = 0.8 - 0.6 * float(np.exp(-0.3 * 1))


def _slabA(W, cols, DC):
    sub = W[:, cols]
    return sub.reshape(DC, 128, -1).transpose(1, 0, 2).reshape(128, -1)


def host_weights(inp, cfg):
    D, DC, DFF, FB = cfg.D, cfg.DC, cfg.DFF, cfg.FB
    wA = np.empty((cfg.NA, 128, cfg.SA), np.float32)
    wB = np.empty((cfg.NB, 128, cfg.SB), np.float32)
    ar = np.arange
    for l in range(2):
        for f in range(2):
            Wgu = inp["ffn_w_gate_up"][l, f]
            for fb in range(FB):
                cols = np.concatenate([ar(fb * 128, fb * 128 + 128), DFF + ar(fb * 128, fb * 128 + 128)])
                wA[cfg.iGU + (l * 2 + f) * FB + fb] = _slabA(Wgu, cols, DC)
            Wd = inp["ffn_w_down"][l, f]
            for ob in range(DC):
                sub = Wd[:, ob * 128:(ob + 1) * 128].reshape(2, cfg.FH, 128, 128)
                for hf in range(2):
                    wB[((l * 2 + f) * DC + ob) * 2 + hf] = sub[hf].transpose(1, 0, 2).reshape(128, -1)
    Win = inp["hgrn_w_in"][0]
    for hd in range(cfg.HGH):
        r = ar(hd * 128, hd * 128 + 128)
        wA[cfg.iWINA + hd] = _slabA(Win, np.concatenate([r, D + r]), DC)
        wA[cfg.iWINB + hd] = _slabA(Win, np.concatenate([2 * D + r, 3 * D + r]), DC)
    for j in range(cfg.NJ):
        r = ar(j * 256, j * 256 + 256)
        wA[cfg.iWOUT + j] = _slabA(inp["hgrn_w_out"][0], r, DC)
        wA[cfg.iDWOUT + j] = _slabA(inp["diff_w_out"][0], r, DC)
        for l in range(2):
            wA[cfg.iPLEG + l * cfg.NJ + j] = _slabA(inp["ple_w_gate"][l], r, DC)
    for hd in range(cfg.DAH):
        r = ar(hd * 256, hd * 256 + 256)
        wA[cfg.iWK + hd] = _slabA(inp["w_kv"], r, DC)
        wA[cfg.iWV + hd] = _slabA(inp["w_kv"], D + r, DC)
        wA[cfg.iWQ + hd] = _slabA(inp["diff_w_q"][0], r, DC)
    for l in range(2):
        Wp = inp["ple_w_proj"][l]
        wA[cfg.iPLEP + l] = Wp.reshape(cfg.PC, 128, D).transpose(1, 0, 2).reshape(128, -1)
    return wA, wB


def host_params(inp, cfg):
    DC = cfg.DC
    prm = np.zeros((128, cfg.NP), np.float32)

    def fm(v):
        return np.asarray(v, np.float32).reshape(-1, 128).T

    gl = [inp["ffn_norm"][0, 0], inp["ffn_norm"][0, 1], inp["ffn_norm"][1, 0], inp["ffn_norm"][1, 1],
          inp["mix_norm"][0], inp["mix_norm"][1], inp["kv_norm"], inp["ple_norm"][0], inp["ple_norm"][1],
          inp["final_norm"]]
    for i, g in enumerate(gl):
        prm[:, cfg.pG + i * DC: cfg.pG + (i + 1) * DC] = fm(g)
    prm[:, cfg.pLB0:cfg.pLB0 + cfg.HGH] = fm(inp["hgrn_lower_bounds"][0])
    prm[:, cfg.pLB1:cfg.pLB1 + cfg.HGH] = fm(inp["hgrn_lower_bounds"][1])
    prm[:, cfg.pON] = np.asarray(inp["hgrn_out_norm"][0], np.float32)
    prm[:, cfg.pSUB:cfg.pSUB + 2] = fm(inp["diff_subln"][0])
    prm[:, cfg.pLAM:cfg.pLAM + 512] = np.broadcast_to(
        np.asarray(inp["diff_lambda"][0], np.float32).reshape(1, 512), (128, 512))
    return prm


def host_consts(cfg):
    T, S = cfg.T, cfg.S
    cb = np.zeros((128, cfg.NCB), np.float32)
    cb[:, cfg.cbID:cfg.cbID + 128] = np.eye(128)
    cb[:, cfg.cbONE:cfg.cbONE + 128] = 1.0
    sw = np.zeros((128, 128), np.float32)
    for d in range(128):
        sw[d, (d + 64) % 128] = 1.0
    cb[:, cfg.cbSW:cfg.cbSW + 128] = sw
    p = np.arange(128)[:, None]
    q = np.arange(T)[None, :]
    for j in range(cfg.NTB):
        cb[:, cfg.cbMASK + j * T: cfg.cbMASK + (j + 1) * T] = np.where(j * 128 + p <= q, 0.0, NEG)
    cb[:, cfg.cbM4:cfg.cbM4 + 4] = (np.arange(128)[:, None] // 32 == np.arange(4)[None, :]).astype(np.float32)
    cf = np.zeros((128, cfg.NCF), np.float32)
    cf[:, cfg.cfID:cfg.cfID + 128] = np.eye(128)
    s = np.arange(128)[:, None]
    t = np.arange(128)[None, :]
    cf[:, cfg.cfBD:cfg.cfBD + 128] = ((s <= t) & (s // 32 == t // 32)).astype(np.float32)
    cf[:, cfg.cfM4:cfg.cfM4 + 4] = (s // 32 == np.arange(4)[None, :]).astype(np.float32)
    cf[:, cfg.cfSCAN:cfg.cfSCAN + T] = (np.arange(T) % 32 != 0).astype(np.float32)[None, :]
    inv = (ROPE_THETA ** (-np.arange(0, 128, 2, dtype=np.float32) / 128)).astype(np.float32)
    ang = np.arange(S, dtype=np.float32)[None, :] * np.concatenate([inv, inv])[:, None]
    rope = np.stack([np.cos(ang), np.sin(ang) * np.where(np.arange(128) < 64, -1.0, 1.0)[:, None]]).astype(np.float32)
    return cb, cf, rope


def host_acts(x, p, cfg):
    NS, NT, T, DC, PC = cfg.NSEQ, cfg.NT, cfg.T, cfg.DC, cfg.PC
    xT = np.ascontiguousarray(x.reshape(NS, NT, T, DC, 128).transpose(0, 1, 4, 3, 2))
    pT = np.ascontiguousarray(p.reshape(2, NS, NT, T, PC, 128).transpose(0, 1, 2, 5, 4, 3))
    return xT, pT


def host_out(oT, cfg):
    return np.ascontiguousarray(oT.transpose(0, 1, 4, 3, 2)).reshape(cfg.NSEQ, cfg.S, cfg.D)


class Buf:
    __slots__ = ("name", "w", "r", "dsem", "dcnt", "excl")

    def __init__(self, name, excl=False):
        self.name = name
        self.excl = excl
        self.w = {}
        self.r = {}
        self.dsem = None
        self.dcnt = 0


class Sched:
    def __init__(self, nc):
        self.nc = nc
        self.engs = {"pe": nc.tensor, "act": nc.scalar, "dve": nc.vector, "pool": nc.gpsimd, "sp": nc.sync}
        self.csem = {k: nc.alloc_semaphore("cs_" + k) for k in self.engs}
        self.cnt = {k: 0 for k in self.engs}
        self.seen = {k: {} for k in self.engs}
        self.nsem = 0

    def new_dsem(self, buf):
        buf.dsem = self.nc.alloc_semaphore("ds_%d_%s" % (self.nsem, buf.name))
        self.nsem += 1

    def _wait(self, e, reads, writes):
        need = {}
        own = self.csem[e]
        for b in reads:
            for s, v in b.w.items():
                if need.get(s, 0) < v:
                    need[s] = v
            if b.excl:
                for s, v in b.r.items():
                    if s is not own and s != own and need.get(s, 0) < v:
                        need[s] = v
        for b in writes:
            for s, v in b.w.items():
                if need.get(s, 0) < v:
                    need[s] = v
            for s, v in b.r.items():
                if need.get(s, 0) < v:
                    need[s] = v
        seen = self.seen[e]
        for s, v in need.items():
            if seen.get(s, 0) >= v:
                continue
            self.engs[e].wait_ge(s, v)
            seen[s] = v

    def _mark(self, s, v, reads, writes):
        for b in reads:
            if b.r.get(s, 0) < v:
                b.r[s] = v
        for b in writes:
            if b.w.get(s, 0) < v:
                b.w[s] = v

    def op(self, e, fn, reads=(), writes=()):
        self._wait(e, reads, writes)
        ins = fn(self.engs[e])
        self.cnt[e] += 1
        s = self.csem[e]
        ins.then_inc(s, 1)
        self._mark(s, self.cnt[e], reads, writes)

    def dma(self, q, fns, owner, reads=(), writes=()):
        self._wait(q, reads, writes)
        if owner.dsem is None:
            self.new_dsem(owner)
        for fn in fns:
            fn(self.engs[q]).then_inc(owner.dsem, 16)
        owner.dcnt += 16 * len(fns)
        self._mark(owner.dsem, owner.dcnt, reads, writes)

    def final_wait(self, e, bufs):
        self._wait(e, bufs, bufs)


class Pl:
    __slots__ = ("ap", "bufs")

    def __init__(self, ap, bufs):
        self.ap = ap
        self.bufs = bufs


class Stream:
    def __init__(self, nc, c, i, NF):
        A = nc.alloc_sbuf_tensor
        T = c.T
        n = "_s%d" % i
        self.seq = i
        self.t_h = A("h" + n, [128, c.DC * T], F32)
        self.t_xn = A("xn" + n, [128, c.DC * T], BF16)
        self.t_ar = A("arena" + n, [128, c.NAR * T], BF16)
        self.t_sh = A("sh" + n, [128, (c.NCH + 1) * 128], F32)
        self.t_carry = A("carry" + n, [128, c.HGH * 128], F32)
        self.t_f = A("ftmp" + n, [128, NF * T], F32)
        self.t_rope = A("ropes" + n, [128, 2 * T], F32)
        self.t_pb = A("pb" + n, [128, c.PC * T], BF16)
        self.t_sq = A("sqtmp" + n, [128, 2 * T], BF16)
        B = Buf
        self.b_h = [B("h%d" % j + n) for j in range(c.DC)]
        self.b_xn = [B("xn%d" % j + n) for j in range(c.DC)]
        self.b_ar = [B("ar%d" % j + n) for j in range(c.NAR)]
        self.b_sh = [B("sh%d" % j + n) for j in range(c.NCH + 1)]
        self.b_carry = [B("carry%d" % j + n) for j in range(c.HGH)]
        self.b_f = [B("f%d" % j + n) for j in range(NF)]
        self.b_rope = B("rope" + n)
        self.b_pb = B("pb" + n)
        self.b_sq = [B("sq0" + n), B("sq1" + n)]
        self.sq_i = 0
        self.tile = 0
        self.b_out = B("out" + n)


class Prog:
    def __init__(self, cfg):
        self.cfg = cfg
        c = cfg
        nc = bass.Bass("TRN2", target_bir_lowering=False)
        self.nc = nc
        T = c.T
        self.d_x = nc.dram_tensor("xT", [c.NSEQ, c.NT, 128, c.DC * T], F32, kind="ExternalInput").ap()
        self.d_p = nc.dram_tensor("pT", [2, c.NSEQ, c.NT, 128, c.PC * T], F32, kind="ExternalInput").ap()
        self.d_wA = nc.dram_tensor("wA", [c.NA, 128, c.SA], F32, kind="ExternalInput").ap()
        self.d_wB = nc.dram_tensor("wB", [c.NB, 128, c.SB], F32, kind="ExternalInput").ap()
        self.d_prm = nc.dram_tensor("prm", [128, c.NP], F32, kind="ExternalInput").ap()
        self.d_cb = nc.dram_tensor("cb", [128, c.NCB], F32, kind="ExternalInput").ap()
        self.d_cf = nc.dram_tensor("cf", [128, c.NCF], F32, kind="ExternalInput").ap()
        self.d_rope = nc.dram_tensor("rope", [2, 128, c.S], F32, kind="ExternalInput").ap()
        self.d_out = nc.dram_tensor("outT", [c.NSEQ, c.NT, 128, c.DC * T], F32, kind="ExternalOutput").ap()
        self.d_k = nc.dram_tensor("kscr", [c.NSEQ, c.DAH, 128, 2 * c.S], BF16).ap()
        self.d_v = nc.dram_tensor("vscr", [c.NSEQ, c.DAH, c.S, 256], BF16).ap()
        self.NAH = (c.NA + 2) // 3
        self.d_wAb = [nc.dram_tensor("wAb%d" % i, [self.NAH, 128, c.SA], BF16).ap() for i in range(3)]
        self.d_wBb = nc.dram_tensor("wBb", [c.NB, 128, c.SB], BF16).ap()
        A = nc.alloc_sbuf_tensor
        self.NF = 12
        self.streams = [Stream(nc, c, i, self.NF) for i in range(c.NSEQ)]
        self.cur = self.streams[0]
        self.t_slot = [A("slot%d" % i, [128, c.SLOTE], BF16) for i in range(c.NSLOT)]
        self.t_cb = A("cbs", [128, c.NCB], BF16)
        self.t_cf = A("cfs", [128, c.NCF], F32)
        self.t_prm = A("prms", [128, c.NP], F32)
        self.t_sm = A("small", [128, 64], F32)
        self.t_lt = A("lamtmp", [128, 256], F32)
        self.t_ps = nc.alloc_psum_tensor("ps", [128, 8, 512], F32)
        self.WP0 = c.NAR - (c.SA + T - 1) // T
        B = Buf
        self.b_slot = [B("slot%d" % i) for i in range(c.NSLOT)]
        self.b_cb = B("cb")
        self.b_cf = B("cf")
        self.b_prm = B("prm")
        self.b_sm = B("sm")
        self.b_lt = B("lt")
        self.b_ps = [B("ps%d" % i, excl=True) for i in range(8)]
        self.b_kd = [[B("kd%d_%d" % (s, h)) for h in range(c.DAH)] for s in range(c.NSEQ)]
        self.b_vd = [[B("vd%d_%d" % (s, h)) for h in range(c.DAH)] for s in range(c.NSEQ)]
        self.slab_cache = {}
        self.slot_key = [None] * c.NSLOT
        self.slot_dirty = [None] * c.NSLOT
        self.b_sst = [B("sst%d" % i) for i in range(c.NSLOT)]
        self.b_wsc = {}
        self.stored = set()
        self.wp_key = None
        self.S = Sched(nc)
        self.free_banks = list(range(8))
        self.slot_i = 0
        self.stages = ALL_STAGES

    t_h = property(lambda self: self.cur.t_h)
    t_xn = property(lambda self: self.cur.t_xn)
    t_ar = property(lambda self: self.cur.t_ar)
    t_f = property(lambda self: self.cur.t_f)
    t_sh = property(lambda self: self.cur.t_sh)
    t_carry = property(lambda self: self.cur.t_carry)
    t_rope = property(lambda self: self.cur.t_rope)
    t_pb = property(lambda self: self.cur.t_pb)
    t_sq = property(lambda self: self.cur.t_sq)
    b_h = property(lambda self: self.cur.b_h)
    b_xn = property(lambda self: self.cur.b_xn)
    b_ar = property(lambda self: self.cur.b_ar)
    b_f = property(lambda self: self.cur.b_f)
    b_sh = property(lambda self: self.cur.b_sh)
    b_carry = property(lambda self: self.cur.b_carry)
    b_rope = property(lambda self: self.cur.b_rope)
    b_pb = property(lambda self: self.cur.b_pb)
    b_sq = property(lambda self: self.cur.b_sq)

    def h(self, i):
        T = self.cfg.T
        return Pl(self.t_h[:, i * T:(i + 1) * T], [self.b_h[i]])

    def xn(self, i):
        T = self.cfg.T
        return Pl(self.t_xn[:, i * T:(i + 1) * T], [self.b_xn[i]])

    def ar(self, i, n=1):
        T = self.cfg.T
        return Pl(self.t_ar[:, i * T:(i + n) * T], self.b_ar[i:i + n])

    def arf(self, i, n=1):
        T = self.cfg.T
        return Pl(self.t_ar[:, i * T:(i + 2 * n) * T].bitcast(F32), self.b_ar[i:i + 2 * n])

    def f(self, i):
        T = self.cfg.T
        return Pl(self.t_f[:, i * T:(i + 1) * T], [self.b_f[i]])

    def prm(self, col, n=1):
        return self.t_prm[:, col:col + n]

    def cbv(self, col, n):
        return self.t_cb[:, col:col + n]

    def cfv(self, col, n):
        return self.t_cf[:, col:col + n]

    def sm(self, col, n=1):
        return self.t_sm[:, col:col + n]

    def bank(self):
        i = self.free_banks.pop(0)
        return i

    def rel(self, i):
        self.free_banks.append(i)

    def pb(self, i, lo=0, n=None):
        n = self.cfg.T if n is None else n
        return Pl(self.t_ps[:, i, lo:lo + n], [self.b_ps[i]])

    def _slab(self, key, src, dst, n):
        c = self.cfg
        k = self.slab_cache.get(key)
        if k is not None and self.slot_key[k] == key:
            return Pl(self.t_slot[k], [self.b_slot[k]])
        k = self.slot_i
        self.slot_i = (k + 1) % c.NSLOT
        t, b = self.t_slot[k], self.b_slot[k]
        if self.slot_dirty[k] is not None:
            dkey, ddst, dn = self.slot_dirty[k]
            wb = self.b_wsc.setdefault(dkey, Buf("wsc%s%d" % dkey))
            self.S.dma("sp", [lambda e: e.dma_start(out=ddst, in_=t[:, 0:dn])], self.b_sst[k], reads=[b], writes=[wb])
            self.stored.add(dkey)
            self.slot_dirty[k] = None
        self.slot_key[k] = key
        self.slab_cache[key] = k
        if key in self.stored:
            wb = self.b_wsc[key]
            self.S.dma("pool", [lambda e: e.dma_start(out=t[:, 0:n], in_=dst)], b, reads=[wb], writes=[b])
        else:
            self.S.dma("pool", [lambda e: e.dma_start(out=t[:, 0:n], in_=src)], b, writes=[b])
            self.slot_dirty[k] = (key, dst, n)
        return Pl(t, [b])

    def slabA(self, idx):
        return self._slab(("A", idx), self.d_wA[idx], self.d_wAb[idx // self.NAH][idx % self.NAH], self.cfg.SA)

    def slabB(self, idx):
        return self._slab(("B", idx), self.d_wB[idx], self.d_wBb[idx], self.cfg.SB)

    def tile_of_cur(self):
        return self.cur.tile

    def wproj(self, l):
        c = self.cfg
        T = c.T
        s0 = self.streams[0]
        n = (c.SA + T - 1) // T
        ap = s0.t_ar[:, self.WP0 * T: self.WP0 * T + c.SA]
        bufs = s0.b_ar[self.WP0:self.WP0 + n]
        key = (l, self.cur.tile)
        if self.wp_key != key:
            self.wp_key = key
            src = self.d_wA[c.iPLEP + l]
            self.S.dma("pool", [lambda e: e.dma_start(out=ap, in_=src)], bufs[0], writes=bufs)
        return Pl(ap, bufs)

    def mm(self, out, pairs, reads):
        def fn(e):
            n = len(pairs)
            ins = None
            for i, (l, r) in enumerate(pairs):
                ins = e.matmul(out.ap, l, r, start=(i == 0), stop=(i == n - 1))
            return ins
        self.S.op("pe", fn, reads=reads, writes=out.bufs)

    def rstd(self, srcs, dim, fa, fb):
        c = self.cfg
        T = c.T
        S = self.S
        bk = self.bank()
        bp = self.pb(bk)
        ones = self.cbv(c.cbONE, 128)
        n = len(srcs)
        for i, s in enumerate(srcs):
            k = self.cur.sq_i
            self.cur.sq_i ^= 1
            sq = Pl(self.t_sq[:, k * T:(k + 1) * T], [self.b_sq[k]])
            if n > 2 and i % 2 == 1:
                S.op("dve", lambda e, s=s, sq=sq: e.tensor_tensor(out=sq.ap, in0=s.ap, in1=s.ap, op=ALU.mult),
                     reads=s.bufs, writes=sq.bufs)
            else:
                S.op("act", lambda e, s=s, sq=sq: e.activation(out=sq.ap, in_=s.ap, func=AF.Square),
                     reads=s.bufs, writes=sq.bufs)
            S.op("pe", lambda e, sq=sq, i=i: e.matmul(bp.ap, ones, sq.ap, start=(i == 0), stop=(i == n - 1)),
                 reads=sq.bufs + [self.b_cb], writes=bp.bufs)
        sd = self.f(fa)
        S.op("act", lambda e: e.activation(out=sd.ap, in_=bp.ap, func=AF.Ln, bias=self.sm(0), scale=1.0 / dim),
             reads=bp.bufs + [self.b_sm], writes=sd.bufs)
        self.rel(bk)
        rs = self.f(fb)
        S.op("act", lambda e: e.activation(out=rs.ap, in_=sd.ap, func=AF.Exp, scale=-0.5), reads=sd.bufs, writes=rs.bufs)
        return rs

    def stream_norm(self, gi, dst_fn):
        c = self.cfg
        rs = self.rstd([self.h(i) for i in range(c.DC)], c.D, 6, 7)
        for i in range(c.DC):
            hp = self.h(i)
            d = dst_fn(i)
            g = self.prm(c.pG + gi * c.DC + i)
            self.S.op("dve", lambda e, hp=hp, d=d, g=g: e.scalar_tensor_tensor(
                out=d.ap, in0=hp.ap, scalar=g, in1=rs.ap, op0=ALU.mult, op1=ALU.mult),
                reads=hp.bufs + rs.bufs + [self.b_prm], writes=d.bufs)

    def proj_fm(self, slab, col, srcs, ncols=128):
        c = self.cfg
        bk = self.bank()
        out = self.pb(bk)
        W = slab.ap
        n = len(srcs)
        per = c.SA // c.DC if n == c.DC else None
        pairs = []
        rd = list(slab.bufs)
        for i, s in enumerate(srcs):
            pairs.append((W[:, i * per + col: i * per + col + ncols], s.ap))
            rd += s.bufs
        self.mm(out, pairs, rd)
        return bk, out

    def ffn(self, l, f):
        c = self.cfg
        S = self.S
        T = c.T
        self.stream_norm(l * 2 + f, self.xn)
        xs = [self.xn(i) for i in range(c.DC)]
        yield
        for fb in range(c.FB):
            slab = self.slabA(c.iGU + (l * 2 + f) * c.FB + fb)
            bg, pg = self.proj_fm(slab, 0, xs)
            bu, pu = self.proj_fm(slab, 128, xs)
            sg = self.f(fb % 2)
            S.op("act", lambda e, sg=sg, pg=pg: e.activation(out=sg.ap, in_=pg.ap, func=AF.Silu),
                 reads=pg.bufs, writes=sg.bufs)
            self.rel(bg)
            a = self.ar(fb)
            S.op("dve", lambda e, a=a, sg=sg, pu=pu: e.tensor_tensor(out=a.ap, in0=sg.ap, in1=pu.ap, op=ALU.mult),
                 reads=sg.bufs + pu.bufs, writes=a.bufs)
            self.rel(bu)
            yield
        acts = [self.ar(i) for i in range(c.FB)]
        for ob in range(c.DC):
            bk = self.bank()
            out = self.pb(bk)
            for hf in range(2):
                sl = self.slabB(((l * 2 + f) * c.DC + ob) * 2 + hf)
                rd = list(sl.bufs)
                for a in acts[hf * c.FH:(hf + 1) * c.FH]:
                    rd += a.bufs

                def fn(e, sl=sl, hf=hf):
                    ins = None
                    for i in range(c.FH):
                        ins = e.matmul(out.ap, sl.ap[:, i * 128:(i + 1) * 128], acts[hf * c.FH + i].ap,
                                       start=(hf == 0 and i == 0), stop=(hf == 1 and i == c.FH - 1))
                    return ins
                S.op("pe", fn, reads=rd, writes=out.bufs)
                if hf == 0:
                    yield
            hp = self.h(ob)
            S.op("dve", lambda e, hp=hp, out=out: e.scalar_tensor_tensor(
                out=hp.ap, in0=out.ap, scalar=0.5, in1=hp.ap, op0=ALU.mult, op1=ALU.add),
                reads=out.bufs + hp.bufs, writes=hp.bufs)
            self.rel(bk)
            yield

    def add_proj(self, base, srcs):
        c = self.cfg
        for j in range(c.NJ):
            slab = self.slabA(base + j)
            for half in range(2):
                bk, out = self.proj_fm(slab, half * 128, srcs)
                hp = self.h(j * 2 + half)
                self.S.op("dve", lambda e, hp=hp, out=out: e.tensor_tensor(out=hp.ap, in0=out.ap, in1=hp.ap, op=ALU.add),
                          reads=out.bufs + hp.bufs, writes=hp.bufs)
                self.rel(bk)
            yield

    def ple(self, l, seq, tile):
        c = self.cfg
        S = self.S
        T = c.T
        self.stream_norm(7 + l, self.xn)
        xs = [self.xn(i) for i in range(c.DC)]
        src = self.d_p[l, seq, tile]
        S.dma("pool", [lambda e: e.dma_start(out=self.t_pb[:, :], in_=src)], self.b_pb, writes=[self.b_pb])
        wp = self.wproj(l)
        yield
        for j in range(c.NJ):
            slab = self.slabA(c.iPLEG + l * c.NJ + j)
            for half in range(2):
                ob = j * 2 + half
                bg, pg = self.proj_fm(slab, half * 128, xs)
                bp_ = self.bank()
                pp = self.pb(bp_)
                pairs = [(wp.ap[:, pc * c.D + ob * 128: pc * c.D + ob * 128 + 128], self.t_pb[:, pc * T:(pc + 1) * T])
                         for pc in range(c.PC)]
                self.mm(pp, pairs, wp.bufs + [self.b_pb])
                sg = self.f(ob % 2)
                S.op("act", lambda e, sg=sg, pg=pg: e.activation(out=sg.ap, in_=pg.ap, func=AF.Sigmoid),
                     reads=pg.bufs, writes=sg.bufs)
                self.rel(bg)
                t2 = self.f(2 + ob % 2)
                S.op("dve", lambda e, t2=t2, sg=sg, pp=pp: e.tensor_tensor(out=t2.ap, in0=sg.ap, in1=pp.ap, op=ALU.mult),
                     reads=sg.bufs + pp.bufs, writes=t2.bufs)
                self.rel(bp_)
                hp = self.h(ob)
                S.op("dve", lambda e, hp=hp, t2=t2: e.tensor_tensor(out=hp.ap, in0=hp.ap, in1=t2.ap, op=ALU.add),
                     reads=hp.bufs + t2.bufs, writes=hp.bufs)
            yield

    def hgrn(self, first_tile):
        c = self.cfg
        S = self.S
        T = c.T
        NTB, NCH = c.NTB, c.NCH
        NF = self.NF
        self.stream_norm(4, self.xn)
        xs = [self.xn(i) for i in range(c.DC)]
        yield
        per = c.SA // c.DC
        ident_f = self.cfv(c.cfID, 128)
        one = self.sm(2)
        ctx = {}

        def X1(hd):
            st = hd % 2
            PB = 16 + 15 * st
            FO = 8 * st
            slA = self.slabA(c.iWINA + hd)
            slB = self.slabA(c.iWINB + hd)
            bq, pq = self.proj_fm(slA, 0, xs)
            bf_, pf = self.proj_fm(slA, 128, xs)
            bg, pg = self.proj_fm(slB, 128, xs)
            bv = self.bank()
            for tb in range(NTB):
                out = self.pb(bv, tb * 128, 128)
                pairs = [(xs[i].ap[:, tb * 128:(tb + 1) * 128], slB.ap[:, i * per: i * per + 128]) for i in range(c.DC)]
                rd = list(slB.bufs)
                for x_ in xs:
                    rd += x_.bufs
                self.mm(out, pairs, rd)
            pv = self.pb(bv, 0, NTB * 128)
            f0, f1, f2, f3 = self.f(FO), self.f(FO + 1), self.f(FO + 2), self.f(FO + 3)
            qt = self.ar(PB)
            S.op("act", lambda e: e.activation(out=qt.ap, in_=pq.ap, func=AF.Copy, scale=128.0 ** -0.5), reads=pq.bufs, writes=qt.bufs)
            self.rel(bq)
            S.op("act", lambda e: e.activation(out=f1.ap, in_=pf.ap, func=AF.Exp, scale=-1.0), reads=pf.bufs, writes=f1.bufs)
            self.rel(bf_)
            S.op("act", lambda e: e.activation(out=f0.ap, in_=pg.ap, func=AF.Exp, scale=-1.0), reads=pg.bufs, writes=f0.bufs)
            S.op("act", lambda e: e.activation(out=f0.ap, in_=f0.ap, func=AF.Ln, bias=one), reads=f0.bufs + [self.b_sm], writes=f0.bufs)
            S.op("act", lambda e: e.activation(out=f0.ap, in_=f0.ap, func=AF.Exp, scale=-1.0), reads=f0.bufs, writes=f0.bufs)
            gs = self.ar(PB + 9)
            S.op("dve", lambda e: e.tensor_tensor(out=gs.ap, in0=pg.ap, in1=f0.ap, op=ALU.mult), reads=pg.bufs + f0.bufs, writes=gs.bufs)
            self.rel(bg)
            vb = self.ar(PB + 3)
            S.op("act", lambda e: e.activation(out=vb.ap, in_=pv.ap, func=AF.Copy), reads=pv.bufs, writes=vb.bufs)
            v4 = self.ar(PB + 4, 4)
            self.rel(bv)
            m4 = bass.AP(self.t_cb, c.cbM4, [[c.NCB, 128], [1, 4], [0, 128]])
            for tb in range(NTB):
                vin = bass.AP(self.t_ar, (PB + 3) * T + tb * 128, [[c.NAR * T, 128], [0, 4], [1, 128]])
                vout = bass.AP(self.t_ar, (PB + 4) * T + tb * 512, [[c.NAR * T, 128], [128, 4], [1, 128]])
                S.op("dve", lambda e, vin=vin, vout=vout: e.tensor_tensor(out=vout, in0=vin, in1=m4, op=ALU.mult),
                     reads=vb.bufs + [self.b_cb], writes=v4.bufs)
            S.op("act", lambda e: e.activation(out=f2.ap, in_=f1.ap, func=AF.Ln, bias=one, scale=self.sm(8 + hd)),
                 reads=f1.bufs + [self.b_sm], writes=f2.bufs)
            S.op("act", lambda e: e.activation(out=f3.ap, in_=f1.ap, func=AF.Ln, bias=one), reads=f1.bufs + [self.b_sm], writes=f3.bufs)
            S.op("dve", lambda e: e.tensor_tensor(out=f2.ap, in0=f2.ap, in1=f3.ap, op=ALU.subtract), reads=f2.bufs + f3.bufs, writes=f2.bufs)
            S.op("dve", lambda e: e.tensor_tensor_scan(out=f0.ap, data0=self.cfv(c.cfSCAN, T), data1=f2.ap,
                                                       initial=0.0, op0=ALU.mult, op1=ALU.add),
                 reads=f2.bufs + [self.b_cf], writes=f0.bufs)
            ctx[hd] = dict(PB=PB, FO=FO, qt=qt, vb=vb, v4=v4, gs=gs)

        def X3(hd):
            x = ctx[hd]
            PB, FO = x["PB"], x["FO"]
            qt = x["qt"]
            f0, f1, f2, f3 = self.f(FO), self.f(FO + 1), self.f(FO + 2), self.f(FO + 3)
            S.op("act", lambda e: e.activation(out=f3.ap, in_=f2.ap, func=AF.Exp), reads=f2.bufs, writes=f3.bufs)
            S.op("act", lambda e: e.activation(out=f1.ap, in_=f0.ap, func=AF.Exp), reads=f0.bufs, writes=f1.bufs)
            S.op("act", lambda e: e.activation(out=f2.ap, in_=f0.ap, func=AF.Exp, scale=-1.0), reads=f0.bufs, writes=f2.bufs)
            S.op("act", lambda e: e.activation(out=f3.ap, in_=f3.ap, func=AF.Identity, scale=-1.0, bias=one),
                 reads=f3.bufs + [self.b_sm], writes=f3.bufs)
            S.op("dve", lambda e: e.tensor_tensor(out=qt.ap, in0=qt.ap, in1=f1.ap, op=ALU.mult), reads=qt.bufs + f1.bufs, writes=qt.bufs)
            S.op("dve", lambda e: e.tensor_tensor(out=f3.ap, in0=f3.ap, in1=f2.ap, op=ALU.mult), reads=f3.bufs + f2.bufs, writes=f3.bufs)
            kt = self.ar(PB + 1)
            S.op("act", lambda e: e.activation(out=kt.ap, in_=f3.ap, func=AF.Copy), reads=f3.bufs, writes=kt.bufs)
            ebl = bass.AP(self.t_f, (FO + 1) * T + 31, [[NF * T, 128], [32, NCH], [0, 32]])
            kh3 = bass.AP(self.t_f, (FO + 2) * T, [[NF * T, 128], [32, NCH], [1, 32]])
            kk3 = bass.AP(self.t_f, (FO + 3) * T, [[NF * T, 128], [32, NCH], [1, 32]])
            S.op("dve", lambda e: e.tensor_tensor(out=kh3, in0=kk3, in1=ebl, op=ALU.mult),
                 reads=f3.bufs + f1.bufs + f2.bufs, writes=f2.bufs)
            x["kt"], x["eb"], x["kh"] = kt, f1, f2

        def X4(hd):
            x = ctx[hd]
            PB = x["PB"]
            kh = x["kh"]
            bt = self.bank()
            for tb in range(NTB):
                o = self.pb(bt, tb * 128, 128)
                S.op("pe", lambda e, o=o, tb=tb: e.transpose(o.ap, kh.ap[:, tb * 128:(tb + 1) * 128], ident_f),
                     reads=kh.bufs + [self.b_cf], writes=o.bufs)
            ktok = self.ar(PB + 2)
            pt_ = self.pb(bt, 0, NTB * 128)
            S.op("act", lambda e: e.activation(out=ktok.ap, in_=pt_.ap, func=AF.Copy), reads=pt_.bufs, writes=ktok.bufs)
            self.rel(bt)
            x["ktok"] = ktok

        def X2(hd):
            x = ctx[hd]
            PB, FO = x["PB"], x["FO"]
            eb, kt, qt, v4, ktok = x["eb"], x["kt"], x["qt"], x["v4"], x["ktok"]
            bus = []
            for tb in range(NTB):
                bu = self.bank()
                bus.append(bu)
                o = self.pb(bu, 0, 512)
                self.mm(o, [(ktok.ap[:, tb * 128:(tb + 1) * 128], v4.ap[:, tb * 512:(tb + 1) * 512])],
                        ktok.bufs + v4.bufs)
            ba = self.bank()
            for tb in range(NTB):
                o = self.pb(ba, tb * 128, 128)
                self.mm(o, [(kt.ap[:, tb * 128:(tb + 1) * 128], qt.ap[:, tb * 128:(tb + 1) * 128])], kt.bufs + qt.bufs)
            ptm = self.ar(PB + 8)
            pa3 = bass.AP(self.t_ps, ba * 512, [[8 * 512, 128], [128, NTB], [1, 128]])
            pt3 = bass.AP(self.t_ar, (PB + 8) * T, [[c.NAR * T, 128], [128, NTB], [1, 128]])
            bd3 = bass.AP(self.t_cf, c.cfBD, [[c.NCF, 128], [0, NTB], [1, 128]])
            S.op("dve", lambda e: e.tensor_tensor(out=pt3, in0=pa3, in1=bd3, op=ALU.mult),
                 reads=[self.b_ps[ba], self.b_cf], writes=ptm.bufs)
            self.rel(ba)
            sh0 = Pl(self.t_sh[:, 0:128], [self.b_sh[0]])
            car = Pl(self.t_carry[:, hd * 128:(hd + 1) * 128], [self.b_carry[hd]])
            if first_tile:
                S.op("dve", lambda e: e.memset(sh0.ap, 0.0), writes=sh0.bufs)
            else:
                S.op("dve", lambda e: e.tensor_copy(out=sh0.ap, in_=car.ap), reads=car.bufs, writes=sh0.bufs)
            for ch in range(NCH):
                sp = Pl(self.t_sh[:, ch * 128:(ch + 1) * 128], [self.b_sh[ch]])
                sn = Pl(self.t_sh[:, (ch + 1) * 128:(ch + 2) * 128], [self.b_sh[ch + 1]])
                u = self.pb(bus[ch // 4], (ch % 4) * 128, 128)
                dec = self.t_f[:, (FO + 1) * T + ch * 32 + 31: (FO + 1) * T + ch * 32 + 32]
                S.op("dve", lambda e, sp=sp, sn=sn, u=u, dec=dec: e.scalar_tensor_tensor(
                    out=sn.ap, in0=sp.ap, scalar=dec, in1=u.ap, op0=ALU.mult, op1=ALU.add),
                    reads=sp.bufs + u.bufs + eb.bufs, writes=sn.bufs)
            for bu in bus:
                self.rel(bu)
            sb = self.ar(PB + 10, 4)
            shall = Pl(self.t_sh[:, 0:NCH * 128], self.b_sh[0:NCH])
            S.op("act", lambda e: e.activation(out=sb.ap, in_=shall.ap, func=AF.Copy), reads=shall.bufs, writes=sb.bufs)
            slast = Pl(self.t_sh[:, NCH * 128:(NCH + 1) * 128], [self.b_sh[NCH]])
            S.op("act", lambda e: e.activation(out=car.ap, in_=slast.ap, func=AF.Copy), reads=slast.bufs, writes=car.bufs)
            x["ptm"], x["sb"] = ptm, sb

        def P3(hd):
            x = ctx.pop(hd)
            vb, ptm, sb, qt, gs = x["vb"], x["ptm"], x["sb"], x["qt"], x["gs"]
            bo = self.bank()
            for tb in range(NTB):
                o = self.pb(bo, tb * 128, 128)

                def fn(e, tb=tb, o=o):
                    e.matmul(o.ap, vb.ap[:, tb * 128:(tb + 1) * 128], ptm.ap[:, tb * 128:(tb + 1) * 128],
                             start=True, stop=False)
                    ins = None
                    for j in range(4):
                        ch = tb * 4 + j
                        ins = e.matmul(self.t_ps[:, bo, ch * 32:(ch + 1) * 32], sb.ap[:, ch * 128:(ch + 1) * 128],
                                       qt.ap[:, ch * 32:(ch + 1) * 32], start=False, stop=(j == 3))
                    return ins
                S.op("pe", fn, reads=vb.bufs + ptm.bufs + sb.bufs + qt.bufs, writes=o.bufs)
            po = self.pb(bo)
            rs = self.rstd([po], 128, 4, 5)
            t1 = self.f(4)
            S.op("dve", lambda e: e.scalar_tensor_tensor(out=t1.ap, in0=po.ap, scalar=self.prm(c.pON), in1=rs.ap,
                                                         op0=ALU.mult, op1=ALU.mult),
                 reads=po.bufs + rs.bufs + [self.b_prm], writes=t1.bufs)
            self.rel(bo)
            on = self.ar(hd)
            S.op("dve", lambda e: e.tensor_tensor(out=on.ap, in0=t1.ap, in1=gs.ap, op=ALU.mult),
                 reads=t1.bufs + gs.bufs, writes=on.bufs)

        X1(0)
        yield
        X3(0)
        yield
        X4(0)
        yield
        for hd in range(c.HGH):
            nxt = hd + 1 < c.HGH
            if nxt:
                X1(hd + 1)
                yield
            X2(hd)
            yield
            if nxt:
                X3(hd + 1)
                yield
                X4(hd + 1)
                yield
            P3(hd)
            yield
        yield from self.add_proj(c.iWOUT, [self.ar(i) for i in range(c.HGH)])

    def rope_a(self, pk, tmp_plane):
        xb = self.ar(tmp_plane)
        self.S.op("act", lambda e: e.activation(out=xb.ap, in_=pk.ap, func=AF.Copy), reads=pk.bufs, writes=xb.bufs)
        return xb

    def rope_b(self, pk, xb, dst):
        c = self.cfg
        S = self.S
        T = c.T
        bs = self.bank()
        ps_ = self.pb(bs)
        self.mm(ps_, [(self.cbv(c.cbSW, 128), xb.ap)], xb.bufs + [self.b_cb])
        t1 = self.f(4)
        S.op("dve", lambda e: e.tensor_tensor(out=t1.ap, in0=pk.ap, in1=self.t_rope[:, 0:T], op=ALU.mult),
             reads=pk.bufs + [self.b_rope], writes=t1.bufs)
        t2 = self.f(5)
        S.op("dve", lambda e: e.tensor_tensor(out=t2.ap, in0=ps_.ap, in1=self.t_rope[:, T:2 * T], op=ALU.mult),
             reads=ps_.bufs + [self.b_rope], writes=t2.bufs)
        self.rel(bs)
        S.op("dve", lambda e: e.tensor_tensor(out=dst.ap, in0=t1.ap, in1=t2.ap, op=ALU.add),
             reads=t1.bufs + t2.bufs, writes=dst.bufs)

    def load_rope(self, tile):
        c = self.cfg
        T = c.T
        self.S.dma("sp", [lambda e: e.dma_start(out=self.t_rope[:, 0:T], in_=self.d_rope[0, :, tile * T:(tile + 1) * T]),
                          lambda e: e.dma_start(out=self.t_rope[:, T:2 * T], in_=self.d_rope[1, :, tile * T:(tile + 1) * T])],
                   self.b_rope, writes=[self.b_rope])

    def kv(self, seq, tile):
        c = self.cfg
        S = self.S
        T = c.T
        NTB = c.NTB
        per = c.SA // c.DC
        self.stream_norm(6, self.xn)
        xs = [self.xn(i) for i in range(c.DC)]
        yield
        for hd in range(c.DAH):
            base = (hd % 2) * 4
            slK = self.slabA(c.iWK + hd)
            bk0, pk0 = self.proj_fm(slK, 0, xs)
            xb0 = self.rope_a(pk0, 8)
            yield
            bk1, pk1 = self.proj_fm(slK, 128, xs)
            xb1 = self.rope_a(pk1, 9)
            self.rope_b(pk0, xb0, self.ar(base))
            self.rel(bk0)
            yield
            slV = self.slabA(c.iWV + hd)
            vt = self.ar(base + 2, 2)
            for tb in range(NTB):
                if tb == 1 or NTB == 1:
                    self.rope_b(pk1, xb1, self.ar(base + 1))
                    self.rel(bk1)
                bv = self.bank()
                o = self.pb(bv, 0, 256)
                pairs = [(xs[i].ap[:, tb * 128:(tb + 1) * 128], slV.ap[:, i * per: i * per + 256]) for i in range(c.DC)]
                rd = list(slV.bufs)
                for x_ in xs:
                    rd += x_.bufs
                self.mm(o, pairs, rd)
                S.op("act", lambda e, o=o, tb=tb: e.activation(out=vt.ap[:, tb * 256:(tb + 1) * 256], in_=o.ap, func=AF.Copy),
                     reads=o.bufs, writes=vt.bufs)
                self.rel(bv)
            kd, vd = self.b_kd[seq][hd], self.b_vd[seq][hd]
            ksrc = self.ar(base, 2)
            fns = []
            for comp in range(2):
                fns.append(lambda e, comp=comp: e.dma_start(
                    out=self.d_k[seq, hd, :, comp * c.S + tile * T: comp * c.S + (tile + 1) * T],
                    in_=self.t_ar[:, (base + comp) * T:(base + comp + 1) * T]))
            S.dma("sp", fns, kd, reads=ksrc.bufs, writes=[kd])
            vdst = self.d_v[seq, hd, tile * T:(tile + 1) * T, :].rearrange("(tb p) v -> p tb v", p=128)
            vsrc = self.t_ar[:, (base + 2) * T:(base + 4) * T].rearrange("p (tb v) -> p tb v", v=256)
            S.dma("sp", [lambda e: e.dma_start(out=vdst, in_=vsrc)], vd, reads=vt.bufs, writes=[vd])
            yield

    def dattn(self, seq, tile):
        c = self.cfg
        S = self.S
        T = c.T
        NTB = c.NTB
        self.stream_norm(5, self.xn)
        xs = [self.xn(i) for i in range(c.DC)]
        yield
        qpend = None
        for hd in range(c.DAH):
            slQ = self.slabA(c.iWQ + hd)
            for comp in range(2):
                bq, pq = self.proj_fm(slQ, comp * 128, xs)
                xb = self.rope_a(pq, 16 + comp)
                if qpend is not None:
                    self.rope_b(qpend[1], qpend[2], qpend[3])
                    self.rel(qpend[0])
                qpend = (bq, pq, xb, self.ar(hd * 2 + comp))
                yield
        if qpend is not None:
            self.rope_b(qpend[1], qpend[2], qpend[3])
            self.rel(qpend[0])
            qpend = None
            yield
        ntok = (tile + 1) * T
        NKB = ntok // 128
        npl = 2 * c.S // T
        K0 = 18
        V0 = 18 + npl
        P0 = V0 + npl
        ident = self.cbv(c.cbID, 128)
        ones = self.cbv(c.cbONE, 128)
        scale = 128.0 ** -0.5
        pending = None
        for hd in range(c.DAH):
            kd, vd = self.b_kd[seq][hd], self.b_vd[seq][hd]
            ks = self.ar(K0, npl)
            vs = self.ar(V0, npl)
            kdst = self.t_ar[:, K0 * T: K0 * T + 2 * c.S].rearrange("p (c t) -> p c t", c=2)[:, :, 0:ntok]
            ksrc = self.d_k[seq, hd].rearrange("p (c t) -> p c t", c=2)[:, :, 0:ntok]
            S.dma("sp", [lambda e, kdst=kdst, ksrc=ksrc: e.dma_start(out=kdst, in_=ksrc)], ks.bufs[0], reads=[kd], writes=ks.bufs)
            vdst = self.t_ar[:, V0 * T: V0 * T + NKB * 256].rearrange("p (kb v) -> p kb v", v=256)
            vsrc = self.d_v[seq, hd, 0:ntok, :].rearrange("(kb p) v -> p kb v", p=128)
            S.dma("sp", [lambda e, vdst=vdst, vsrc=vsrc: e.dma_start(out=vdst, in_=vsrc)], vs.bufs[0], reads=[vd], writes=vs.bufs)
            FO = 8 * (hd % 2)
            od = [self.f(FO), self.f(FO + 1)]
            for comp in range(2):
                qt = self.ar(hd * 2 + comp)
                bo0, bo1, bl = self.bank(), self.bank(), self.bank()
                po0, po1, pl_ = self.pb(bo0), self.pb(bo1), self.pb(bl)
                def qk2(kb0, it):
                    nk = min(2, NKB - kb0)
                    bs = self.bank()
                    ps2 = self.pb(bs, 0, nk * T)
                    rd = ks.bufs + qt.bufs + [self.b_cb]

                    def fn(e):
                        ins = None
                        for u in range(nk):
                            kb = kb0 + u
                            o = self.t_ps[:, bs, u * T:(u + 1) * T]
                            jd = kb - tile * NTB
                            ins = e.matmul(o, self.t_ar[:, K0 * T + comp * c.S + kb * 128: K0 * T + comp * c.S + (kb + 1) * 128],
                                           qt.ap, start=True, stop=(jd < 0))
                            if jd >= 0:
                                ins = e.matmul(o, ident, self.cbv(c.cbMASK + jd * T, T), start=False, stop=True)
                        return ins
                    S.op("pe", fn, reads=rd, writes=ps2.bufs)
                    pt = self.ar(P0 + 2 * (it % 2), 2)
                    pt2 = Pl(pt.ap[:, 0:nk * T], pt.bufs)
                    S.op("act", lambda e: e.activation(out=pt2.ap, in_=ps2.ap, func=AF.Exp, scale=scale),
                         reads=ps2.bufs, writes=pt2.bufs)
                    self.rel(bs)
                    return (kb0, nk, pt2)

                def pv2(item):
                    kb0, nk, pt2 = item

                    def fn(e):
                        ins = None
                        for u in range(nk):
                            kb = kb0 + u
                            vblk = self.t_ar[:, V0 * T + kb * 256: V0 * T + (kb + 1) * 256]
                            p = pt2.ap[:, u * T:(u + 1) * T]
                            st, sp_ = (kb == 0), (kb == NKB - 1)
                            e.matmul(po0.ap, vblk[:, 0:128], p, start=st, stop=sp_)
                            e.matmul(po1.ap, vblk[:, 128:256], p, start=st, stop=sp_)
                            ins = e.matmul(pl_.ap, ones, p, start=st, stop=sp_)
                        return ins
                    S.op("pe", fn, reads=vs.bufs + pt2.bufs + [self.b_cb], writes=po0.bufs + po1.bufs + pl_.bufs)

                prev = None
                nit = (NKB + 1) // 2
                for it in range(nit + 1):
                    cur_it = qk2(2 * it, it) if it < nit else None
                    if prev is not None:
                        pv2(prev)
                    prev = cur_it
                    if comp == 0 and it == min(1, nit) and pending is not None:
                        pending()
                        pending = None
                    yield
                rl = self.f(FO + 2)
                S.op("act", lambda e: e.activation(out=rl.ap, in_=pl_.ap, func=AF.Ln), reads=pl_.bufs, writes=rl.bufs)
                self.rel(bl)
                S.op("act", lambda e: e.activation(out=rl.ap, in_=rl.ap, func=AF.Exp, scale=-1.0), reads=rl.bufs, writes=rl.bufs)
                for v, (bo, po) in enumerate(((bo0, po0), (bo1, po1))):
                    if comp == 0:
                        S.op("dve", lambda e, po=po, v=v: e.tensor_tensor(out=od[v].ap, in0=po.ap, in1=rl.ap, op=ALU.mult),
                             reads=po.bufs + rl.bufs, writes=od[v].bufs)
                    else:
                        t = self.f(FO + 3)
                        S.op("dve", lambda e, po=po, t=t: e.tensor_tensor(out=t.ap, in0=po.ap, in1=rl.ap, op=ALU.mult),
                             reads=po.bufs + rl.bufs, writes=t.bufs)
                        S.op("dve", lambda e, t=t, v=v: e.scalar_tensor_tensor(
                            out=od[v].ap, in0=t.ap, scalar=self.sm(3), in1=od[v].ap, op0=ALU.mult, op1=ALU.add),
                            reads=t.bufs + od[v].bufs + [self.b_sm], writes=od[v].bufs)
                    self.rel(bo)
            def epi(hd=hd, od=od):
                rs = self.rstd(od, 256, 4, 5)
                for v in range(2):
                    on = self.xn(hd * 2 + v)
                    S.op("dve", lambda e, on=on, v=v: e.scalar_tensor_tensor(
                        out=on.ap, in0=od[v].ap, scalar=self.sm(4 + v), in1=rs.ap, op0=ALU.mult, op1=ALU.mult),
                        reads=od[v].bufs + rs.bufs + [self.b_sm], writes=on.bufs)
            pending = epi
            yield
        if pending is not None:
            pending()
            yield
        yield from self.add_proj(c.iDWOUT, [self.xn(i) for i in range(c.DC)])

    def setup(self):
        c = self.cfg
        S = self.S
        S.dma("sp", [lambda e: e.dma_start(out=self.t_prm[:, :], in_=self.d_prm)], self.b_prm, writes=[self.b_prm])
        S.dma("sp", [lambda e: e.dma_start(out=self.t_cf[:, :], in_=self.d_cf)], self.b_cf, writes=[self.b_cf])
        S.dma("pool", [lambda e: e.dma_start(out=self.t_cb[:, :], in_=self.d_cb)], self.b_cb, writes=[self.b_cb])
        sm = self.sm
        H = c.HGH
        S.op("dve", lambda e: e.memset(self.t_sm[:, :], 0.0), writes=[self.b_sm])
        S.op("dve", lambda e: e.memset(sm(0), EPS), reads=[self.b_sm], writes=[self.b_sm])
        S.op("dve", lambda e: e.memset(sm(2), 1.0), reads=[self.b_sm], writes=[self.b_sm])
        S.op("dve", lambda e: e.tensor_tensor(out=sm(8, H), in0=self.prm(c.pLB0, H), in1=self.prm(c.pLB1, H), op=ALU.subtract),
             reads=[self.b_prm, self.b_sm], writes=[self.b_sm])
        S.op("act", lambda e: e.activation(out=sm(8, H), in_=sm(8, H), func=AF.Sigmoid), reads=[self.b_sm], writes=[self.b_sm])
        S.op("dve", lambda e: e.tensor_scalar(out=sm(8 + H, H), in0=sm(8, H), scalar1=-1.0, scalar2=1.0, op0=ALU.mult, op1=ALU.add),
             reads=[self.b_sm], writes=[self.b_sm])
        S.op("dve", lambda e: e.tensor_scalar(out=sm(8 + 2 * H, H), in0=sm(8 + H, H), scalar1=-1.0, scalar2=None, op0=ALU.mult),
             reads=[self.b_sm], writes=[self.b_sm])
        L = c.pLAM
        S.op("dve", lambda e: e.tensor_tensor(out=self.t_lt[:, 0:128], in0=self.prm(L, 128), in1=self.prm(L + 128, 128), op=ALU.mult),
             reads=[self.b_prm], writes=[self.b_lt])
        S.op("dve", lambda e: e.tensor_tensor(out=self.t_lt[:, 128:256], in0=self.prm(L + 256, 128), in1=self.prm(L + 384, 128), op=ALU.mult),
             reads=[self.b_prm, self.b_lt], writes=[self.b_lt])
        S.op("dve", lambda e: e.reduce_sum(out=sm(6, 2), in_=self.t_lt[:, :].rearrange("p (a b) -> p a b", a=2), axis=AX.X),
             reads=[self.b_lt, self.b_sm], writes=[self.b_sm])
        S.op("act", lambda e: e.activation(out=sm(6, 2), in_=sm(6, 2), func=AF.Exp), reads=[self.b_sm], writes=[self.b_sm])
        S.op("dve", lambda e: e.tensor_tensor(out=sm(1), in0=sm(6), in1=sm(7), op=ALU.subtract), reads=[self.b_sm], writes=[self.b_sm])
        S.op("dve", lambda e: e.tensor_scalar(out=sm(3), in0=sm(1), scalar1=c.lam_init, scalar2=-1.0, op0=ALU.add, op1=ALU.mult),
             reads=[self.b_sm], writes=[self.b_sm])
        S.op("dve", lambda e: e.tensor_scalar(out=sm(4, 2), in0=self.prm(c.pSUB, 2), scalar1=1.0 - c.lam_init, scalar2=None, op0=ALU.mult),
             reads=[self.b_sm, self.b_prm], writes=[self.b_sm])

    def final(self, seq, tile):
        c = self.cfg
        T = c.T
        outp = self.arf(0, c.DC)
        self.stream_norm(9, lambda i: Pl(self.t_ar[:, 2 * i * T:(2 * i + 2) * T].bitcast(F32), self.b_ar[2 * i:2 * i + 2]))
        src = self.t_ar[:, 0:2 * c.DC * T].bitcast(F32)
        self.S.dma("sp", [lambda e: e.dma_start(out=self.d_out[seq, tile], in_=src)], self.cur.b_out, reads=outp.bufs, writes=[self.cur.b_out])

    def seq_gen(self, st):
        c = self.cfg
        S = self.S
        seq = st.seq
        for tile in range(c.NT):
            st.tile = tile
            hall = Pl(self.t_h[:, :], self.b_h)
            S.dma("sp", [lambda e: e.dma_start(out=self.t_h[:, :], in_=self.d_x[seq, tile])],
                  self.b_h[0], writes=hall.bufs)
            self.load_rope(tile)
            stg = self.stages
            if "ffn00" in stg: yield from self.ffn(0, 0)
            if "hgrn" in stg: yield from self.hgrn(tile == 0)
            if "ffn01" in stg: yield from self.ffn(0, 1)
            if "ple0" in stg: yield from self.ple(0, seq, tile)
            if "kv" in stg: yield from self.kv(seq, tile)
            if "ffn10" in stg: yield from self.ffn(1, 0)
            if "dattn" in stg: yield from self.dattn(seq, tile)
            if "ffn11" in stg: yield from self.ffn(1, 1)
            if "ple1" in stg: yield from self.ple(1, seq, tile)
            self.final(seq, tile)
            yield

    def build(self):
        c = self.cfg
        self.setup()
        gens = [self.seq_gen(st) for st in self.streams]
        alive = [True] * len(gens)
        steps = [0] * len(gens)
        while any(alive):
            for i, g in enumerate(gens):
                if not alive[i]:
                    continue
                if i > 0 and alive[i - 1] and steps[i - 1] - steps[i] < c.LAG:
                    continue
                self.cur = self.streams[i]
                try:
                    next(g)
                    steps[i] += 1
                except StopIteration:
                    alive[i] = False
        self.S.final_wait("sp", [st.b_out for st in self.streams])
        return self.nc


_CACHE = {}


def kernel(**inputs):
    cfg = Cfg()
    n = 8
    x = np.asarray(inputs["x"], np.float32)
    p = np.asarray(inputs["p"], np.float32)
    inp = {k: np.asarray(v) for k, v in inputs.items()}
    wA, wB = host_weights(inp, cfg)
    prm = host_params(inp, cfg)
    cb, cf, rope = host_consts(cfg)
    nc = Prog(cfg).build()
    in_maps = []
    for i in range(n):
        xs = x[i * cfg.NSEQ:(i + 1) * cfg.NSEQ]
        ps = p[:, i * cfg.NSEQ:(i + 1) * cfg.NSEQ]
        xT, pT = host_acts(xs, ps, cfg)
        in_maps.append({"xT": xT.reshape(cfg.NSEQ, cfg.NT, 128, -1), "pT": pT.reshape(2, cfg.NSEQ, cfg.NT, 128, -1),
                        "wA": wA, "wB": wB, "prm": prm, "cb": cb, "cf": cf, "rope": rope})
    res = run_bass_kernel_spmd(nc, in_maps, core_ids=list(range(n)))
    outs = []
    for i in range(n):
        oT = np.asarray(res.results[i]["outT"]).reshape(cfg.NSEQ, cfg.NT, 128, cfg.DC, cfg.T)
        outs.append(host_out(oT, cfg))
    return np.concatenate(outs, axis=0).astype(np.float32)
```

```python
import math
import numpy as np
import concourse.bass as bass
import concourse.mybir as mybir
from concourse.bass_utils import run_bass_kernel_spmd

F32 = mybir.dt.float32
BF16 = mybir.dt.bfloat16
AF = mybir.ActivationFunctionType
ALU = mybir.AluOpType
AX = mybir.AxisListType

EPS = 1e-6
ROPE_THETA = 10000.0
NEG = -30000.0


ALL_STAGES = ("ffn00", "hgrn", "ffn01", "ple0", "kv", "ffn10", "dattn", "ffn11", "ple1")


class Cfg:
    def __init__(self, D=2048, DFF=5632, S=2048, T=256, NSEQ=2, PLE=256, NSLOT=5, LAG=2):
        self.D, self.DFF, self.S, self.T, self.NSEQ, self.PLE, self.NSLOT = D, DFF, S, T, NSEQ, PLE, NSLOT
        self.LAG = LAG
        self.DC = D // 128
        self.FB = DFF // 128
        self.HGH = D // 128
        self.DAH = D // 256
        self.NT = S // T
        self.NTB = T // 128
        self.NCH = T // 32
        self.PC = PLE // 128
        assert self.PC * D == self.DC * 256
        self.SA = self.DC * 256
        self.FH = self.FB // 2
        assert self.FH * 2 == self.FB
        self.SB = self.FH * 128
        self.SLOTE = max(self.SA, self.SB)
        self.NJ = self.DC // 2
        o = 0
        self.iGU = o; o += 4 * self.FB
        self.iWINA = o; o += self.HGH
        self.iWINB = o; o += self.HGH
        self.iWOUT = o; o += self.NJ
        self.iWK = o; o += self.DAH
        self.iWV = o; o += self.DAH
        self.iWQ = o; o += self.DAH
        self.iDWOUT = o; o += self.NJ
        self.iPLEG = o; o += 2 * self.NJ
        self.iPLEP = o; o += 2
        self.NA = o
        self.NB = 4 * self.DC * 2
        c = 0
        self.pG = c; c += 10 * self.DC
        self.pLB0 = c; c += self.HGH
        self.pLB1 = c; c += self.HGH
        self.pON = c; c += 1
        self.pSUB = c; c += 2
        self.pLAM = c; c += 512
        self.NP = c
        self.cbID = 0
        self.cbONE = 128
        self.cbSW = 256
        self.cbMASK = 384
        self.cbM4 = 384 + self.NTB * T
        self.NCB = 384 + self.NTB * T + 4
        self.cfID = 0
        self.cfBD = 128
        self.cfM4 = 256
        self.cfSCAN = 260
        self.NCF = 260 + T
        self.NAR = max(self.FB, 22 + 2 * (2 * S // T), 2 * self.DC, 46)
        self.lam_init = 0.8 - 0.6 * float(np.exp(-0.3 * 1))


def _slabA(W, cols, DC):
    sub = W[:, cols]
    return sub.reshape(DC, 128, -1).transpose(1, 0, 2).reshape(128, -1)


def host_weights(inp, cfg):
    D, DC, DFF, FB = cfg.D, cfg.DC, cfg.DFF, cfg.FB
    wA = np.empty((cfg.NA, 128, cfg.SA), np.float32)
    wB = np.empty((cfg.NB, 128, cfg.SB), np.float32)
    ar = np.arange
    for l in range(2):
        for f in range(2):
            Wgu = inp["ffn_w_gate_up"][l, f]
            for fb in range(FB):
                cols = np.concatenate([ar(fb * 128, fb * 128 + 128), DFF + ar(fb * 128, fb * 128 + 128)])
                wA[cfg.iGU + (l * 2 + f) * FB + fb] = _slabA(Wgu, cols, DC)
            Wd = inp["ffn_w_down"][l, f]
            for ob in range(DC):
                sub = Wd[:, ob * 128:(ob + 1) * 128].reshape(2, cfg.FH, 128, 128)
                for hf in range(2):
                    wB[((l * 2 + f) * DC + ob) * 2 + hf] = sub[hf].transpose(1, 0, 2).reshape(128, -1)
    Win = inp["hgrn_w_in"][0]
    for hd in range(cfg.HGH):
        r = ar(hd * 128, hd * 128 + 128)
        wA[cfg.iWINA + hd] = _slabA(Win, np.concatenate([r, D + r]), DC)
        wA[cfg.iWINB + hd] = _slabA(Win, np.concatenate([2 * D + r, 3 * D + r]), DC)
    for j in range(cfg.NJ):
        r = ar(j * 256, j * 256 + 256)
        wA[cfg.iWOUT + j] = _slabA(inp["hgrn_w_out"][0], r, DC)
        wA[cfg.iDWOUT + j] = _slabA(inp["diff_w_out"][0], r, DC)
        for l in range(2):
            wA[cfg.iPLEG + l * cfg.NJ + j] = _slabA(inp["ple_w_gate"][l], r, DC)
    for hd in range(cfg.DAH):
        r = ar(hd * 256, hd * 256 + 256)
        wA[cfg.iWK + hd] = _slabA(inp["w_kv"], r, DC)
        wA[cfg.iWV + hd] = _slabA(inp["w_kv"], D + r, DC)
        wA[cfg.iWQ + hd] = _slabA(inp["diff_w_q"][0], r, DC)
    for l in range(2):
        Wp = inp["ple_w_proj"][l]
        wA[cfg.iPLEP + l] = Wp.reshape(cfg.PC, 128, D).transpose(1, 0, 2).reshape(128, -1)
    return wA, wB


def host_params(inp, cfg):
    DC = cfg.DC
    prm = np.zeros((128, cfg.NP), np.float32)

    def fm(v):
        return np.asarray(v, np.float32).reshape(-1, 128).T

    gl = [inp["ffn_norm"][0, 0], inp["ffn_norm"][0, 1], inp["ffn_norm"][1, 0], inp["ffn_norm"][1, 1],
          inp["mix_norm"][0], inp["mix_norm"][1], inp["kv_norm"], inp["ple_norm"][0], inp["ple_norm"][1],
          inp["final_norm"]]
    for i, g in enumerate(gl):
        prm[:, cfg.pG + i * DC: cfg.pG + (i + 1) * DC] = fm(g)
    prm[:, cfg.pLB0:cfg.pLB0 + cfg.HGH] = fm(inp["hgrn_lower_bounds"][0])
    prm[:, cfg.pLB1:cfg.pLB1 + cfg.HGH] = fm(inp["hgrn_lower_bounds"][1])
    prm[:, cfg.pON] = np.asarray(inp["hgrn_out_norm"][0], np.float32)
    prm[:, cfg.pSUB:cfg.pSUB + 2] = fm(inp["diff_subln"][0])
    prm[:, cfg.pLAM:cfg.pLAM + 512] = np.broadcast_to(
        np.asarray(inp["diff_lambda"][0], np.float32).reshape(1, 512), (128, 512))
    return prm


def host_consts(cfg):
    T, S = cfg.T, cfg.S
    cb = np.zeros((128, cfg.NCB), np.float32)
    cb[:, cfg.cbID:cfg.cbID + 128] = np.eye(128)
    cb[:, cfg.cbONE:cfg.cbONE + 128] = 1.0
    sw = np.zeros((128, 128), np.float32)
    for d in range(128):
        sw[d, (d + 64) % 128] = 1.0
    cb[:, cfg.cbSW:cfg.cbSW + 128] = sw
    p = np.arange(128)[:, None]
    q = np.arange(T)[None, :]
    for j in range(cfg.NTB):
        cb[:, cfg.cbMASK + j * T: cfg.cbMASK + (j + 1) * T] = np.where(j * 128 + p <= q, 0.0, NEG)
    cb[:, cfg.cbM4:cfg.cbM4 + 4] = (np.arange(128)[:, None] // 32 == np.arange(4)[None, :]).astype(np.float32)
    cf = np.zeros((128, cfg.NCF), np.float32)
    cf[:, cfg.cfID:cfg.cfID + 128] = np.eye(128)
    s = np.arange(128)[:, None]
    t = np.arange(128)[None, :]
    cf[:, cfg.cfBD:cfg.cfBD + 128] = ((s <= t) & (s // 32 == t // 32)).astype(np.float32)
    cf[:, cfg.cfM4:cfg.cfM4 + 4] = (s // 32 == np.arange(4)[None, :]).astype(np.float32)
    cf[:, cfg.cfSCAN:cfg.cfSCAN + T] = (np.arange(T) % 32 != 0).astype(np.float32)[None, :]
    inv = (ROPE_THETA ** (-np.arange(0, 128, 2, dtype=np.float32) / 128)).astype(np.float32)
    ang = np.arange(S, dtype=np.float32)[None, :] * np.concatenate([inv, inv])[:, None]
    rope = np.stack([np.cos(ang), np.sin(ang) * np.where(np.arange(128) < 64, -1.0, 1.0)[:, None]]).astype(np.float32)
    return cb, cf, rope


def host_acts(x, p, cfg):
    NS, NT, T, DC, PC = cfg.NSEQ, cfg.NT, cfg.T, cfg.DC, cfg.PC
    xT = np.ascontiguousarray(x.reshape(NS, NT, T, DC, 128).transpose(0, 1, 4, 3, 2))
    pT = np.ascontiguousarray(p.reshape(2, NS, NT, T, PC, 128).transpose(0, 1, 2, 5, 4, 3))
    return xT, pT


def host_out(oT, cfg):
    return np.ascontiguousarray(oT.transpose(0, 1, 4, 3, 2)).reshape(cfg.NSEQ, cfg.S, cfg.D)


class Buf:
    __slots__ = ("name", "w", "r", "dsem", "dcnt", "excl")

    def __init__(self, name, excl=False):
        self.name = name
        self.excl = excl
        self.w = {}
        self.r = {}
        self.dsem = None
        self.dcnt = 0


class Sched:
    def __init__(self, nc):
        self.nc = nc
        self.engs = {"pe": nc.tensor, "act": nc.scalar, "dve": nc.vector, "pool": nc.gpsimd, "sp": nc.sync}
        self.csem = {k: nc.alloc_semaphore("cs_" + k) for k in self.engs}
        self.cnt = {k: 0 for k in self.engs}
        self.seen = {k: {} for k in self.engs}
        self.nsem = 0

    def new_dsem(self, buf):
        buf.dsem = self.nc.alloc_semaphore("ds_%d_%s" % (self.nsem, buf.name))
        self.nsem += 1

    def _wait(self, e, reads, writes):
        need = {}
        own = self.csem[e]
        for b in reads:
            for s, v in b.w.items():
                if need.get(s, 0) < v:
                    need[s] = v
            if b.excl:
                for s, v in b.r.items():
                    if s is not own and s != own and need.get(s, 0) < v:
                        need[s] = v
        for b in writes:
            for s, v in b.w.items():
                if need.get(s, 0) < v:
                    need[s] = v
            for s, v in b.r.items():
                if need.get(s, 0) < v:
                    need[s] = v
        seen = self.seen[e]
        for s, v in need.items():
            if seen.get(s, 0) >= v:
                continue
            self.engs[e].wait_ge(s, v)
            seen[s] = v

    def _mark(self, s, v, reads, writes):
        for b in reads:
            if b.r.get(s, 0) < v:
                b.r[s] = v
        for b in writes:
            if b.w.get(s, 0) < v:
                b.w[s] = v

    def op(self, e, fn, reads=(), writes=()):
        self._wait(e, reads, writes)
        ins = fn(self.engs[e])
        self.cnt[e] += 1
        s = self.csem[e]
        ins.then_inc(s, 1)
        self._mark(s, self.cnt[e], reads, writes)

    def dma(self, q, fns, owner, reads=(), writes=()):
        self._wait(q, reads, writes)
        if owner.dsem is None:
            self.new_dsem(owner)
        for fn in fns:
            fn(self.engs[q]).then_inc(owner.dsem, 16)
        owner.dcnt += 16 * len(fns)
        self._mark(owner.dsem, owner.dcnt, reads, writes)

    def final_wait(self, e, bufs):
        self._wait(e, bufs, bufs)


class Pl:
    __slots__ = ("ap", "bufs")

    def __init__(self, ap, bufs):
        self.ap = ap
        self.bufs = bufs


class Stream:
    def __init__(self, nc, c, i, NF):
        A = nc.alloc_sbuf_tensor
        T = c.T
        n = "_s%d" % i
        self.seq = i
        self.t_h = A("h" + n, [128, c.DC * T], F32)
        self.t_xn = A("xn" + n, [128, c.DC * T], BF16)
        self.t_ar = A("arena" + n, [128, c.NAR * T], BF16)
        self.t_sh = A("sh" + n, [128, (c.NCH + 1) * 128], F32)
        self.t_carry = A("carry" + n, [128, c.HGH * 128], F32)
        self.t_f = A("ftmp" + n, [128, NF * T], F32)
        self.t_rope = A("ropes" + n, [128, 2 * T], F32)
        self.t_pb = A("pb" + n, [128, c.PC * T], BF16)
        self.t_sq = A("sqtmp" + n, [128, 2 * T], BF16)
        B = Buf
        self.b_h = [B("h%d" % j + n) for j in range(c.DC)]
        self.b_xn = [B("xn%d" % j + n) for j in range(c.DC)]
        self.b_ar = [B("ar%d" % j + n) for j in range(c.NAR)]
        self.b_sh = [B("sh%d" % j + n) for j in range(c.NCH + 1)]
        self.b_carry = [B("carry%d" % j + n) for j in range(c.HGH)]
        self.b_f = [B("f%d" % j + n) for j in range(NF)]
        self.b_rope = B("rope" + n)
        self.b_pb = B("pb" + n)
        self.b_sq = [B("sq0" + n), B("sq1" + n)]
        self.sq_i = 0
        self.tile = 0
        self.b_out = B("out" + n)


class Prog:
    def __init__(self, cfg):
        self.cfg = cfg
        c = cfg
        nc = bass.Bass("TRN2", target_bir_lowering=False)
        self.nc = nc
        T = c.T
        self.d_x = nc.dram_tensor("xT", [c.NSEQ, c.NT, 128, c.DC * T], F32, kind="ExternalInput").ap()
        self.d_p = nc.dram_tensor("pT", [2, c.NSEQ, c.NT, 128, c.PC * T], F32, kind="ExternalInput").ap()
        self.d_wA = nc.dram_tensor("wA", [c.NA, 128, c.SA], F32, kind="ExternalInput").ap()
        self.d_wB = nc.dram_tensor("wB", [c.NB, 128, c.SB], F32, kind="ExternalInput").ap()
        self.d_prm = nc.dram_tensor("prm", [128, c.NP], F32, kind="ExternalInput").ap()
        self.d_cb = nc.dram_tensor("cb", [128, c.NCB], F32, kind="ExternalInput").ap()
        self.d_cf = nc.dram_tensor("cf", [128, c.NCF], F32, kind="ExternalInput").ap()
        self.d_rope = nc.dram_tensor("rope", [2, 128, c.S], F32, kind="ExternalInput").ap()
        self.d_out = nc.dram_tensor("outT", [c.NSEQ, c.NT, 128, c.DC * T], F32, kind="ExternalOutput").ap()
        self.d_k = nc.dram_tensor("kscr", [c.NSEQ, c.DAH, 128, 2 * c.S], BF16).ap()
        self.d_v = nc.dram_tensor("vscr", [c.NSEQ, c.DAH, c.S, 256], BF16).ap()
        self.NAH = (c.NA + 2) // 3
        self.d_wAb = [nc.dram_tensor("wAb%d" % i, [self.NAH, 128, c.SA], BF16).ap() for i in range(3)]
        self.d_wBb = nc.dram_tensor("wBb", [c.NB, 128, c.SB], BF16).ap()
        A = nc.alloc_sbuf_tensor
        self.NF = 12
        self.streams = [Stream(nc, c, i, self.NF) for i in range(c.NSEQ)]
        self.cur = self.streams[0]
        self.t_slot = [A("slot%d" % i, [128, c.SLOTE], BF16) for i in range(c.NSLOT)]
        self.t_cb = A("cbs", [128, c.NCB], BF16)
        self.t_cf = A("cfs", [128, c.NCF], F32)
        self.t_prm = A("prms", [128, c.NP], F32)
        self.t_sm = A("small", [128, 64], F32)
        self.t_lt = A("lamtmp", [128, 256], F32)
        self.t_ps = nc.alloc_psum_tensor("ps", [128, 8, 512], F32)
        self.WP0 = c.NAR - (c.SA + T - 1) // T
        B = Buf
        self.b_slot = [B("slot%d" % i) for i in range(c.NSLOT)]
        self.b_cb = B("cb")
        self.b_cf = B("cf")
        self.b_prm = B("prm")
        self.b_sm = B("sm")
        self.b_lt = B("lt")
        self.b_ps = [B("ps%d" % i, excl=True) for i in range(8)]
        self.b_kd = [[B("kd%d_%d" % (s, h)) for h in range(c.DAH)] for s in range(c.NSEQ)]
        self.b_vd = [[B("vd%d_%d" % (s, h)) for h in range(c.DAH)] for s in range(c.NSEQ)]
        self.slab_cache = {}
        self.slot_key = [None] * c.NSLOT
        self.slot_dirty = [None] * c.NSLOT
        self.b_sst = [B("sst%d" % i) for i in range(c.NSLOT)]
        self.b_wsc = {}
        self.stored = set()
        self.wp_key = None
        self.S = Sched(nc)
        self.free_banks = list(range(8))
        self.slot_i = 0
        self.stages = ALL_STAGES

    t_h = property(lambda self: self.cur.t_h)
    t_xn = property(lambda self: self.cur.t_xn)
    t_ar = property(lambda self: self.cur.t_ar)
    t_f = property(lambda self: self.cur.t_f)
    t_sh = property(lambda self: self.cur.t_sh)
    t_carry = property(lambda self: self.cur.t_carry)
    t_rope = property(lambda self: self.cur.t_rope)
    t_pb = property(lambda self: self.cur.t_pb)
    t_sq = property(lambda self: self.cur.t_sq)
    b_h = property(lambda self: self.cur.b_h)
    b_xn = property(lambda self: self.cur.b_xn)
    b_ar = property(lambda self: self.cur.b_ar)
    b_f = property(lambda self: self.cur.b_f)
    b_sh = property(lambda self: self.cur.b_sh)
    b_carry = property(lambda self: self.cur.b_carry)
    b_rope = property(lambda self: self.cur.b_rope)
    b_pb = property(lambda self: self.cur.b_pb)
    b_sq = property(lambda self: self.cur.b_sq)

    def h(self, i):
        T = self.cfg.T
        return Pl(self.t_h[:, i * T:(i + 1) * T], [self.b_h[i]])

    def xn(self, i):
        T = self.cfg.T
        return Pl(self.t_xn[:, i * T:(i + 1) * T], [self.b_xn[i]])

    def ar(self, i, n=1):
        T = self.cfg.T
        return Pl(self.t_ar[:, i * T:(i + n) * T], self.b_ar[i:i + n])

    def arf(self, i, n=1):
        T = self.cfg.T
        return Pl(self.t_ar[:, i * T:(i + 2 * n) * T].bitcast(F32), self.b_ar[i:i + 2 * n])

    def f(self, i):
        T = self.cfg.T
        return Pl(self.t_f[:, i * T:(i + 1) * T], [self.b_f[i]])

    def prm(self, col, n=1):
        return self.t_prm[:, col:col + n]

    def cbv(self, col, n):
        return self.t_cb[:, col:col + n]

    def cfv(self, col, n):
        return self.t_cf[:, col:col + n]

    def sm(self, col, n=1):
        return self.t_sm[:, col:col + n]

    def bank(self):
        i = self.free_banks.pop(0)
        return i

    def rel(self, i):
        self.free_banks.append(i)

    def pb(self, i, lo=0, n=None):
        n = self.cfg.T if n is None else n
        return Pl(self.t_ps[:, i, lo:lo + n], [self.b_ps[i]])

    def _slab(self, key, src, dst, n):
        c = self.cfg
        k = self.slab_cache.get(key)
        if k is not None and self.slot_key[k] == key:
            return Pl(self.t_slot[k], [self.b_slot[k]])
        k = self.slot_i
        self.slot_i = (k + 1) % c.NSLOT
        t, b = self.t_slot[k], self.b_slot[k]
        if self.slot_dirty[k] is not None:
            dkey, ddst, dn = self.slot_dirty[k]
            wb = self.b_wsc.setdefault(dkey, Buf("wsc%s%d" % dkey))
            self.S.dma("sp", [lambda e: e.dma_start(out=ddst, in_=t[:, 0:dn])], self.b_sst[k], reads=[b], writes=[wb])
            self.stored.add(dkey)
            self.slot_dirty[k] = None
        self.slot_key[k] = key
        self.slab_cache[key] = k
        if key in self.stored:
            wb = self.b_wsc[key]
            self.S.dma("pool", [lambda e: e.dma_start(out=t[:, 0:n], in_=dst)], b, reads=[wb], writes=[b])
        else:
            self.S.dma("pool", [lambda e: e.dma_start(out=t[:, 0:n], in_=src)], b, writes=[b])
            self.slot_dirty[k] = (key, dst, n)
        return Pl(t, [b])

    def slabA(self, idx):
        return self._slab(("A", idx), self.d_wA[idx], self.d_wAb[idx // self.NAH][idx % self.NAH], self.cfg.SA)

    def slabB(self, idx):
        return self._slab(("B", idx), self.d_wB[idx], self.d_wBb[idx], self.cfg.SB)

    def tile_of_cur(self):
        return self.cur.tile

    def wproj(self, l):
        c = self.cfg
        T = c.T
        s0 = self.streams[0]
        n = (c.SA + T - 1) // T
        ap = s0.t_ar[:, self.WP0 * T: self.WP0 * T + c.SA]
        bufs = s0.b_ar[self.WP0:self.WP0 + n]
        key = (l, self.cur.tile)
        if self.wp_key != key:
            self.wp_key = key
            src = self.d_wA[c.iPLEP + l]
            self.S.dma("pool", [lambda e: e.dma_start(out=ap, in_=src)], bufs[0], writes=bufs)
        return Pl(ap, bufs)

    def mm(self, out, pairs, reads):
        def fn(e):
            n = len(pairs)
            ins = None
            for i, (l, r) in enumerate(pairs):
                ins = e.matmul(out.ap, l, r, start=(i == 0), stop=(i == n - 1))
            return ins
        self.S.op("pe", fn, reads=reads, writes=out.bufs)

    def rstd_a(self, srcs, dsts=None):
        c = self.cfg
        T = c.T
        S = self.S
        n = len(srcs)
        out = []
        for i, s in enumerate(srcs):
            if dsts is None:
                k = self.cur.sq_i
                self.cur.sq_i ^= 1
                sq = Pl(self.t_sq[:, k * T:(k + 1) * T], [self.b_sq[k]])
            else:
                sq = dsts[i]
            if n > 2 and i % 2 == 1:
                S.op("dve", lambda e, s=s, sq=sq: e.tensor_tensor(out=sq.ap, in0=s.ap, in1=s.ap, op=ALU.mult),
                     reads=s.bufs, writes=sq.bufs)
            else:
                S.op("act", lambda e, s=s, sq=sq: e.activation(out=sq.ap, in_=s.ap, func=AF.Square),
                     reads=s.bufs, writes=sq.bufs)
            out.append(sq)
        return out

    def rstd_b(self, sqs, dim, fa, fb):
        c = self.cfg
        S = self.S
        bk = self.bank()
        bp = self.pb(bk)
        ones = self.cbv(c.cbONE, 128)
        n = len(sqs)
        for i, sq in enumerate(sqs):
            S.op("pe", lambda e, sq=sq, i=i: e.matmul(bp.ap, ones, sq.ap, start=(i == 0), stop=(i == n - 1)),
                 reads=sq.bufs + [self.b_cb], writes=bp.bufs)
        sd = self.f(fa)
        S.op("act", lambda e: e.activation(out=sd.ap, in_=bp.ap, func=AF.Ln, bias=self.sm(0), scale=1.0 / dim),
             reads=bp.bufs + [self.b_sm], writes=sd.bufs)
        self.rel(bk)
        rs = self.f(fb)
        S.op("act", lambda e: e.activation(out=rs.ap, in_=sd.ap, func=AF.Exp, scale=-0.5), reads=sd.bufs, writes=rs.bufs)
        return rs

    def rstd(self, srcs, dim, fa, fb):
        return self.rstd_b(self.rstd_a(srcs), dim, fa, fb)

    def stream_norm(self, gi, dst_fn):
        c = self.cfg
        sqs = self.rstd_a([self.h(i) for i in range(c.DC)], dsts=[self.xn(i) for i in range(c.DC)])
        yield
        rs = self.rstd_b(sqs, c.D, 6, 7)
        yield
        for i in range(c.DC):
            hp = self.h(i)
            d = dst_fn(i)
            g = self.prm(c.pG + gi * c.DC + i)
            self.S.op("dve", lambda e, hp=hp, d=d, g=g: e.scalar_tensor_tensor(
                out=d.ap, in0=hp.ap, scalar=g, in1=rs.ap, op0=ALU.mult, op1=ALU.mult),
                reads=hp.bufs + rs.bufs + [self.b_prm], writes=d.bufs)

    def proj_fm(self, slab, col, srcs, ncols=128):
        c = self.cfg
        bk = self.bank()
        out = self.pb(bk)
        W = slab.ap
        n = len(srcs)
        per = c.SA // c.DC if n == c.DC else None
        pairs = []
        rd = list(slab.bufs)
        for i, s in enumerate(srcs):
            pairs.append((W[:, i * per + col: i * per + col + ncols], s.ap))
            rd += s.bufs
        self.mm(out, pairs, rd)
        return bk, out

    def ffn(self, l, f):
        c = self.cfg
        S = self.S
        T = c.T
        yield from self.stream_norm(l * 2 + f, self.xn)
        xs = [self.xn(i) for i in range(c.DC)]
        yield
        for fb in range(c.FB):
            slab = self.slabA(c.iGU + (l * 2 + f) * c.FB + fb)
            bg, pg = self.proj_fm(slab, 0, xs)
            bu, pu = self.proj_fm(slab, 128, xs)
            sg = self.f(fb % 2)
            S.op("act", lambda e, sg=sg, pg=pg: e.activation(out=sg.ap, in_=pg.ap, func=AF.Silu),
                 reads=pg.bufs, writes=sg.bufs)
            self.rel(bg)
            a = self.ar(fb)
            S.op("dve", lambda e, a=a, sg=sg, pu=pu: e.tensor_tensor(out=a.ap, in0=sg.ap, in1=pu.ap, op=ALU.mult),
                 reads=sg.bufs + pu.bufs, writes=a.bufs)
            self.rel(bu)
            yield
        acts = [self.ar(i) for i in range(c.FB)]
        for ob in range(c.DC):
            bk = self.bank()
            out = self.pb(bk)
            for hf in range(2):
                sl = self.slabB(((l * 2 + f) * c.DC + ob) * 2 + hf)
                rd = list(sl.bufs)
                for a in acts[hf * c.FH:(hf + 1) * c.FH]:
                    rd += a.bufs

                def fn(e, sl=sl, hf=hf):
                    ins = None
                    for i in range(c.FH):
                        ins = e.matmul(out.ap, sl.ap[:, i * 128:(i + 1) * 128], acts[hf * c.FH + i].ap,
                                       start=(hf == 0 and i == 0), stop=(hf == 1 and i == c.FH - 1))
                    return ins
                S.op("pe", fn, reads=rd, writes=out.bufs)
                if hf == 0:
                    yield
            hp = self.h(ob)
            S.op("dve", lambda e, hp=hp, out=out: e.scalar_tensor_tensor(
                out=hp.ap, in0=out.ap, scalar=0.5, in1=hp.ap, op0=ALU.mult, op1=ALU.add),
                reads=out.bufs + hp.bufs, writes=hp.bufs)
            self.rel(bk)
            yield

    def add_proj(self, base, srcs):
        c = self.cfg
        for j in range(c.NJ):
            slab = self.slabA(base + j)
            for half in range(2):
                bk, out = self.proj_fm(slab, half * 128, srcs)
                hp = self.h(j * 2 + half)
                self.S.op("dve", lambda e, hp=hp, out=out: e.tensor_tensor(out=hp.ap, in0=out.ap, in1=hp.ap, op=ALU.add),
                          reads=out.bufs + hp.bufs, writes=hp.bufs)
                self.rel(bk)
            yield

    def ple(self, l, seq, tile):
        c = self.cfg
        S = self.S
        T = c.T
        yield from self.stream_norm(7 + l, self.xn)
        xs = [self.xn(i) for i in range(c.DC)]
        src = self.d_p[l, seq, tile]
        S.dma("pool", [lambda e: e.dma_start(out=self.t_pb[:, :], in_=src)], self.b_pb, writes=[self.b_pb])
        wp = self.wproj(l)
        yield
        for j in range(c.NJ):
            slab = self.slabA(c.iPLEG + l * c.NJ + j)
            for half in range(2):
                ob = j * 2 + half
                bg, pg = self.proj_fm(slab, half * 128, xs)
                bp_ = self.bank()
                pp = self.pb(bp_)
                pairs = [(wp.ap[:, pc * c.D + ob * 128: pc * c.D + ob * 128 + 128], self.t_pb[:, pc * T:(pc + 1) * T])
                         for pc in range(c.PC)]
                self.mm(pp, pairs, wp.bufs + [self.b_pb])
                sg = self.f(ob % 2)
                S.op("act", lambda e, sg=sg, pg=pg: e.activation(out=sg.ap, in_=pg.ap, func=AF.Sigmoid),
                     reads=pg.bufs, writes=sg.bufs)
                self.rel(bg)
                t2 = self.f(2 + ob % 2)
                S.op("dve", lambda e, t2=t2, sg=sg, pp=pp: e.tensor_tensor(out=t2.ap, in0=sg.ap, in1=pp.ap, op=ALU.mult),
                     reads=sg.bufs + pp.bufs, writes=t2.bufs)
                self.rel(bp_)
                hp = self.h(ob)
                S.op("dve", lambda e, hp=hp, t2=t2: e.tensor_tensor(out=hp.ap, in0=hp.ap, in1=t2.ap, op=ALU.add),
                     reads=hp.bufs + t2.bufs, writes=hp.bufs)
            yield

    def hgrn(self, first_tile):
        c = self.cfg
        S = self.S
        T = c.T
        NTB, NCH = c.NTB, c.NCH
        NF = self.NF
        yield from self.stream_norm(4, self.xn)
        xs = [self.xn(i) for i in range(c.DC)]
        yield
        per = c.SA // c.DC
        ident_f = self.cfv(c.cfID, 128)
        one = self.sm(2)
        ctx = {}

        def X1(hd):
            st = hd % 2
            PB = 16 + 15 * st
            FO = 8 * st
            slA = self.slabA(c.iWINA + hd)
            slB = self.slabA(c.iWINB + hd)
            bq, pq = self.proj_fm(slA, 0, xs)
            bf_, pf = self.proj_fm(slA, 128, xs)
            bg, pg = self.proj_fm(slB, 128, xs)
            bv = self.bank()
            for tb in range(NTB):
                out = self.pb(bv, tb * 128, 128)
                pairs = [(xs[i].ap[:, tb * 128:(tb + 1) * 128], slB.ap[:, i * per: i * per + 128]) for i in range(c.DC)]
                rd = list(slB.bufs)
                for x_ in xs:
                    rd += x_.bufs
                self.mm(out, pairs, rd)
            pv = self.pb(bv, 0, NTB * 128)
            f0, f1, f2, f3 = self.f(FO), self.f(FO + 1), self.f(FO + 2), self.f(FO + 3)
            qt = self.ar(PB)
            S.op("act", lambda e: e.activation(out=qt.ap, in_=pq.ap, func=AF.Copy, scale=128.0 ** -0.5), reads=pq.bufs, writes=qt.bufs)
            self.rel(bq)
            S.op("act", lambda e: e.activation(out=f1.ap, in_=pf.ap, func=AF.Exp, scale=-1.0), reads=pf.bufs, writes=f1.bufs)
            self.rel(bf_)
            S.op("act", lambda e: e.activation(out=f0.ap, in_=pg.ap, func=AF.Exp, scale=-1.0), reads=pg.bufs, writes=f0.bufs)
            S.op("act", lambda e: e.activation(out=f0.ap, in_=f0.ap, func=AF.Ln, bias=one), reads=f0.bufs + [self.b_sm], writes=f0.bufs)
            S.op("act", lambda e: e.activation(out=f0.ap, in_=f0.ap, func=AF.Exp, scale=-1.0), reads=f0.bufs, writes=f0.bufs)
            gs = self.ar(PB + 9)
            S.op("dve", lambda e: e.tensor_tensor(out=gs.ap, in0=pg.ap, in1=f0.ap, op=ALU.mult), reads=pg.bufs + f0.bufs, writes=gs.bufs)
            self.rel(bg)
            vb = self.ar(PB + 3)
            S.op("act", lambda e: e.activation(out=vb.ap, in_=pv.ap, func=AF.Copy), reads=pv.bufs, writes=vb.bufs)
            v4 = self.ar(PB + 4, 4)
            self.rel(bv)
            m4 = bass.AP(self.t_cb, c.cbM4, [[c.NCB, 128], [1, 4], [0, 128]])
            for tb in range(NTB):
                vin = bass.AP(self.t_ar, (PB + 3) * T + tb * 128, [[c.NAR * T, 128], [0, 4], [1, 128]])
                vout = bass.AP(self.t_ar, (PB + 4) * T + tb * 512, [[c.NAR * T, 128], [128, 4], [1, 128]])
                S.op("dve", lambda e, vin=vin, vout=vout: e.tensor_tensor(out=vout, in0=vin, in1=m4, op=ALU.mult),
                     reads=vb.bufs + [self.b_cb], writes=v4.bufs)
            S.op("act", lambda e: e.activation(out=f2.ap, in_=f1.ap, func=AF.Ln, bias=one, scale=self.sm(8 + hd)),
                 reads=f1.bufs + [self.b_sm], writes=f2.bufs)
            S.op("act", lambda e: e.activation(out=f3.ap, in_=f1.ap, func=AF.Ln, bias=one), reads=f1.bufs + [self.b_sm], writes=f3.bufs)
            S.op("dve", lambda e: e.tensor_tensor(out=f2.ap, in0=f2.ap, in1=f3.ap, op=ALU.subtract), reads=f2.bufs + f3.bufs, writes=f2.bufs)
            S.op("dve", lambda e: e.tensor_tensor_scan(out=f0.ap, data0=self.cfv(c.cfSCAN, T), data1=f2.ap,
                                                       initial=0.0, op0=ALU.mult, op1=ALU.add),
                 reads=f2.bufs + [self.b_cf], writes=f0.bufs)
            ctx[hd] = dict(PB=PB, FO=FO, qt=qt, vb=vb, v4=v4, gs=gs)

        def X3(hd):
            x = ctx[hd]
            PB, FO = x["PB"], x["FO"]
            qt = x["qt"]
            f0, f1, f2, f3 = self.f(FO), self.f(FO + 1), self.f(FO + 2), self.f(FO + 3)
            S.op("act", lambda e: e.activation(out=f3.ap, in_=f2.ap, func=AF.Exp), reads=f2.bufs, writes=f3.bufs)
            S.op("act", lambda e: e.activation(out=f1.ap, in_=f0.ap, func=AF.Exp), reads=f0.bufs, writes=f1.bufs)
            S.op("act", lambda e: e.activation(out=f2.ap, in_=f0.ap, func=AF.Exp, scale=-1.0), reads=f0.bufs, writes=f2.bufs)
            S.op("act", lambda e: e.activation(out=f3.ap, in_=f3.ap, func=AF.Identity, scale=-1.0, bias=one),
                 reads=f3.bufs + [self.b_sm], writes=f3.bufs)
            S.op("dve", lambda e: e.tensor_tensor(out=qt.ap, in0=qt.ap, in1=f1.ap, op=ALU.mult), reads=qt.bufs + f1.bufs, writes=qt.bufs)
            S.op("dve", lambda e: e.tensor_tensor(out=f3.ap, in0=f3.ap, in1=f2.ap, op=ALU.mult), reads=f3.bufs + f2.bufs, writes=f3.bufs)
            kt = self.ar(PB + 1)
            S.op("act", lambda e: e.activation(out=kt.ap, in_=f3.ap, func=AF.Copy), reads=f3.bufs, writes=kt.bufs)
            ebl = bass.AP(self.t_f, (FO + 1) * T + 31, [[NF * T, 128], [32, NCH], [0, 32]])
            kh3 = bass.AP(self.t_f, (FO + 2) * T, [[NF * T, 128], [32, NCH], [1, 32]])
            kk3 = bass.AP(self.t_f, (FO + 3) * T, [[NF * T, 128], [32, NCH], [1, 32]])
            S.op("dve", lambda e: e.tensor_tensor(out=kh3, in0=kk3, in1=ebl, op=ALU.mult),
                 reads=f3.bufs + f1.bufs + f2.bufs, writes=f2.bufs)
            x["kt"], x["eb"], x["kh"] = kt, f1, f2

        def X4(hd):
            x = ctx[hd]
            PB = x["PB"]
            kh = x["kh"]
            bt = self.bank()
            for tb in range(NTB):
                o = self.pb(bt, tb * 128, 128)
                S.op("pe", lambda e, o=o, tb=tb: e.transpose(o.ap, kh.ap[:, tb * 128:(tb + 1) * 128], ident_f),
                     reads=kh.bufs + [self.b_cf], writes=o.bufs)
            ktok = self.ar(PB + 2)
            pt_ = self.pb(bt, 0, NTB * 128)
            S.op("act", lambda e: e.activation(out=ktok.ap, in_=pt_.ap, func=AF.Copy), reads=pt_.bufs, writes=ktok.bufs)
            self.rel(bt)
            x["ktok"] = ktok

        def X2(hd):
            x = ctx[hd]
            PB, FO = x["PB"], x["FO"]
            eb, kt, qt, v4, ktok = x["eb"], x["kt"], x["qt"], x["v4"], x["ktok"]
            bus = []
            for tb in range(NTB):
                bu = self.bank()
                bus.append(bu)
                o = self.pb(bu, 0, 512)
                self.mm(o, [(ktok.ap[:, tb * 128:(tb + 1) * 128], v4.ap[:, tb * 512:(tb + 1) * 512])],
                        ktok.bufs + v4.bufs)
            ba = self.bank()
            for tb in range(NTB):
                o = self.pb(ba, tb * 128, 128)
                self.mm(o, [(kt.ap[:, tb * 128:(tb + 1) * 128], qt.ap[:, tb * 128:(tb + 1) * 128])], kt.bufs + qt.bufs)
            ptm = self.ar(PB + 8)
            pa3 = bass.AP(self.t_ps, ba * 512, [[8 * 512, 128], [128, NTB], [1, 128]])
            pt3 = bass.AP(self.t_ar, (PB + 8) * T, [[c.NAR * T, 128], [128, NTB], [1, 128]])
            bd3 = bass.AP(self.t_cf, c.cfBD, [[c.NCF, 128], [0, NTB], [1, 128]])
            S.op("dve", lambda e: e.tensor_tensor(out=pt3, in0=pa3, in1=bd3, op=ALU.mult),
                 reads=[self.b_ps[ba], self.b_cf], writes=ptm.bufs)
            self.rel(ba)
            sh0 = Pl(self.t_sh[:, 0:128], [self.b_sh[0]])
            car = Pl(self.t_carry[:, hd * 128:(hd + 1) * 128], [self.b_carry[hd]])
            if first_tile:
                S.op("dve", lambda e: e.memset(sh0.ap, 0.0), writes=sh0.bufs)
            else:
                S.op("dve", lambda e: e.tensor_copy(out=sh0.ap, in_=car.ap), reads=car.bufs, writes=sh0.bufs)
            for ch in range(NCH):
                sp = Pl(self.t_sh[:, ch * 128:(ch + 1) * 128], [self.b_sh[ch]])
                sn = Pl(self.t_sh[:, (ch + 1) * 128:(ch + 2) * 128], [self.b_sh[ch + 1]])
                u = self.pb(bus[ch // 4], (ch % 4) * 128, 128)
                dec = self.t_f[:, (FO + 1) * T + ch * 32 + 31: (FO + 1) * T + ch * 32 + 32]
                S.op("dve", lambda e, sp=sp, sn=sn, u=u, dec=dec: e.scalar_tensor_tensor(
                    out=sn.ap, in0=sp.ap, scalar=dec, in1=u.ap, op0=ALU.mult, op1=ALU.add),
                    reads=sp.bufs + u.bufs + eb.bufs, writes=sn.bufs)
            for bu in bus:
                self.rel(bu)
            sb = self.ar(PB + 10, 4)
            shall = Pl(self.t_sh[:, 0:NCH * 128], self.b_sh[0:NCH])
            S.op("act", lambda e: e.activation(out=sb.ap, in_=shall.ap, func=AF.Copy), reads=shall.bufs, writes=sb.bufs)
            slast = Pl(self.t_sh[:, NCH * 128:(NCH + 1) * 128], [self.b_sh[NCH]])
            S.op("act", lambda e: e.activation(out=car.ap, in_=slast.ap, func=AF.Copy), reads=slast.bufs, writes=car.bufs)
            x["ptm"], x["sb"] = ptm, sb

        def P3(hd):
            x = ctx.pop(hd)
            vb, ptm, sb, qt, gs = x["vb"], x["ptm"], x["sb"], x["qt"], x["gs"]
            bo = self.bank()
            for tb in range(NTB):
                o = self.pb(bo, tb * 128, 128)

                def fn(e, tb=tb, o=o):
                    e.matmul(o.ap, vb.ap[:, tb * 128:(tb + 1) * 128], ptm.ap[:, tb * 128:(tb + 1) * 128],
                             start=True, stop=False)
                    ins = None
                    for j in range(4):
                        ch = tb * 4 + j
                        ins = e.matmul(self.t_ps[:, bo, ch * 32:(ch + 1) * 32], sb.ap[:, ch * 128:(ch + 1) * 128],
                                       qt.ap[:, ch * 32:(ch + 1) * 32], start=False, stop=(j == 3))
                    return ins
                S.op("pe", fn, reads=vb.bufs + ptm.bufs + sb.bufs + qt.bufs, writes=o.bufs)
            po = self.pb(bo)
            sqs = self.rstd_a([po])
            yield
            rs = self.rstd_b(sqs, 128, 4, 5)
            t1 = self.f(4)
            S.op("dve", lambda e: e.scalar_tensor_tensor(out=t1.ap, in0=po.ap, scalar=self.prm(c.pON), in1=rs.ap,
                                                         op0=ALU.mult, op1=ALU.mult),
                 reads=po.bufs + rs.bufs + [self.b_prm], writes=t1.bufs)
            self.rel(bo)
            on = self.ar(hd)
            S.op("dve", lambda e: e.tensor_tensor(out=on.ap, in0=t1.ap, in1=gs.ap, op=ALU.mult),
                 reads=t1.bufs + gs.bufs, writes=on.bufs)

        X1(0)
        yield
        X3(0)
        yield
        X4(0)
        yield
        for hd in range(c.HGH):
            nxt = hd + 1 < c.HGH
            if nxt:
                X1(hd + 1)
                yield
            X2(hd)
            yield
            if nxt:
                X3(hd + 1)
                yield
                X4(hd + 1)
                yield
            yield from P3(hd)
            yield
        yield from self.add_proj(c.iWOUT, [self.ar(i) for i in range(c.HGH)])

    def rope_a(self, pk, tmp_plane):
        xb = self.ar(tmp_plane)
        self.S.op("act", lambda e: e.activation(out=xb.ap, in_=pk.ap, func=AF.Copy), reads=pk.bufs, writes=xb.bufs)
        return xb

    def rope_b(self, pk, xb, dst):
        c = self.cfg
        S = self.S
        T = c.T
        bs = self.bank()
        ps_ = self.pb(bs)
        self.mm(ps_, [(self.cbv(c.cbSW, 128), xb.ap)], xb.bufs + [self.b_cb])
        t1 = self.f(4)
        S.op("dve", lambda e: e.tensor_tensor(out=t1.ap, in0=pk.ap, in1=self.t_rope[:, 0:T], op=ALU.mult),
             reads=pk.bufs + [self.b_rope], writes=t1.bufs)
        t2 = self.f(5)
        S.op("dve", lambda e: e.tensor_tensor(out=t2.ap, in0=ps_.ap, in1=self.t_rope[:, T:2 * T], op=ALU.mult),
             reads=ps_.bufs + [self.b_rope], writes=t2.bufs)
        self.rel(bs)
        S.op("dve", lambda e: e.tensor_tensor(out=dst.ap, in0=t1.ap, in1=t2.ap, op=ALU.add),
             reads=t1.bufs + t2.bufs, writes=dst.bufs)

    def load_rope(self, tile):
        c = self.cfg
        T = c.T
        self.S.dma("sp", [lambda e: e.dma_start(out=self.t_rope[:, 0:T], in_=self.d_rope[0, :, tile * T:(tile + 1) * T]),
                          lambda e: e.dma_start(out=self.t_rope[:, T:2 * T], in_=self.d_rope[1, :, tile * T:(tile + 1) * T])],
                   self.b_rope, writes=[self.b_rope])

    def kv(self, seq, tile):
        c = self.cfg
        S = self.S
        T = c.T
        NTB = c.NTB
        per = c.SA // c.DC
        yield from self.stream_norm(6, self.xn)
        xs = [self.xn(i) for i in range(c.DC)]
        yield
        for hd in range(c.DAH):
            base = (hd % 2) * 4
            slK = self.slabA(c.iWK + hd)
            bk0, pk0 = self.proj_fm(slK, 0, xs)
            xb0 = self.rope_a(pk0, 8)
            yield
            bk1, pk1 = self.proj_fm(slK, 128, xs)
            xb1 = self.rope_a(pk1, 9)
            self.rope_b(pk0, xb0, self.ar(base))
            self.rel(bk0)
            yield
            slV = self.slabA(c.iWV + hd)
            vt = self.ar(base + 2, 2)
            for tb in range(NTB):
                if tb == 1 or NTB == 1:
                    self.rope_b(pk1, xb1, self.ar(base + 1))
                    self.rel(bk1)
                bv = self.bank()
                o = self.pb(bv, 0, 256)
                pairs = [(xs[i].ap[:, tb * 128:(tb + 1) * 128], slV.ap[:, i * per: i * per + 256]) for i in range(c.DC)]
                rd = list(slV.bufs)
                for x_ in xs:
                    rd += x_.bufs
                self.mm(o, pairs, rd)
                S.op("act", lambda e, o=o, tb=tb: e.activation(out=vt.ap[:, tb * 256:(tb + 1) * 256], in_=o.ap, func=AF.Copy),
                     reads=o.bufs, writes=vt.bufs)
                self.rel(bv)
            kd, vd = self.b_kd[seq][hd], self.b_vd[seq][hd]
            ksrc = self.ar(base, 2)
            fns = []
            for comp in range(2):
                fns.append(lambda e, comp=comp: e.dma_start(
                    out=self.d_k[seq, hd, :, comp * c.S + tile * T: comp * c.S + (tile + 1) * T],
                    in_=self.t_ar[:, (base + comp) * T:(base + comp + 1) * T]))
            S.dma("sp", fns, kd, reads=ksrc.bufs, writes=[kd])
            vdst = self.d_v[seq, hd, tile * T:(tile + 1) * T, :].rearrange("(tb p) v -> p tb v", p=128)
            vsrc = self.t_ar[:, (base + 2) * T:(base + 4) * T].rearrange("p (tb v) -> p tb v", v=256)
            S.dma("sp", [lambda e: e.dma_start(out=vdst, in_=vsrc)], vd, reads=vt.bufs, writes=[vd])
            yield

    def dattn(self, seq, tile):
        c = self.cfg
        S = self.S
        T = c.T
        NTB = c.NTB
        yield from self.stream_norm(5, self.xn)
        xs = [self.xn(i) for i in range(c.DC)]
        yield
        qpend = None
        for hd in range(c.DAH):
            slQ = self.slabA(c.iWQ + hd)
            for comp in range(2):
                bq, pq = self.proj_fm(slQ, comp * 128, xs)
                xb = self.rope_a(pq, 16 + comp)
                if qpend is not None:
                    self.rope_b(qpend[1], qpend[2], qpend[3])
                    self.rel(qpend[0])
                qpend = (bq, pq, xb, self.ar(hd * 2 + comp))
                yield
        if qpend is not None:
            self.rope_b(qpend[1], qpend[2], qpend[3])
            self.rel(qpend[0])
            qpend = None
            yield
        ntok = (tile + 1) * T
        NKB = ntok // 128
        npl = 2 * c.S // T
        K0 = 18
        V0 = 18 + npl
        P0 = V0 + npl
        ident = self.cbv(c.cbID, 128)
        ones = self.cbv(c.cbONE, 128)
        scale = 128.0 ** -0.5
        pending = None
        for hd in range(c.DAH):
            kd, vd = self.b_kd[seq][hd], self.b_vd[seq][hd]
            ks = self.ar(K0, npl)
            vs = self.ar(V0, npl)
            kdst = self.t_ar[:, K0 * T: K0 * T + 2 * c.S].rearrange("p (c t) -> p c t", c=2)[:, :, 0:ntok]
            ksrc = self.d_k[seq, hd].rearrange("p (c t) -> p c t", c=2)[:, :, 0:ntok]
            S.dma("sp", [lambda e, kdst=kdst, ksrc=ksrc: e.dma_start(out=kdst, in_=ksrc)], ks.bufs[0], reads=[kd], writes=ks.bufs)
            vdst = self.t_ar[:, V0 * T: V0 * T + NKB * 256].rearrange("p (kb v) -> p kb v", v=256)
            vsrc = self.d_v[seq, hd, 0:ntok, :].rearrange("(kb p) v -> p kb v", p=128)
            S.dma("sp", [lambda e, vdst=vdst, vsrc=vsrc: e.dma_start(out=vdst, in_=vsrc)], vs.bufs[0], reads=[vd], writes=vs.bufs)
            FO = 8 * (hd % 2)
            od = [self.f(FO), self.f(FO + 1)]
            for comp in range(2):
                qt = self.ar(hd * 2 + comp)
                bo0, bo1, bl = self.bank(), self.bank(), self.bank()
                po0, po1, pl_ = self.pb(bo0), self.pb(bo1), self.pb(bl)
                def qk2(kb0, it):
                    nk = min(2, NKB - kb0)
                    bs = self.bank()
                    ps2 = self.pb(bs, 0, nk * T)
                    rd = ks.bufs + qt.bufs + [self.b_cb]

                    def fn(e):
                        ins = None
                        for u in range(nk):
                            kb = kb0 + u
                            o = self.t_ps[:, bs, u * T:(u + 1) * T]
                            jd = kb - tile * NTB
                            ins = e.matmul(o, self.t_ar[:, K0 * T + comp * c.S + kb * 128: K0 * T + comp * c.S + (kb + 1) * 128],
                                           qt.ap, start=True, stop=(jd < 0))
                            if jd >= 0:
                                ins = e.matmul(o, ident, self.cbv(c.cbMASK + jd * T, T), start=False, stop=True)
                        return ins
                    S.op("pe", fn, reads=rd, writes=ps2.bufs)
                    pt = self.ar(P0 + 2 * (it % 2), 2)
                    pt2 = Pl(pt.ap[:, 0:nk * T], pt.bufs)
                    S.op("act", lambda e: e.activation(out=pt2.ap, in_=ps2.ap, func=AF.Exp, scale=scale),
                         reads=ps2.bufs, writes=pt2.bufs)
                    self.rel(bs)
                    return (kb0, nk, pt2)

                def pv2(item):
                    kb0, nk, pt2 = item

                    def fn(e):
                        ins = None
                        for u in range(nk):
                            kb = kb0 + u
                            vblk = self.t_ar[:, V0 * T + kb * 256: V0 * T + (kb + 1) * 256]
                            p = pt2.ap[:, u * T:(u + 1) * T]
                            st, sp_ = (kb == 0), (kb == NKB - 1)
                            e.matmul(po0.ap, vblk[:, 0:128], p, start=st, stop=sp_)
                            e.matmul(po1.ap, vblk[:, 128:256], p, start=st, stop=sp_)
                            ins = e.matmul(pl_.ap, ones, p, start=st, stop=sp_)
                        return ins
                    S.op("pe", fn, reads=vs.bufs + pt2.bufs + [self.b_cb], writes=po0.bufs + po1.bufs + pl_.bufs)

                prev = None
                nit = (NKB + 1) // 2
                for it in range(nit + 1):
                    cur_it = qk2(2 * it, it) if it < nit else None
                    if prev is not None:
                        pv2(prev)
                    prev = cur_it
                    if comp == 0 and it == min(1, nit) and pending is not None:
                        pending()
                        pending = None
                    yield
                rl = self.f(FO + 2)
                S.op("act", lambda e: e.activation(out=rl.ap, in_=pl_.ap, func=AF.Ln), reads=pl_.bufs, writes=rl.bufs)
                self.rel(bl)
                S.op("act", lambda e: e.activation(out=rl.ap, in_=rl.ap, func=AF.Exp, scale=-1.0), reads=rl.bufs, writes=rl.bufs)
                for v, (bo, po) in enumerate(((bo0, po0), (bo1, po1))):
                    if comp == 0:
                        S.op("dve", lambda e, po=po, v=v: e.tensor_tensor(out=od[v].ap, in0=po.ap, in1=rl.ap, op=ALU.mult),
                             reads=po.bufs + rl.bufs, writes=od[v].bufs)
                    else:
                        t = self.f(FO + 3)
                        S.op("dve", lambda e, po=po, t=t: e.tensor_tensor(out=t.ap, in0=po.ap, in1=rl.ap, op=ALU.mult),
                             reads=po.bufs + rl.bufs, writes=t.bufs)
                        S.op("dve", lambda e, t=t, v=v: e.scalar_tensor_tensor(
                            out=od[v].ap, in0=t.ap, scalar=self.sm(3), in1=od[v].ap, op0=ALU.mult, op1=ALU.add),
                            reads=t.bufs + od[v].bufs + [self.b_sm], writes=od[v].bufs)
                    self.rel(bo)
            def epi(hd=hd, od=od):
                rs = self.rstd(od, 256, 4, 5)
                for v in range(2):
                    on = self.xn(hd * 2 + v)
                    S.op("dve", lambda e, on=on, v=v: e.scalar_tensor_tensor(
                        out=on.ap, in0=od[v].ap, scalar=self.sm(4 + v), in1=rs.ap, op0=ALU.mult, op1=ALU.mult),
                        reads=od[v].bufs + rs.bufs + [self.b_sm], writes=on.bufs)
            pending = epi
            yield
        if pending is not None:
            pending()
            yield
        yield from self.add_proj(c.iDWOUT, [self.xn(i) for i in range(c.DC)])

    def setup(self):
        c = self.cfg
        S = self.S
        S.dma("sp", [lambda e: e.dma_start(out=self.t_prm[:, :], in_=self.d_prm)], self.b_prm, writes=[self.b_prm])
        S.dma("sp", [lambda e: e.dma_start(out=self.t_cf[:, :], in_=self.d_cf)], self.b_cf, writes=[self.b_cf])
        S.dma("pool", [lambda e: e.dma_start(out=self.t_cb[:, :], in_=self.d_cb)], self.b_cb, writes=[self.b_cb])
        sm = self.sm
        H = c.HGH
        S.op("dve", lambda e: e.memset(self.t_sm[:, :], 0.0), writes=[self.b_sm])
        S.op("dve", lambda e: e.memset(sm(0), EPS), reads=[self.b_sm], writes=[self.b_sm])
        S.op("dve", lambda e: e.memset(sm(2), 1.0), reads=[self.b_sm], writes=[self.b_sm])
        S.op("dve", lambda e: e.tensor_tensor(out=sm(8, H), in0=self.prm(c.pLB0, H), in1=self.prm(c.pLB1, H), op=ALU.subtract),
             reads=[self.b_prm, self.b_sm], writes=[self.b_sm])
        S.op("act", lambda e: e.activation(out=sm(8, H), in_=sm(8, H), func=AF.Sigmoid), reads=[self.b_sm], writes=[self.b_sm])
        S.op("dve", lambda e: e.tensor_scalar(out=sm(8 + H, H), in0=sm(8, H), scalar1=-1.0, scalar2=1.0, op0=ALU.mult, op1=ALU.add),
             reads=[self.b_sm], writes=[self.b_sm])
        S.op("dve", lambda e: e.tensor_scalar(out=sm(8 + 2 * H, H), in0=sm(8 + H, H), scalar1=-1.0, scalar2=None, op0=ALU.mult),
             reads=[self.b_sm], writes=[self.b_sm])
        L = c.pLAM
        S.op("dve", lambda e: e.tensor_tensor(out=self.t_lt[:, 0:128], in0=self.prm(L, 128), in1=self.prm(L + 128, 128), op=ALU.mult),
             reads=[self.b_prm], writes=[self.b_lt])
        S.op("dve", lambda e: e.tensor_tensor(out=self.t_lt[:, 128:256], in0=self.prm(L + 256, 128), in1=self.prm(L + 384, 128), op=ALU.mult),
             reads=[self.b_prm, self.b_lt], writes=[self.b_lt])
        S.op("dve", lambda e: e.reduce_sum(out=sm(6, 2), in_=self.t_lt[:, :].rearrange("p (a b) -> p a b", a=2), axis=AX.X),
             reads=[self.b_lt, self.b_sm], writes=[self.b_sm])
        S.op("act", lambda e: e.activation(out=sm(6, 2), in_=sm(6, 2), func=AF.Exp), reads=[self.b_sm], writes=[self.b_sm])
        S.op("dve", lambda e: e.tensor_tensor(out=sm(1), in0=sm(6), in1=sm(7), op=ALU.subtract), reads=[self.b_sm], writes=[self.b_sm])
        S.op("dve", lambda e: e.tensor_scalar(out=sm(3), in0=sm(1), scalar1=c.lam_init, scalar2=-1.0, op0=ALU.add, op1=ALU.mult),
             reads=[self.b_sm], writes=[self.b_sm])
        S.op("dve", lambda e: e.tensor_scalar(out=sm(4, 2), in0=self.prm(c.pSUB, 2), scalar1=1.0 - c.lam_init, scalar2=None, op0=ALU.mult),
             reads=[self.b_sm, self.b_prm], writes=[self.b_sm])

    def final(self, seq, tile):
        c = self.cfg
        T = c.T
        outp = self.arf(0, c.DC)
        yield from self.stream_norm(9, lambda i: Pl(self.t_ar[:, 2 * i * T:(2 * i + 2) * T].bitcast(F32), self.b_ar[2 * i:2 * i + 2]))
        src = self.t_ar[:, 0:2 * c.DC * T].bitcast(F32)
        self.S.dma("sp", [lambda e: e.dma_start(out=self.d_out[seq, tile], in_=src)], self.cur.b_out, reads=outp.bufs, writes=[self.cur.b_out])

    def seq_gen(self, st):
        c = self.cfg
        S = self.S
        seq = st.seq
        for tile in range(c.NT):
            st.tile = tile
            hall = Pl(self.t_h[:, :], self.b_h)
            S.dma("sp", [lambda e: e.dma_start(out=self.t_h[:, :], in_=self.d_x[seq, tile])],
                  self.b_h[0], writes=hall.bufs)
            self.load_rope(tile)
            stg = self.stages
            if "ffn00" in stg: yield from self.ffn(0, 0)
            if "hgrn" in stg: yield from self.hgrn(tile == 0)
            if "ffn01" in stg: yield from self.ffn(0, 1)
            if "ple0" in stg: yield from self.ple(0, seq, tile)
            if "kv" in stg: yield from self.kv(seq, tile)
            if "ffn10" in stg: yield from self.ffn(1, 0)
            if "dattn" in stg: yield from self.dattn(seq, tile)
            if "ffn11" in stg: yield from self.ffn(1, 1)
            if "ple1" in stg: yield from self.ple(1, seq, tile)
            yield from self.final(seq, tile)
            yield

    def build(self):
        c = self.cfg
        self.setup()
        gens = [self.seq_gen(st) for st in self.streams]
        alive = [True] * len(gens)
        steps = [0] * len(gens)
        while any(alive):
            for i, g in enumerate(gens):
                if not alive[i]:
                    continue
                if i > 0 and alive[i - 1] and steps[i - 1] - steps[i] < c.LAG:
                    continue
                self.cur = self.streams[i]
                try:
                    next(g)
                    steps[i] += 1
                except StopIteration:
                    alive[i] = False
        self.S.final_wait("sp", [st.b_out for st in self.streams])
        return self.nc


_CACHE = {}


def kernel(**inputs):
    cfg = Cfg()
    n = 8
    x = np.asarray(inputs["x"], np.float32)
    p = np.asarray(inputs["p"], np.float32)
    inp = {k: np.asarray(v) for k, v in inputs.items()}
    wA, wB = host_weights(inp, cfg)
    prm = host_params(inp, cfg)
    cb, cf, rope = host_consts(cfg)
    nc = Prog(cfg).build()
    in_maps = []
    for i in range(n):
        xs = x[i * cfg.NSEQ:(i + 1) * cfg.NSEQ]
        ps = p[:, i * cfg.NSEQ:(i + 1) * cfg.NSEQ]
        xT, pT = host_acts(xs, ps, cfg)
        in_maps.append({"xT": xT.reshape(cfg.NSEQ, cfg.NT, 128, -1), "pT": pT.reshape(2, cfg.NSEQ, cfg.NT, 128, -1),
                        "wA": wA, "wB": wB, "prm": prm, "cb": cb, "cf": cf, "rope": rope})
    res = run_bass_kernel_spmd(nc, in_maps, core_ids=list(range(n)))
    outs = []
    for i in range(n):
        oT = np.asarray(res.results[i]["outT"]).reshape(cfg.NSEQ, cfg.NT, 128, cfg.DC, cfg.T)
        outs.append(host_out(oT, cfg))
    return np.concatenate(outs, axis=0).astype(np.float32)
```

```python
import math
import numpy as np
import concourse.bass as bass
import concourse.mybir as mybir
from concourse.bass_utils import run_bass_kernel_spmd

F32 = mybir.dt.float32
BF16 = mybir.dt.bfloat16
AF = mybir.ActivationFunctionType
ALU = mybir.AluOpType
AX = mybir.AxisListType

EPS = 1e-6
ROPE_THETA = 10000.0
NEG = -30000.0


ALL_STAGES = ("ffn00", "hgrn", "ffn01", "ple0", "kv", "ffn10", "dattn", "ffn11", "ple1")


class Cfg:
    def __init__(self, D=2048, DFF=5632, S=2048, T=256, NSEQ=2, PLE=256, NSLOT=5, LAG=2):
        self.D, self.DFF, self.S, self.T, self.NSEQ, self.PLE, self.NSLOT = D, DFF, S, T, NSEQ, PLE, NSLOT
        self.LAG = LAG
        self.DC = D // 128
        self.FB = DFF // 128
        self.HGH = D // 128
        self.DAH = D // 256
        self.NT = S // T
        self.NTB = T // 128
        self.NCH = T // 32
        self.PC = PLE // 128
        assert self.PC * D == self.DC * 256
        self.SA = self.DC * 256
        self.FH = self.FB // 2
        assert self.FH * 2 == self.FB
        self.SB = self.FH * 128
        self.SLOTE = max(self.SA, self.SB)
        self.NJ = self.DC // 2
        o = 0
        self.iGU = o; o += 4 * self.FB
        self.iWINA = o; o += self.HGH
        self.iWINB = o; o += self.HGH
        self.iWOUT = o; o += self.NJ
        self.iWK = o; o += self.DAH
        self.iWV = o; o += self.DAH
        self.iWQ = o; o += self.DAH
        self.iDWOUT = o; o += self.NJ
        self.iPLEG = o; o += 2 * self.NJ
        self.iPLEP = o; o += 2
        self.NA = o
        self.NB = 4 * self.DC * 2
        c = 0
        self.pG = c; c += 10 * self.DC
        self.pLB0 = c; c += self.HGH
        self.pLB1 = c; c += self.HGH
        self.pON = c; c += 1
        self.pSUB = c; c += 2
        self.pLAM = c; c += 512
        self.NP = c
        self.cbID = 0
        self.cbONE = 128
        self.cbSW = 256
        self.cbMASK = 384
        self.cbM4 = 384 + self.NTB * T
        self.NCB = 384 + self.NTB * T + 4
        self.cfID = 0
        self.cfBD = 128
        self.cfM4 = 256
        self.cfSCAN = 260
        self.NCF = 260 + T
        self.NAR = max(self.FB, 22 + 2 * (2 * S // T), 2 * self.DC, 46)
        self.lam_init = 0.8 - 0.6 * float(np.exp(-0.3 * 1))


def _slabA(W, cols, DC):
    sub = W[:, cols]
    return sub.reshape(DC, 128, -1).transpose(1, 0, 2).reshape(128, -1)


def host_weights(inp, cfg):
    D, DC, DFF, FB = cfg.D, cfg.DC, cfg.DFF, cfg.FB
    wA = np.empty((cfg.NA, 128, cfg.SA), np.float32)
    wB = np.empty((cfg.NB, 128, cfg.SB), np.float32)
    ar = np.arange
    for l in range(2):
        for f in range(2):
            Wgu = inp["ffn_w_gate_up"][l, f]
            for fb in range(FB):
                cols = np.concatenate([ar(fb * 128, fb * 128 + 128), DFF + ar(fb * 128, fb * 128 + 128)])
                wA[cfg.iGU + (l * 2 + f) * FB + fb] = _slabA(Wgu, cols, DC)
            Wd = inp["ffn_w_down"][l, f]
            for ob in range(DC):
                sub = Wd[:, ob * 128:(ob + 1) * 128].reshape(2, cfg.FH, 128, 128)
                for hf in range(2):
                    wB[((l * 2 + f) * DC + ob) * 2 + hf] = sub[hf].transpose(1, 0, 2).reshape(128, -1)
    Win = inp["hgrn_w_in"][0]
    for hd in range(cfg.HGH):
        r = ar(hd * 128, hd * 128 + 128)
        wA[cfg.iWINA + hd] = _slabA(Win, np.concatenate([r, D + r]), DC)
        wA[cfg.iWINB + hd] = _slabA(Win, np.concatenate([2 * D + r, 3 * D + r]), DC)
    for j in range(cfg.NJ):
        r = ar(j * 256, j * 256 + 256)
        wA[cfg.iWOUT + j] = _slabA(inp["hgrn_w_out"][0], r, DC)
        wA[cfg.iDWOUT + j] = _slabA(inp["diff_w_out"][0], r, DC)
        for l in range(2):
            wA[cfg.iPLEG + l * cfg.NJ + j] = _slabA(inp["ple_w_gate"][l], r, DC)
    for hd in range(cfg.DAH):
        r = ar(hd * 256, hd * 256 + 256)
        wA[cfg.iWK + hd] = _slabA(inp["w_kv"], r, DC)
        wA[cfg.iWV + hd] = _slabA(inp["w_kv"], D + r, DC)
        wA[cfg.iWQ + hd] = _slabA(inp["diff_w_q"][0], r, DC)
    for l in range(2):
        Wp = inp["ple_w_proj"][l]
        wA[cfg.iPLEP + l] = Wp.reshape(cfg.PC, 128, D).transpose(1, 0, 2).reshape(128, -1)
    return wA, wB


def host_params(inp, cfg):
    DC = cfg.DC
    prm = np.zeros((128, cfg.NP), np.float32)

    def fm(v):
        return np.asarray(v, np.float32).reshape(-1, 128).T

    gl = [inp["ffn_norm"][0, 0], inp["ffn_norm"][0, 1], inp["ffn_norm"][1, 0], inp["ffn_norm"][1, 1],
          inp["mix_norm"][0], inp["mix_norm"][1], inp["kv_norm"], inp["ple_norm"][0], inp["ple_norm"][1],
          inp["final_norm"]]
    for i, g in enumerate(gl):
        prm[:, cfg.pG + i * DC: cfg.pG + (i + 1) * DC] = fm(g)
    prm[:, cfg.pLB0:cfg.pLB0 + cfg.HGH] = fm(inp["hgrn_lower_bounds"][0])
    prm[:, cfg.pLB1:cfg.pLB1 + cfg.HGH] = fm(inp["hgrn_lower_bounds"][1])
    prm[:, cfg.pON] = np.asarray(inp["hgrn_out_norm"][0], np.float32)
    prm[:, cfg.pSUB:cfg.pSUB + 2] = fm(inp["diff_subln"][0])
    prm[:, cfg.pLAM:cfg.pLAM + 512] = np.broadcast_to(
        np.asarray(inp["diff_lambda"][0], np.float32).reshape(1, 512), (128, 512))
    return prm


def host_consts(cfg):
    T, S = cfg.T, cfg.S
    cb = np.zeros((128, cfg.NCB), np.float32)
    cb[:, cfg.cbID:cfg.cbID + 128] = np.eye(128)
    cb[:, cfg.cbONE:cfg.cbONE + 128] = 1.0
    sw = np.zeros((128, 128), np.float32)
    for d in range(128):
        sw[d, (d + 64) % 128] = 1.0
    cb[:, cfg.cbSW:cfg.cbSW + 128] = sw
    p = np.arange(128)[:, None]
    q = np.arange(T)[None, :]
    for j in range(cfg.NTB):
        cb[:, cfg.cbMASK + j * T: cfg.cbMASK + (j + 1) * T] = np.where(j * 128 + p <= q, 0.0, NEG)
    cb[:, cfg.cbM4:cfg.cbM4 + 4] = (np.arange(128)[:, None] // 32 == np.arange(4)[None, :]).astype(np.float32)
    cf = np.zeros((128, cfg.NCF), np.float32)
    cf[:, cfg.cfID:cfg.cfID + 128] = np.eye(128)
    s = np.arange(128)[:, None]
    t = np.arange(128)[None, :]
    cf[:, cfg.cfBD:cfg.cfBD + 128] = ((s <= t) & (s // 32 == t // 32)).astype(np.float32)
    cf[:, cfg.cfM4:cfg.cfM4 + 4] = (s // 32 == np.arange(4)[None, :]).astype(np.float32)
    cf[:, cfg.cfSCAN:cfg.cfSCAN + T] = (np.arange(T) % 32 != 0).astype(np.float32)[None, :]
    inv = (ROPE_THETA ** (-np.arange(0, 128, 2, dtype=np.float32) / 128)).astype(np.float32)
    ang = np.arange(S, dtype=np.float32)[None, :] * np.concatenate([inv, inv])[:, None]
    rope = np.stack([np.cos(ang), np.sin(ang) * np.where(np.arange(128) < 64, -1.0, 1.0)[:, None]]).astype(np.float32)
    return cb, cf, rope


def host_acts(x, p, cfg):
    NS, NT, T, DC, PC = cfg.NSEQ, cfg.NT, cfg.T, cfg.DC, cfg.PC
    xT = np.ascontiguousarray(x.reshape(NS, NT, T, DC, 128).transpose(0, 1, 4, 3, 2))
    pT = np.ascontiguousarray(p.reshape(2, NS, NT, T, PC, 128).transpose(0, 1, 2, 5, 4, 3))
    return xT, pT


def host_out(oT, cfg):
    return np.ascontiguousarray(oT.transpose(0, 1, 4, 3, 2)).reshape(cfg.NSEQ, cfg.S, cfg.D)


class Buf:
    __slots__ = ("name", "w", "r", "dsem", "dcnt", "excl")

    def __init__(self, name, excl=False):
        self.name = name
        self.excl = excl
        self.w = {}
        self.r = {}
        self.dsem = None
        self.dcnt = 0


class Sched:
    def __init__(self, nc):
        self.nc = nc
        self.engs = {"pe": nc.tensor, "act": nc.scalar, "dve": nc.vector, "pool": nc.gpsimd, "sp": nc.sync}
        self.csem = {k: nc.alloc_semaphore("cs_" + k) for k in self.engs}
        self.cnt = {k: 0 for k in self.engs}
        self.seen = {k: {} for k in self.engs}
        self.nsem = 0

    def new_dsem(self, buf):
        buf.dsem = self.nc.alloc_semaphore("ds_%d_%s" % (self.nsem, buf.name))
        self.nsem += 1

    def _wait(self, e, reads, writes):
        need = {}
        own = self.csem[e]
        for b in reads:
            for s, v in b.w.items():
                if need.get(s, 0) < v:
                    need[s] = v
            if b.excl:
                for s, v in b.r.items():
                    if s is not own and s != own and need.get(s, 0) < v:
                        need[s] = v
        for b in writes:
            for s, v in b.w.items():
                if need.get(s, 0) < v:
                    need[s] = v
            for s, v in b.r.items():
                if need.get(s, 0) < v:
                    need[s] = v
        seen = self.seen[e]
        for s, v in need.items():
            if seen.get(s, 0) >= v:
                continue
            self.engs[e].wait_ge(s, v)
            seen[s] = v

    def _mark(self, s, v, reads, writes):
        for b in reads:
            if b.r.get(s, 0) < v:
                b.r[s] = v
        for b in writes:
            if b.w.get(s, 0) < v:
                b.w[s] = v

    def op(self, e, fn, reads=(), writes=()):
        self._wait(e, reads, writes)
        ins = fn(self.engs[e])
        self.cnt[e] += 1
        s = self.csem[e]
        ins.then_inc(s, 1)
        self._mark(s, self.cnt[e], reads, writes)

    def dma(self, q, fns, owner, reads=(), writes=()):
        self._wait(q, reads, writes)
        if owner.dsem is None:
            self.new_dsem(owner)
        for fn in fns:
            fn(self.engs[q]).then_inc(owner.dsem, 16)
        owner.dcnt += 16 * len(fns)
        self._mark(owner.dsem, owner.dcnt, reads, writes)

    def final_wait(self, e, bufs):
        self._wait(e, bufs, bufs)


class Pl:
    __slots__ = ("ap", "bufs")

    def __init__(self, ap, bufs):
        self.ap = ap
        self.bufs = bufs


class Stream:
    def __init__(self, nc, c, i, NF):
        A = nc.alloc_sbuf_tensor
        T = c.T
        n = "_s%d" % i
        self.seq = i
        self.t_h = A("h" + n, [128, c.DC * T], F32)
        self.t_xn = A("xn" + n, [128, c.DC * T], BF16)
        self.t_ar = A("arena" + n, [128, c.NAR * T], BF16)
        self.t_sh = A("sh" + n, [128, (c.NCH + 1) * 128], F32)
        self.t_carry = A("carry" + n, [128, c.HGH * 128], F32)
        self.t_f = A("ftmp" + n, [128, NF * T], F32)
        self.t_rope = A("ropes" + n, [128, 2 * T], F32)
        self.t_pb = A("pb" + n, [128, c.PC * T], BF16)
        self.t_sq = A("sqtmp" + n, [128, 2 * T], BF16)
        B = Buf
        self.b_h = [B("h%d" % j + n) for j in range(c.DC)]
        self.b_xn = [B("xn%d" % j + n) for j in range(c.DC)]
        self.b_ar = [B("ar%d" % j + n) for j in range(c.NAR)]
        self.b_sh = [B("sh%d" % j + n) for j in range(c.NCH + 1)]
        self.b_carry = [B("carry%d" % j + n) for j in range(c.HGH)]
        self.b_f = [B("f%d" % j + n) for j in range(NF)]
        self.b_rope = B("rope" + n)
        self.b_pb = B("pb" + n)
        self.b_sq = [B("sq0" + n), B("sq1" + n)]
        self.sq_i = 0
        self.tile = 0
        self.b_out = B("out" + n)


class Prog:
    def __init__(self, cfg):
        self.cfg = cfg
        c = cfg
        nc = bass.Bass("TRN2", target_bir_lowering=False)
        self.nc = nc
        T = c.T
        self.d_x = nc.dram_tensor("xT", [c.NSEQ, c.NT, 128, c.DC * T], F32, kind="ExternalInput").ap()
        self.d_p = nc.dram_tensor("pT", [2, c.NSEQ, c.NT, 128, c.PC * T], F32, kind="ExternalInput").ap()
        self.d_wA = nc.dram_tensor("wA", [c.NA, 128, c.SA], F32, kind="ExternalInput").ap()
        self.d_wB = nc.dram_tensor("wB", [c.NB, 128, c.SB], F32, kind="ExternalInput").ap()
        self.d_prm = nc.dram_tensor("prm", [128, c.NP], F32, kind="ExternalInput").ap()
        self.d_cb = nc.dram_tensor("cb", [128, c.NCB], F32, kind="ExternalInput").ap()
        self.d_cf = nc.dram_tensor("cf", [128, c.NCF], F32, kind="ExternalInput").ap()
        self.d_rope = nc.dram_tensor("rope", [2, 128, c.S], F32, kind="ExternalInput").ap()
        self.d_out = nc.dram_tensor("outT", [c.NSEQ, c.NT, 128, c.DC * T], F32, kind="ExternalOutput").ap()
        self.d_k = nc.dram_tensor("kscr", [c.NSEQ, c.DAH, 128, 2 * c.S], BF16).ap()
        self.d_v = nc.dram_tensor("vscr", [c.NSEQ, c.DAH, c.S, 256], BF16).ap()
        self.NAH = (c.NA + 2) // 3
        self.d_wAb = [nc.dram_tensor("wAb%d" % i, [self.NAH, 128, c.SA], BF16).ap() for i in range(3)]
        self.d_wBb = nc.dram_tensor("wBb", [c.NB, 128, c.SB], BF16).ap()
        A = nc.alloc_sbuf_tensor
        self.NF = 12
        self.streams = [Stream(nc, c, i, self.NF) for i in range(c.NSEQ)]
        self.cur = self.streams[0]
        self.t_slot = [A("slot%d" % i, [128, c.SLOTE], BF16) for i in range(c.NSLOT)]
        self.t_cb = A("cbs", [128, c.NCB], BF16)
        self.t_cf = A("cfs", [128, c.NCF], F32)
        self.t_prm = A("prms", [128, c.NP], F32)
        self.t_sm = A("small", [128, 64], F32)
        self.t_lt = A("lamtmp", [128, 256], F32)
        self.t_ps = nc.alloc_psum_tensor("ps", [128, 8, 512], F32)
        self.WP0 = c.NAR - (c.SA + T - 1) // T
        B = Buf
        self.b_slot = [B("slot%d" % i) for i in range(c.NSLOT)]
        self.b_cb = B("cb")
        self.b_cf = B("cf")
        self.b_prm = B("prm")
        self.b_sm = B("sm")
        self.b_lt = B("lt")
        self.b_ps = [B("ps%d" % i, excl=True) for i in range(8)]
        self.b_kd = [[B("kd%d_%d" % (s, h)) for h in range(c.DAH)] for s in range(c.NSEQ)]
        self.b_vd = [[B("vd%d_%d" % (s, h)) for h in range(c.DAH)] for s in range(c.NSEQ)]
        self.slab_cache = {}
        self.slot_key = [None] * c.NSLOT
        self.slot_dirty = [None] * c.NSLOT
        self.b_sst = [B("sst%d" % i) for i in range(c.NSLOT)]
        self.b_wsc = {}
        self.stored = set()
        self.wp_key = None
        self.S = Sched(nc)
        self.free_banks = list(range(8))
        self.slot_i = 0
        self.stages = ALL_STAGES

    t_h = property(lambda self: self.cur.t_h)
    t_xn = property(lambda self: self.cur.t_xn)
    t_ar = property(lambda self: self.cur.t_ar)
    t_f = property(lambda self: self.cur.t_f)
    t_sh = property(lambda self: self.cur.t_sh)
    t_carry = property(lambda self: self.cur.t_carry)
    t_rope = property(lambda self: self.cur.t_rope)
    t_pb = property(lambda self: self.cur.t_pb)
    t_sq = property(lambda self: self.cur.t_sq)
    b_h = property(lambda self: self.cur.b_h)
    b_xn = property(lambda self: self.cur.b_xn)
    b_ar = property(lambda self: self.cur.b_ar)
    b_f = property(lambda self: self.cur.b_f)
    b_sh = property(lambda self: self.cur.b_sh)
    b_carry = property(lambda self: self.cur.b_carry)
    b_rope = property(lambda self: self.cur.b_rope)
    b_pb = property(lambda self: self.cur.b_pb)
    b_sq = property(lambda self: self.cur.b_sq)

    def h(self, i):
        T = self.cfg.T
        return Pl(self.t_h[:, i * T:(i + 1) * T], [self.b_h[i]])

    def xn(self, i):
        T = self.cfg.T
        return Pl(self.t_xn[:, i * T:(i + 1) * T], [self.b_xn[i]])

    def ar(self, i, n=1):
        T = self.cfg.T
        return Pl(self.t_ar[:, i * T:(i + n) * T], self.b_ar[i:i + n])

    def arf(self, i, n=1):
        T = self.cfg.T
        return Pl(self.t_ar[:, i * T:(i + 2 * n) * T].bitcast(F32), self.b_ar[i:i + 2 * n])

    def f(self, i):
        T = self.cfg.T
        return Pl(self.t_f[:, i * T:(i + 1) * T], [self.b_f[i]])

    def prm(self, col, n=1):
        return self.t_prm[:, col:col + n]

    def cbv(self, col, n):
        return self.t_cb[:, col:col + n]

    def cfv(self, col, n):
        return self.t_cf[:, col:col + n]

    def sm(self, col, n=1):
        return self.t_sm[:, col:col + n]

    def bank(self):
        i = self.free_banks.pop(0)
        return i

    def rel(self, i):
        self.free_banks.append(i)

    def pb(self, i, lo=0, n=None):
        n = self.cfg.T if n is None else n
        return Pl(self.t_ps[:, i, lo:lo + n], [self.b_ps[i]])

    def _slab(self, key, src, dst, n):
        c = self.cfg
        k = self.slab_cache.get(key)
        if k is not None and self.slot_key[k] == key:
            return Pl(self.t_slot[k], [self.b_slot[k]])
        k = self.slot_i
        self.slot_i = (k + 1) % c.NSLOT
        t, b = self.t_slot[k], self.b_slot[k]
        if self.slot_dirty[k] is not None:
            dkey, ddst, dn = self.slot_dirty[k]
            wb = self.b_wsc.setdefault(dkey, Buf("wsc%s%d" % dkey))
            self.S.dma("sp", [lambda e: e.dma_start(out=ddst, in_=t[:, 0:dn])], self.b_sst[k], reads=[b], writes=[wb])
            self.stored.add(dkey)
            self.slot_dirty[k] = None
        self.slot_key[k] = key
        self.slab_cache[key] = k
        if key in self.stored:
            wb = self.b_wsc[key]
            self.S.dma("pool", [lambda e: e.dma_start(out=t[:, 0:n], in_=dst)], b, reads=[wb], writes=[b])
        else:
            self.S.dma("pool", [lambda e: e.dma_start(out=t[:, 0:n], in_=src)], b, writes=[b])
            self.slot_dirty[k] = (key, dst, n)
        return Pl(t, [b])

    def slabA(self, idx):
        return self._slab(("A", idx), self.d_wA[idx], self.d_wAb[idx // self.NAH][idx % self.NAH], self.cfg.SA)

    def slabB(self, idx):
        return self._slab(("B", idx), self.d_wB[idx], self.d_wBb[idx], self.cfg.SB)

    def tile_of_cur(self):
        return self.cur.tile

    def wproj(self, l):
        c = self.cfg
        T = c.T
        s0 = self.streams[0]
        n = (c.SA + T - 1) // T
        ap = s0.t_ar[:, self.WP0 * T: self.WP0 * T + c.SA]
        bufs = s0.b_ar[self.WP0:self.WP0 + n]
        key = (l, self.cur.tile)
        if self.wp_key != key:
            self.wp_key = key
            src = self.d_wA[c.iPLEP + l]
            self.S.dma("pool", [lambda e: e.dma_start(out=ap, in_=src)], bufs[0], writes=bufs)
        return Pl(ap, bufs)

    def mm(self, out, pairs, reads):
        def fn(e):
            n = len(pairs)
            ins = None
            for i, (l, r) in enumerate(pairs):
                ins = e.matmul(out.ap, l, r, start=(i == 0), stop=(i == n - 1))
            return ins
        self.S.op("pe", fn, reads=reads, writes=out.bufs)

    def rstd_a(self, srcs, dsts=None):
        c = self.cfg
        T = c.T
        S = self.S
        n = len(srcs)
        out = []
        for i, s in enumerate(srcs):
            if dsts is None:
                k = self.cur.sq_i
                self.cur.sq_i ^= 1
                sq = Pl(self.t_sq[:, k * T:(k + 1) * T], [self.b_sq[k]])
            else:
                sq = dsts[i]
            if n > 2 and i % 2 == 1:
                S.op("dve", lambda e, s=s, sq=sq: e.tensor_tensor(out=sq.ap, in0=s.ap, in1=s.ap, op=ALU.mult),
                     reads=s.bufs, writes=sq.bufs)
            else:
                S.op("act", lambda e, s=s, sq=sq: e.activation(out=sq.ap, in_=s.ap, func=AF.Square),
                     reads=s.bufs, writes=sq.bufs)
            out.append(sq)
        return out

    def rstd_b(self, sqs, dim, fa, fb):
        c = self.cfg
        S = self.S
        bk = self.bank()
        bp = self.pb(bk)
        ones = self.cbv(c.cbONE, 128)
        n = len(sqs)
        for i, sq in enumerate(sqs):
            S.op("pe", lambda e, sq=sq, i=i: e.matmul(bp.ap, ones, sq.ap, start=(i == 0), stop=(i == n - 1)),
                 reads=sq.bufs + [self.b_cb], writes=bp.bufs)
        sd = self.f(fa)
        S.op("act", lambda e: e.activation(out=sd.ap, in_=bp.ap, func=AF.Ln, bias=self.sm(0), scale=1.0 / dim),
             reads=bp.bufs + [self.b_sm], writes=sd.bufs)
        self.rel(bk)
        rs = self.f(fb)
        S.op("act", lambda e: e.activation(out=rs.ap, in_=sd.ap, func=AF.Exp, scale=-0.5), reads=sd.bufs, writes=rs.bufs)
        return rs

    def rstd(self, srcs, dim, fa, fb):
        return self.rstd_b(self.rstd_a(srcs), dim, fa, fb)

    def stream_norm(self, gi, dst_fn):
        c = self.cfg
        sqs = self.rstd_a([self.h(i) for i in range(c.DC)], dsts=[self.xn(i) for i in range(c.DC)])
        yield
        rs = self.rstd_b(sqs, c.D, 6, 7)
        yield
        for i in range(c.DC):
            hp = self.h(i)
            d = dst_fn(i)
            g = self.prm(c.pG + gi * c.DC + i)
            self.S.op("dve", lambda e, hp=hp, d=d, g=g: e.scalar_tensor_tensor(
                out=d.ap, in0=hp.ap, scalar=g, in1=rs.ap, op0=ALU.mult, op1=ALU.mult),
                reads=hp.bufs + rs.bufs + [self.b_prm], writes=d.bufs)

    def proj_fm(self, slab, col, srcs, ncols=128):
        c = self.cfg
        bk = self.bank()
        out = self.pb(bk)
        W = slab.ap
        n = len(srcs)
        per = c.SA // c.DC if n == c.DC else None
        pairs = []
        rd = list(slab.bufs)
        for i, s in enumerate(srcs):
            pairs.append((W[:, i * per + col: i * per + col + ncols], s.ap))
            rd += s.bufs
        self.mm(out, pairs, rd)
        return bk, out

    def ffn(self, l, f):
        c = self.cfg
        S = self.S
        T = c.T
        yield from self.stream_norm(l * 2 + f, self.xn)
        xs = [self.xn(i) for i in range(c.DC)]
        yield
        for fb in range(c.FB):
            slab = self.slabA(c.iGU + (l * 2 + f) * c.FB + fb)
            bg, pg = self.proj_fm(slab, 0, xs)
            bu, pu = self.proj_fm(slab, 128, xs)
            sg = self.f(fb % 2)
            S.op("act", lambda e, sg=sg, pg=pg: e.activation(out=sg.ap, in_=pg.ap, func=AF.Silu),
                 reads=pg.bufs, writes=sg.bufs)
            self.rel(bg)
            a = self.ar(fb)
            S.op("dve", lambda e, a=a, sg=sg, pu=pu: e.tensor_tensor(out=a.ap, in0=sg.ap, in1=pu.ap, op=ALU.mult),
                 reads=sg.bufs + pu.bufs, writes=a.bufs)
            self.rel(bu)
            yield
        acts = [self.ar(i) for i in range(c.FB)]
        for ob in range(c.DC):
            bk = self.bank()
            out = self.pb(bk)
            for hf in range(2):
                sl = self.slabB(((l * 2 + f) * c.DC + ob) * 2 + hf)
                rd = list(sl.bufs)
                for a in acts[hf * c.FH:(hf + 1) * c.FH]:
                    rd += a.bufs

                def fn(e, sl=sl, hf=hf):
                    ins = None
                    for i in range(c.FH):
                        ins = e.matmul(out.ap, sl.ap[:, i * 128:(i + 1) * 128], acts[hf * c.FH + i].ap,
                                       start=(hf == 0 and i == 0), stop=(hf == 1 and i == c.FH - 1))
                    return ins
                S.op("pe", fn, reads=rd, writes=out.bufs)
                if hf == 0:
                    yield
            hp = self.h(ob)
            S.op("dve", lambda e, hp=hp, out=out: e.scalar_tensor_tensor(
                out=hp.ap, in0=out.ap, scalar=0.5, in1=hp.ap, op0=ALU.mult, op1=ALU.add),
                reads=out.bufs + hp.bufs, writes=hp.bufs)
            self.rel(bk)
            yield

    def add_proj(self, base, srcs):
        c = self.cfg
        for j in range(c.NJ):
            slab = self.slabA(base + j)
            for half in range(2):
                bk, out = self.proj_fm(slab, half * 128, srcs)
                hp = self.h(j * 2 + half)
                self.S.op("dve", lambda e, hp=hp, out=out: e.tensor_tensor(out=hp.ap, in0=out.ap, in1=hp.ap, op=ALU.add),
                          reads=out.bufs + hp.bufs, writes=hp.bufs)
                self.rel(bk)
            yield

    def ple(self, l, seq, tile):
        c = self.cfg
        S = self.S
        T = c.T
        yield from self.stream_norm(7 + l, self.xn)
        xs = [self.xn(i) for i in range(c.DC)]
        src = self.d_p[l, seq, tile]
        S.dma("pool", [lambda e: e.dma_start(out=self.t_pb[:, :], in_=src)], self.b_pb, writes=[self.b_pb])
        wp = self.wproj(l)
        yield
        for j in range(c.NJ):
            slab = self.slabA(c.iPLEG + l * c.NJ + j)
            for half in range(2):
                ob = j * 2 + half
                bg, pg = self.proj_fm(slab, half * 128, xs)
                bp_ = self.bank()
                pp = self.pb(bp_)
                pairs = [(wp.ap[:, pc * c.D + ob * 128: pc * c.D + ob * 128 + 128], self.t_pb[:, pc * T:(pc + 1) * T])
                         for pc in range(c.PC)]
                self.mm(pp, pairs, wp.bufs + [self.b_pb])
                sg = self.f(ob % 2)
                S.op("act", lambda e, sg=sg, pg=pg: e.activation(out=sg.ap, in_=pg.ap, func=AF.Sigmoid),
                     reads=pg.bufs, writes=sg.bufs)
                self.rel(bg)
                t2 = self.f(2 + ob % 2)
                S.op("dve", lambda e, t2=t2, sg=sg, pp=pp: e.tensor_tensor(out=t2.ap, in0=sg.ap, in1=pp.ap, op=ALU.mult),
                     reads=sg.bufs + pp.bufs, writes=t2.bufs)
                self.rel(bp_)
                hp = self.h(ob)
                S.op("dve", lambda e, hp=hp, t2=t2: e.tensor_tensor(out=hp.ap, in0=hp.ap, in1=t2.ap, op=ALU.add),
                     reads=hp.bufs + t2.bufs, writes=hp.bufs)
            yield

    def hgrn(self, first_tile):
        c = self.cfg
        S = self.S
        T = c.T
        NTB, NCH = c.NTB, c.NCH
        NF = self.NF
        yield from self.stream_norm(4, self.xn)
        xs = [self.xn(i) for i in range(c.DC)]
        yield
        per = c.SA // c.DC
        ident_f = self.cfv(c.cfID, 128)
        one = self.sm(2)
        ctx = {}

        def X1(hd):
            st = hd % 2
            PB = 16 + 15 * st
            FO = 8 * st
            slA = self.slabA(c.iWINA + hd)
            slB = self.slabA(c.iWINB + hd)
            bq, pq = self.proj_fm(slA, 0, xs)
            bf_, pf = self.proj_fm(slA, 128, xs)
            bg, pg = self.proj_fm(slB, 128, xs)
            bv = self.bank()
            for tb in range(NTB):
                out = self.pb(bv, tb * 128, 128)
                pairs = [(xs[i].ap[:, tb * 128:(tb + 1) * 128], slB.ap[:, i * per: i * per + 128]) for i in range(c.DC)]
                rd = list(slB.bufs)
                for x_ in xs:
                    rd += x_.bufs
                self.mm(out, pairs, rd)
            pv = self.pb(bv, 0, NTB * 128)
            f0, f1, f2, f3 = self.f(FO), self.f(FO + 1), self.f(FO + 2), self.f(FO + 3)
            qt = self.ar(PB)
            S.op("act", lambda e: e.activation(out=qt.ap, in_=pq.ap, func=AF.Copy, scale=128.0 ** -0.5), reads=pq.bufs, writes=qt.bufs)
            self.rel(bq)
            S.op("act", lambda e: e.activation(out=f1.ap, in_=pf.ap, func=AF.Exp, scale=-1.0), reads=pf.bufs, writes=f1.bufs)
            self.rel(bf_)
            S.op("act", lambda e: e.activation(out=f0.ap, in_=pg.ap, func=AF.Exp, scale=-1.0), reads=pg.bufs, writes=f0.bufs)
            S.op("act", lambda e: e.activation(out=f0.ap, in_=f0.ap, func=AF.Ln, bias=one), reads=f0.bufs + [self.b_sm], writes=f0.bufs)
            S.op("act", lambda e: e.activation(out=f0.ap, in_=f0.ap, func=AF.Exp, scale=-1.0), reads=f0.bufs, writes=f0.bufs)
            gs = self.ar(PB + 9)
            S.op("dve", lambda e: e.tensor_tensor(out=gs.ap, in0=pg.ap, in1=f0.ap, op=ALU.mult), reads=pg.bufs + f0.bufs, writes=gs.bufs)
            self.rel(bg)
            vb = self.ar(PB + 3)
            S.op("act", lambda e: e.activation(out=vb.ap, in_=pv.ap, func=AF.Copy), reads=pv.bufs, writes=vb.bufs)
            v4 = self.ar(PB + 4, 4)
            self.rel(bv)
            m4 = bass.AP(self.t_cb, c.cbM4, [[c.NCB, 128], [1, 4], [0, 128]])
            for tb in range(NTB):
                vin = bass.AP(self.t_ar, (PB + 3) * T + tb * 128, [[c.NAR * T, 128], [0, 4], [1, 128]])
                vout = bass.AP(self.t_ar, (PB + 4) * T + tb * 512, [[c.NAR * T, 128], [128, 4], [1, 128]])
                S.op("dve", lambda e, vin=vin, vout=vout: e.tensor_tensor(out=vout, in0=vin, in1=m4, op=ALU.mult),
                     reads=vb.bufs + [self.b_cb], writes=v4.bufs)
            S.op("act", lambda e: e.activation(out=f2.ap, in_=f1.ap, func=AF.Ln, bias=one, scale=self.sm(8 + hd)),
                 reads=f1.bufs + [self.b_sm], writes=f2.bufs)
            S.op("act", lambda e: e.activation(out=f3.ap, in_=f1.ap, func=AF.Ln, bias=one), reads=f1.bufs + [self.b_sm], writes=f3.bufs)
            S.op("dve", lambda e: e.tensor_tensor(out=f2.ap, in0=f2.ap, in1=f3.ap, op=ALU.subtract), reads=f2.bufs + f3.bufs, writes=f2.bufs)
            S.op("dve", lambda e: e.tensor_tensor_scan(out=f0.ap, data0=self.cfv(c.cfSCAN, T), data1=f2.ap,
                                                       initial=0.0, op0=ALU.mult, op1=ALU.add),
                 reads=f2.bufs + [self.b_cf], writes=f0.bufs)
            ctx[hd] = dict(PB=PB, FO=FO, qt=qt, vb=vb, v4=v4, gs=gs)

        def X3(hd):
            x = ctx[hd]
            PB, FO = x["PB"], x["FO"]
            qt = x["qt"]
            f0, f1, f2, f3 = self.f(FO), self.f(FO + 1), self.f(FO + 2), self.f(FO + 3)
            S.op("act", lambda e: e.activation(out=f3.ap, in_=f2.ap, func=AF.Exp), reads=f2.bufs, writes=f3.bufs)
            S.op("act", lambda e: e.activation(out=f1.ap, in_=f0.ap, func=AF.Exp), reads=f0.bufs, writes=f1.bufs)
            S.op("act", lambda e: e.activation(out=f2.ap, in_=f0.ap, func=AF.Exp, scale=-1.0), reads=f0.bufs, writes=f2.bufs)
            S.op("act", lambda e: e.activation(out=f3.ap, in_=f3.ap, func=AF.Identity, scale=-1.0, bias=one),
                 reads=f3.bufs + [self.b_sm], writes=f3.bufs)
            S.op("dve", lambda e: e.tensor_tensor(out=qt.ap, in0=qt.ap, in1=f1.ap, op=ALU.mult), reads=qt.bufs + f1.bufs, writes=qt.bufs)
            S.op("dve", lambda e: e.tensor_tensor(out=f3.ap, in0=f3.ap, in1=f2.ap, op=ALU.mult), reads=f3.bufs + f2.bufs, writes=f3.bufs)
            kt = self.ar(PB + 1)
            S.op("act", lambda e: e.activation(out=kt.ap, in_=f3.ap, func=AF.Copy), reads=f3.bufs, writes=kt.bufs)
            ebl = bass.AP(self.t_f, (FO + 1) * T + 31, [[NF * T, 128], [32, NCH], [0, 32]])
            kh3 = bass.AP(self.t_f, (FO + 2) * T, [[NF * T, 128], [32, NCH], [1, 32]])
            kk3 = bass.AP(self.t_f, (FO + 3) * T, [[NF * T, 128], [32, NCH], [1, 32]])
            S.op("dve", lambda e: e.tensor_tensor(out=kh3, in0=kk3, in1=ebl, op=ALU.mult),
                 reads=f3.bufs + f1.bufs + f2.bufs, writes=f2.bufs)
            x["kt"], x["eb"], x["kh"] = kt, f1, f2

        def X4(hd):
            x = ctx[hd]
            PB = x["PB"]
            kh = x["kh"]
            bt = self.bank()
            for tb in range(NTB):
                o = self.pb(bt, tb * 128, 128)
                S.op("pe", lambda e, o=o, tb=tb: e.transpose(o.ap, kh.ap[:, tb * 128:(tb + 1) * 128], ident_f),
                     reads=kh.bufs + [self.b_cf], writes=o.bufs)
            ktok = self.ar(PB + 2)
            pt_ = self.pb(bt, 0, NTB * 128)
            S.op("act", lambda e: e.activation(out=ktok.ap, in_=pt_.ap, func=AF.Copy), reads=pt_.bufs, writes=ktok.bufs)
            self.rel(bt)
            x["ktok"] = ktok

        def X2(hd):
            x = ctx[hd]
            PB, FO = x["PB"], x["FO"]
            eb, kt, qt, v4, ktok = x["eb"], x["kt"], x["qt"], x["v4"], x["ktok"]
            bus = []
            for tb in range(NTB):
                bu = self.bank()
                bus.append(bu)
                o = self.pb(bu, 0, 512)
                self.mm(o, [(ktok.ap[:, tb * 128:(tb + 1) * 128], v4.ap[:, tb * 512:(tb + 1) * 512])],
                        ktok.bufs + v4.bufs)
            ba = self.bank()
            for tb in range(NTB):
                o = self.pb(ba, tb * 128, 128)
                self.mm(o, [(kt.ap[:, tb * 128:(tb + 1) * 128], qt.ap[:, tb * 128:(tb + 1) * 128])], kt.bufs + qt.bufs)
            ptm = self.ar(PB + 8)
            pa3 = bass.AP(self.t_ps, ba * 512, [[8 * 512, 128], [128, NTB], [1, 128]])
            pt3 = bass.AP(self.t_ar, (PB + 8) * T, [[c.NAR * T, 128], [128, NTB], [1, 128]])
            bd3 = bass.AP(self.t_cf, c.cfBD, [[c.NCF, 128], [0, NTB], [1, 128]])
            S.op("dve", lambda e: e.tensor_tensor(out=pt3, in0=pa3, in1=bd3, op=ALU.mult),
                 reads=[self.b_ps[ba], self.b_cf], writes=ptm.bufs)
            self.rel(ba)
            x["ptm"] = ptm
            sh0 = Pl(self.t_sh[:, 0:128], [self.b_sh[0]])
            car = Pl(self.t_carry[:, hd * 128:(hd + 1) * 128], [self.b_carry[hd]])
            if first_tile:
                S.op("dve", lambda e: e.memset(sh0.ap, 0.0), writes=sh0.bufs)
            else:
                S.op("dve", lambda e: e.tensor_copy(out=sh0.ap, in_=car.ap), reads=car.bufs, writes=sh0.bufs)
            for ch in range(NCH):
                sp = Pl(self.t_sh[:, ch * 128:(ch + 1) * 128], [self.b_sh[ch]])
                sn = Pl(self.t_sh[:, (ch + 1) * 128:(ch + 2) * 128], [self.b_sh[ch + 1]])
                u = self.pb(bus[ch // 4], (ch % 4) * 128, 128)
                dec = self.t_f[:, (FO + 1) * T + ch * 32 + 31: (FO + 1) * T + ch * 32 + 32]
                S.op("dve", lambda e, sp=sp, sn=sn, u=u, dec=dec: e.scalar_tensor_tensor(
                    out=sn.ap, in0=sp.ap, scalar=dec, in1=u.ap, op0=ALU.mult, op1=ALU.add),
                    reads=sp.bufs + u.bufs + eb.bufs, writes=sn.bufs)
            for bu in bus:
                self.rel(bu)

        def X2b(hd):
            x = ctx[hd]
            PB = x["PB"]
            car = Pl(self.t_carry[:, hd * 128:(hd + 1) * 128], [self.b_carry[hd]])
            sb = self.ar(PB + 10, 4)
            shall = Pl(self.t_sh[:, 0:NCH * 128], self.b_sh[0:NCH])
            S.op("act", lambda e: e.activation(out=sb.ap, in_=shall.ap, func=AF.Copy), reads=shall.bufs, writes=sb.bufs)
            slast = Pl(self.t_sh[:, NCH * 128:(NCH + 1) * 128], [self.b_sh[NCH]])
            S.op("act", lambda e: e.activation(out=car.ap, in_=slast.ap, func=AF.Copy), reads=slast.bufs, writes=car.bufs)
            x["sb"] = sb

        def P3(hd):
            x = ctx.pop(hd)
            vb, ptm, sb, qt, gs = x["vb"], x["ptm"], x["sb"], x["qt"], x["gs"]
            bo = self.bank()
            for tb in range(NTB):
                o = self.pb(bo, tb * 128, 128)

                def fn(e, tb=tb, o=o):
                    e.matmul(o.ap, vb.ap[:, tb * 128:(tb + 1) * 128], ptm.ap[:, tb * 128:(tb + 1) * 128],
                             start=True, stop=False)
                    ins = None
                    for j in range(4):
                        ch = tb * 4 + j
                        ins = e.matmul(self.t_ps[:, bo, ch * 32:(ch + 1) * 32], sb.ap[:, ch * 128:(ch + 1) * 128],
                                       qt.ap[:, ch * 32:(ch + 1) * 32], start=False, stop=(j == 3))
                    return ins
                S.op("pe", fn, reads=vb.bufs + ptm.bufs + sb.bufs + qt.bufs, writes=o.bufs)
            po = self.pb(bo)
            sqs = self.rstd_a([po])
            yield
            rs = self.rstd_b(sqs, 128, 4, 5)
            t1 = self.f(4)
            S.op("dve", lambda e: e.scalar_tensor_tensor(out=t1.ap, in0=po.ap, scalar=self.prm(c.pON), in1=rs.ap,
                                                         op0=ALU.mult, op1=ALU.mult),
                 reads=po.bufs + rs.bufs + [self.b_prm], writes=t1.bufs)
            self.rel(bo)
            on = self.ar(hd)
            S.op("dve", lambda e: e.tensor_tensor(out=on.ap, in0=t1.ap, in1=gs.ap, op=ALU.mult),
                 reads=t1.bufs + gs.bufs, writes=on.bufs)

        X1(0)
        yield
        X3(0)
        yield
        X4(0)
        yield
        for hd in range(c.HGH):
            nxt = hd + 1 < c.HGH
            if nxt:
                X1(hd + 1)
                yield
            X2(hd)
            yield
            if nxt:
                X3(hd + 1)
                yield
            X2b(hd)
            yield
            if nxt:
                X4(hd + 1)
                yield
            yield from P3(hd)
            yield
        yield from self.add_proj(c.iWOUT, [self.ar(i) for i in range(c.HGH)])

    def rope_a(self, pk, tmp_plane):
        xb = self.ar(tmp_plane)
        self.S.op("act", lambda e: e.activation(out=xb.ap, in_=pk.ap, func=AF.Copy), reads=pk.bufs, writes=xb.bufs)
        return xb

    def rope_b(self, pk, xb, dst):
        c = self.cfg
        S = self.S
        T = c.T
        bs = self.bank()
        ps_ = self.pb(bs)
        self.mm(ps_, [(self.cbv(c.cbSW, 128), xb.ap)], xb.bufs + [self.b_cb])
        t1 = self.f(4)
        S.op("dve", lambda e: e.tensor_tensor(out=t1.ap, in0=pk.ap, in1=self.t_rope[:, 0:T], op=ALU.mult),
             reads=pk.bufs + [self.b_rope], writes=t1.bufs)
        t2 = self.f(5)
        S.op("dve", lambda e: e.tensor_tensor(out=t2.ap, in0=ps_.ap, in1=self.t_rope[:, T:2 * T], op=ALU.mult),
             reads=ps_.bufs + [self.b_rope], writes=t2.bufs)
        self.rel(bs)
        S.op("dve", lambda e: e.tensor_tensor(out=dst.ap, in0=t1.ap, in1=t2.ap, op=ALU.add),
             reads=t1.bufs + t2.bufs, writes=dst.bufs)

    def load_rope(self, tile):
        c = self.cfg
        T = c.T
        self.S.dma("sp", [lambda e: e.dma_start(out=self.t_rope[:, 0:T], in_=self.d_rope[0, :, tile * T:(tile + 1) * T]),
                          lambda e: e.dma_start(out=self.t_rope[:, T:2 * T], in_=self.d_rope[1, :, tile * T:(tile + 1) * T])],
                   self.b_rope, writes=[self.b_rope])

    def kv(self, seq, tile):
        c = self.cfg
        S = self.S
        T = c.T
        NTB = c.NTB
        per = c.SA // c.DC
        yield from self.stream_norm(6, self.xn)
        xs = [self.xn(i) for i in range(c.DC)]
        yield
        for hd in range(c.DAH):
            base = (hd % 2) * 4
            slK = self.slabA(c.iWK + hd)
            bk0, pk0 = self.proj_fm(slK, 0, xs)
            xb0 = self.rope_a(pk0, 8)
            yield
            bk1, pk1 = self.proj_fm(slK, 128, xs)
            xb1 = self.rope_a(pk1, 9)
            self.rope_b(pk0, xb0, self.ar(base))
            self.rel(bk0)
            yield
            slV = self.slabA(c.iWV + hd)
            vt = self.ar(base + 2, 2)
            for tb in range(NTB):
                if tb == 1 or NTB == 1:
                    self.rope_b(pk1, xb1, self.ar(base + 1))
                    self.rel(bk1)
                bv = self.bank()
                o = self.pb(bv, 0, 256)
                pairs = [(xs[i].ap[:, tb * 128:(tb + 1) * 128], slV.ap[:, i * per: i * per + 256]) for i in range(c.DC)]
                rd = list(slV.bufs)
                for x_ in xs:
                    rd += x_.bufs
                self.mm(o, pairs, rd)
                S.op("act", lambda e, o=o, tb=tb: e.activation(out=vt.ap[:, tb * 256:(tb + 1) * 256], in_=o.ap, func=AF.Copy),
                     reads=o.bufs, writes=vt.bufs)
                self.rel(bv)
            kd, vd = self.b_kd[seq][hd], self.b_vd[seq][hd]
            ksrc = self.ar(base, 2)
            fns = []
            for comp in range(2):
                fns.append(lambda e, comp=comp: e.dma_start(
                    out=self.d_k[seq, hd, :, comp * c.S + tile * T: comp * c.S + (tile + 1) * T],
                    in_=self.t_ar[:, (base + comp) * T:(base + comp + 1) * T]))
            S.dma("sp", fns, kd, reads=ksrc.bufs, writes=[kd])
            vdst = self.d_v[seq, hd, tile * T:(tile + 1) * T, :].rearrange("(tb p) v -> p tb v", p=128)
            vsrc = self.t_ar[:, (base + 2) * T:(base + 4) * T].rearrange("p (tb v) -> p tb v", v=256)
            S.dma("sp", [lambda e: e.dma_start(out=vdst, in_=vsrc)], vd, reads=vt.bufs, writes=[vd])
            yield

    def dattn(self, seq, tile):
        c = self.cfg
        S = self.S
        T = c.T
        NTB = c.NTB
        yield from self.stream_norm(5, self.xn)
        xs = [self.xn(i) for i in range(c.DC)]
        yield
        qpend = None
        for hd in range(c.DAH):
            slQ = self.slabA(c.iWQ + hd)
            for comp in range(2):
                bq, pq = self.proj_fm(slQ, comp * 128, xs)
                xb = self.rope_a(pq, 16 + comp)
                if qpend is not None:
                    self.rope_b(qpend[1], qpend[2], qpend[3])
                    self.rel(qpend[0])
                qpend = (bq, pq, xb, self.ar(hd * 2 + comp))
                yield
        if qpend is not None:
            self.rope_b(qpend[1], qpend[2], qpend[3])
            self.rel(qpend[0])
            qpend = None
            yield
        ntok = (tile + 1) * T
        NKB = ntok // 128
        npl = 2 * c.S // T
        K0 = 18
        V0 = 18 + npl
        P0 = V0 + npl
        ident = self.cbv(c.cbID, 128)
        ones = self.cbv(c.cbONE, 128)
        scale = 128.0 ** -0.5
        pending = None
        for hd in range(c.DAH):
            kd, vd = self.b_kd[seq][hd], self.b_vd[seq][hd]
            ks = self.ar(K0, npl)
            vs = self.ar(V0, npl)
            kdst = self.t_ar[:, K0 * T: K0 * T + 2 * c.S].rearrange("p (c t) -> p c t", c=2)[:, :, 0:ntok]
            ksrc = self.d_k[seq, hd].rearrange("p (c t) -> p c t", c=2)[:, :, 0:ntok]
            S.dma("sp", [lambda e, kdst=kdst, ksrc=ksrc: e.dma_start(out=kdst, in_=ksrc)], ks.bufs[0], reads=[kd], writes=ks.bufs)
            vdst = self.t_ar[:, V0 * T: V0 * T + NKB * 256].rearrange("p (kb v) -> p kb v", v=256)
            vsrc = self.d_v[seq, hd, 0:ntok, :].rearrange("(kb p) v -> p kb v", p=128)
            S.dma("sp", [lambda e, vdst=vdst, vsrc=vsrc: e.dma_start(out=vdst, in_=vsrc)], vs.bufs[0], reads=[vd], writes=vs.bufs)
            FO = 8 * (hd % 2)
            od = [self.f(FO), self.f(FO + 1)]
            for comp in range(2):
                qt = self.ar(hd * 2 + comp)
                bo0, bo1, bl = self.bank(), self.bank(), self.bank()
                po0, po1, pl_ = self.pb(bo0), self.pb(bo1), self.pb(bl)
                def qk2(kb0, it):
                    nk = min(2, NKB - kb0)
                    bs = self.bank()
                    ps2 = self.pb(bs, 0, nk * T)
                    rd = ks.bufs + qt.bufs + [self.b_cb]

                    def fn(e):
                        ins = None
                        for u in range(nk):
                            kb = kb0 + u
                            o = self.t_ps[:, bs, u * T:(u + 1) * T]
                            jd = kb - tile * NTB
                            ins = e.matmul(o, self.t_ar[:, K0 * T + comp * c.S + kb * 128: K0 * T + comp * c.S + (kb + 1) * 128],
                                           qt.ap, start=True, stop=(jd < 0))
                            if jd >= 0:
                                ins = e.matmul(o, ident, self.cbv(c.cbMASK + jd * T, T), start=False, stop=True)
                        return ins
                    S.op("pe", fn, reads=rd, writes=ps2.bufs)
                    pt = self.ar(P0 + 2 * (it % 2), 2)
                    pt2 = Pl(pt.ap[:, 0:nk * T], pt.bufs)
                    S.op("act", lambda e: e.activation(out=pt2.ap, in_=ps2.ap, func=AF.Exp, scale=scale),
                         reads=ps2.bufs, writes=pt2.bufs)
                    self.rel(bs)
                    return (kb0, nk, pt2)

                def pv2(item):
                    kb0, nk, pt2 = item

                    def fn(e):
                        ins = None
                        for u in range(nk):
                            kb = kb0 + u
                            vblk = self.t_ar[:, V0 * T + kb * 256: V0 * T + (kb + 1) * 256]
                            p = pt2.ap[:, u * T:(u + 1) * T]
                            st, sp_ = (kb == 0), (kb == NKB - 1)
                            e.matmul(po0.ap, vblk[:, 0:128], p, start=st, stop=sp_)
                            e.matmul(po1.ap, vblk[:, 128:256], p, start=st, stop=sp_)
                            ins = e.matmul(pl_.ap, ones, p, start=st, stop=sp_)
                        return ins
                    S.op("pe", fn, reads=vs.bufs + pt2.bufs + [self.b_cb], writes=po0.bufs + po1.bufs + pl_.bufs)

                prev = None
                nit = (NKB + 1) // 2
                for it in range(nit + 1):
                    cur_it = qk2(2 * it, it) if it < nit else None
                    if prev is not None:
                        pv2(prev)
                    prev = cur_it
                    if comp == 0 and it == min(1, nit) and pending is not None:
                        pending()
                        pending = None
                    yield
                rl = self.f(FO + 2)
                S.op("act", lambda e: e.activation(out=rl.ap, in_=pl_.ap, func=AF.Ln), reads=pl_.bufs, writes=rl.bufs)
                self.rel(bl)
                S.op("act", lambda e: e.activation(out=rl.ap, in_=rl.ap, func=AF.Exp, scale=-1.0), reads=rl.bufs, writes=rl.bufs)
                for v, (bo, po) in enumerate(((bo0, po0), (bo1, po1))):
                    if comp == 0:
                        S.op("dve", lambda e, po=po, v=v: e.tensor_tensor(out=od[v].ap, in0=po.ap, in1=rl.ap, op=ALU.mult),
                             reads=po.bufs + rl.bufs, writes=od[v].bufs)
                    else:
                        t = self.f(FO + 3)
                        S.op("dve", lambda e, po=po, t=t: e.tensor_tensor(out=t.ap, in0=po.ap, in1=rl.ap, op=ALU.mult),
                             reads=po.bufs + rl.bufs, writes=t.bufs)
                        S.op("dve", lambda e, t=t, v=v: e.scalar_tensor_tensor(
                            out=od[v].ap, in0=t.ap, scalar=self.sm(3), in1=od[v].ap, op0=ALU.mult, op1=ALU.add),
                            reads=t.bufs + od[v].bufs + [self.b_sm], writes=od[v].bufs)
                    self.rel(bo)
            def epi(hd=hd, od=od):
                rs = self.rstd(od, 256, 4, 5)
                for v in range(2):
                    on = self.xn(hd * 2 + v)
                    S.op("dve", lambda e, on=on, v=v: e.scalar_tensor_tensor(
                        out=on.ap, in0=od[v].ap, scalar=self.sm(4 + v), in1=rs.ap, op0=ALU.mult, op1=ALU.mult),
                        reads=od[v].bufs + rs.bufs + [self.b_sm], writes=on.bufs)
            pending = epi
            yield
        if pending is not None:
            pending()
            yield
        yield from self.add_proj(c.iDWOUT, [self.xn(i) for i in range(c.DC)])

    def setup(self):
        c = self.cfg
        S = self.S
        S.dma("sp", [lambda e: e.dma_start(out=self.t_prm[:, :], in_=self.d_prm)], self.b_prm, writes=[self.b_prm])
        S.dma("sp", [lambda e: e.dma_start(out=self.t_cf[:, :], in_=self.d_cf)], self.b_cf, writes=[self.b_cf])
        S.dma("pool", [lambda e: e.dma_start(out=self.t_cb[:, :], in_=self.d_cb)], self.b_cb, writes=[self.b_cb])
        sm = self.sm
        H = c.HGH
        S.op("dve", lambda e: e.memset(self.t_sm[:, :], 0.0), writes=[self.b_sm])
        S.op("dve", lambda e: e.memset(sm(0), EPS), reads=[self.b_sm], writes=[self.b_sm])
        S.op("dve", lambda e: e.memset(sm(2), 1.0), reads=[self.b_sm], writes=[self.b_sm])
        S.op("dve", lambda e: e.tensor_tensor(out=sm(8, H), in0=self.prm(c.pLB0, H), in1=self.prm(c.pLB1, H), op=ALU.subtract),
             reads=[self.b_prm, self.b_sm], writes=[self.b_sm])
        S.op("act", lambda e: e.activation(out=sm(8, H), in_=sm(8, H), func=AF.Sigmoid), reads=[self.b_sm], writes=[self.b_sm])
        S.op("dve", lambda e: e.tensor_scalar(out=sm(8 + H, H), in0=sm(8, H), scalar1=-1.0, scalar2=1.0, op0=ALU.mult, op1=ALU.add),
             reads=[self.b_sm], writes=[self.b_sm])
        S.op("dve", lambda e: e.tensor_scalar(out=sm(8 + 2 * H, H), in0=sm(8 + H, H), scalar1=-1.0, scalar2=None, op0=ALU.mult),
             reads=[self.b_sm], writes=[self.b_sm])
        L = c.pLAM
        S.op("dve", lambda e: e.tensor_tensor(out=self.t_lt[:, 0:128], in0=self.prm(L, 128), in1=self.prm(L + 128, 128), op=ALU.mult),
             reads=[self.b_prm], writes=[self.b_lt])
        S.op("dve", lambda e: e.tensor_tensor(out=self.t_lt[:, 128:256], in0=self.prm(L + 256, 128), in1=self.prm(L + 384, 128), op=ALU.mult),
             reads=[self.b_prm, self.b_lt], writes=[self.b_lt])
        S.op("dve", lambda e: e.reduce_sum(out=sm(6, 2), in_=self.t_lt[:, :].rearrange("p (a b) -> p a b", a=2), axis=AX.X),
             reads=[self.b_lt, self.b_sm], writes=[self.b_sm])
        S.op("act", lambda e: e.activation(out=sm(6, 2), in_=sm(6, 2), func=AF.Exp), reads=[self.b_sm], writes=[self.b_sm])
        S.op("dve", lambda e: e.tensor_tensor(out=sm(1), in0=sm(6), in1=sm(7), op=ALU.subtract), reads=[self.b_sm], writes=[self.b_sm])
        S.op("dve", lambda e: e.tensor_scalar(out=sm(3), in0=sm(1), scalar1=c.lam_init, scalar2=-1.0, op0=ALU.add, op1=ALU.mult),
             reads=[self.b_sm], writes=[self.b_sm])
        S.op("dve", lambda e: e.tensor_scalar(out=sm(4, 2), in0=self.prm(c.pSUB, 2), scalar1=1.0 - c.lam_init, scalar2=None, op0=ALU.mult),
             reads=[self.b_sm, self.b_prm], writes=[self.b_sm])

    def final(self, seq, tile):
        c = self.cfg
        T = c.T
        outp = self.arf(0, c.DC)
        yield from self.stream_norm(9, lambda i: Pl(self.t_ar[:, 2 * i * T:(2 * i + 2) * T].bitcast(F32), self.b_ar[2 * i:2 * i + 2]))
        src = self.t_ar[:, 0:2 * c.DC * T].bitcast(F32)
        self.S.dma("sp", [lambda e: e.dma_start(out=self.d_out[seq, tile], in_=src)], self.cur.b_out, reads=outp.bufs, writes=[self.cur.b_out])

    def seq_gen(self, st):
        c = self.cfg
        S = self.S
        seq = st.seq
        for tile in range(c.NT):
            st.tile = tile
            hall = Pl(self.t_h[:, :], self.b_h)
            S.dma("sp", [lambda e: e.dma_start(out=self.t_h[:, :], in_=self.d_x[seq, tile])],
                  self.b_h[0], writes=hall.bufs)
            self.load_rope(tile)
            stg = self.stages
            if "ffn00" in stg: yield from self.ffn(0, 0)
            if "hgrn" in stg: yield from self.hgrn(tile == 0)
            if "ffn01" in stg: yield from self.ffn(0, 1)
            if "ple0" in stg: yield from self.ple(0, seq, tile)
            if "kv" in stg: yield from self.kv(seq, tile)
            if "ffn10" in stg: yield from self.ffn(1, 0)
            if "dattn" in stg: yield from self.dattn(seq, tile)
            if "ffn11" in stg: yield from self.ffn(1, 1)
            if "ple1" in stg: yield from self.ple(1, seq, tile)
            yield from self.final(seq, tile)
            yield

    def build(self):
        c = self.cfg
        self.setup()
        gens = [self.seq_gen(st) for st in self.streams]
        alive = [True] * len(gens)
        steps = [0] * len(gens)
        while any(alive):
            for i, g in enumerate(gens):
                if not alive[i]:
                    continue
                if i > 0 and alive[i - 1] and steps[i - 1] - steps[i] < c.LAG:
                    continue
                self.cur = self.streams[i]
                try:
                    next(g)
                    steps[i] += 1
                except StopIteration:
                    alive[i] = False
        self.S.final_wait("sp", [st.b_out for st in self.streams])
        return self.nc


_CACHE = {}


def kernel(**inputs):
    cfg = Cfg()
    n = 8
    x = np.asarray(inputs["x"], np.float32)
    p = np.asarray(inputs["p"], np.float32)
    inp = {k: np.asarray(v) for k, v in inputs.items()}
    wA, wB = host_weights(inp, cfg)
    prm = host_params(inp, cfg)
    cb, cf, rope = host_consts(cfg)
    nc = Prog(cfg).build()
    in_maps = []
    for i in range(n):
        xs = x[i * cfg.NSEQ:(i + 1) * cfg.NSEQ]
        ps = p[:, i * cfg.NSEQ:(i + 1) * cfg.NSEQ]
        xT, pT = host_acts(xs, ps, cfg)
        in_maps.append({"xT": xT.reshape(cfg.NSEQ, cfg.NT, 128, -1), "pT": pT.reshape(2, cfg.NSEQ, cfg.NT, 128, -1),
                        "wA": wA, "wB": wB, "prm": prm, "cb": cb, "cf": cf, "rope": rope})
    res = run_bass_kernel_spmd(nc, in_maps, core_ids=list(range(n)))
    outs = []
    for i in range(n):
        oT = np.asarray(res.results[i]["outT"]).reshape(cfg.NSEQ, cfg.NT, 128, cfg.DC, cfg.T)
        outs.append(host_out(oT, cfg))
    return np.concatenate(outs, axis=0).astype(np.float32)
```
